# Optimizing a Trainium2 kernel written in Bass

```python
import math
import jax, jax.numpy as jnp
from jax import lax
import numpy as np

D_MODEL = 1024
BATCH = 4
SEQ = 8192
DEPTH = 1
DEC_BATCH = 32
DEC_SEQ = 64
PAST_LEN = 2048

CHUNK = 64
Q_BLOCK = 128
ATTN_WIDTH = D_MODEL // 2
LRU_WIDTH = D_MODEL - ATTN_WIDTH
HEAD_DIM = 64
N_ATTN_HEADS = ATTN_WIDTH // (2 * HEAD_DIM)
V_DIM = 2 * HEAD_DIM
LRU_BLOCKS = 8
LRU_BLOCK_W = LRU_WIDTH // LRU_BLOCKS
CONV_WIDTH = 4
LRU_C = 8.0
D_FF = 2816
NUM_BUCKETS = 32
MAX_DISTANCE = 128
IN_WIDTH = 3 * ATTN_WIDTH + 2 * LRU_WIDTH
EPS = 1e-6
NEG_INF = -1e30

kernel_name = "hymba_diffattn_rglru_macaron_stream"


def rms_norm(x, g):
    xf = x.astype(jnp.float32)
    y = xf * lax.rsqrt(jnp.mean(xf * xf, axis=-1, keepdims=True) + EPS)
    return (y * g.astype(jnp.float32)).astype(x.dtype)


def swiglu(x, w_gate, w_up, w_down):
    return (jax.nn.silu(x @ w_gate) * (x @ w_up)) @ w_down


def t5_bucket(rel):
    n = NUM_BUCKETS // 2
    max_exact = n // 2
    ret = jnp.where(rel > 0, n, 0)
    rel = jnp.abs(rel)
    relf = jnp.maximum(rel, 1).astype(jnp.float32)
    large = max_exact + (jnp.log(relf / max_exact) / math.log(MAX_DISTANCE / max_exact)
                         * (n - max_exact)).astype(jnp.int32)
    large = jnp.minimum(large, n - 1)
    return ret + jnp.where(rel < max_exact, rel, large)


def diff_attend(q, k, v, q_pos, k_pos, rel_table, lam):
    logits = jnp.einsum('bqhcd,bkhcd->bhcqk', q, k).astype(jnp.float32) * (HEAD_DIM ** -0.5)
    bias = rel_table.astype(jnp.float32)[t5_bucket(k_pos[None, :] - q_pos[:, None])]
    bias = jnp.transpose(bias, (2, 0, 1))[None, :, None]
    visible = (k_pos[None, :] // CHUNK) <= (q_pos[:, None] // CHUNK)
    logits = jnp.where(visible, logits + bias, NEG_INF)
    p = jax.nn.softmax(logits, axis=-1)
    attn = p[:, :, 0] - lam * p[:, :, 1]
    return jnp.einsum('bhqk,bkhe->bqhe', attn.astype(v.dtype), v)


def prompt_attention(q, k, v, rel_table, lam):
    B, S = q.shape[0], q.shape[1]
    nb = S // Q_BLOCK
    q_blocks = jnp.moveaxis(q.reshape(B, nb, Q_BLOCK, N_ATTN_HEADS, 2, HEAD_DIM), 1, 0)
    k_pos = jnp.arange(S, dtype=jnp.int32)

    def one_block(args):
        qb, start = args
        return diff_attend(qb, k, v, start + jnp.arange(Q_BLOCK, dtype=jnp.int32), k_pos, rel_table, lam)

    o = lax.map(one_block, (q_blocks, jnp.arange(nb, dtype=jnp.int32) * Q_BLOCK))
    return jnp.moveaxis(o, 0, 1).reshape(B, S, N_ATTN_HEADS, V_DIM)


def lru_combine(left, right):
    a1, b1 = left
    a2, b2 = right
    return a1 * a2, a2 * b1 + b2


def rglru_group(xr, xg, p, l, h0, conv0):
    B, S = xr.shape[0], xr.shape[1]
    xpad = jnp.concatenate([conv0.astype(xr.dtype), xr], axis=1)
    w = p['conv_w'][l]
    xc = p['conv_b'][l] + sum(xpad[:, j:j + S] * w[j] for j in range(CONV_WIDTH))
    conv_new = xpad[:, -(CONV_WIDTH - 1):]
    xb = xc.reshape(B, S, LRU_BLOCKS, LRU_BLOCK_W)
    r = jax.nn.sigmoid(jnp.einsum('bsnc,ncd->bsnd', xb, p['gate_a_w'][l]).reshape(B, S, LRU_WIDTH) + p['gate_a_b'][l])
    i = jax.nn.sigmoid(jnp.einsum('bsnc,ncd->bsnd', xb, p['gate_x_w'][l]).reshape(B, S, LRU_WIDTH) + p['gate_x_b'][l])
    log_a = -LRU_C * r.astype(jnp.float32) * jax.nn.softplus(-p['lru_L'][l].astype(jnp.float32))
    a = jnp.exp(log_a)
    b = jnp.sqrt(-jnp.expm1(2.0 * log_a)) * (i * xc).astype(jnp.float32)
    b = b.at[:, 0].add(a[:, 0] * h0.astype(jnp.float32))
    _, h = lax.associative_scan(lru_combine, (a, b), axis=1)
    out = h.astype(xr.dtype) * jax.nn.gelu(xg)
    out = rms_norm(out, p['lru_out_norm'][l])
    return out, h[:, -1].astype(h0.dtype), conv_new.astype(conv0.dtype)


def encoder_layer(x, p, l, rel_table, k_past, v_past, h0, conv0):
    B, S = x.shape[0], x.shape[1]
    x = x + 0.5 * swiglu(rms_norm(x, p['norm_ffn1'][l]), p['ffn1_gate'][l], p['ffn1_up'][l], p['ffn1_down'][l])
    hn = rms_norm(x, p['norm_mix'][l])
    proj = hn @ p['w_in'][l]
    q, k, v, xr, xg = jnp.split(proj, [ATTN_WIDTH, 2 * ATTN_WIDTH, 3 * ATTN_WIDTH, 3 * ATTN_WIDTH + LRU_WIDTH], axis=-1)
    q = rms_norm(q.reshape(B, S, N_ATTN_HEADS, 2, HEAD_DIM), p['q_norm'][l])
    k = rms_norm(k.reshape(B, S, N_ATTN_HEADS, 2, HEAD_DIM), p['k_norm'][l])
    v = v.reshape(B, S, N_ATTN_HEADS, V_DIM)
    lam_init = 0.8 - 0.6 * math.exp(-0.3 * l)
    lam = (jnp.exp(jnp.sum(p['lambda_q1'][l].astype(jnp.float32) * p['lambda_k1'][l].astype(jnp.float32)))
           - jnp.exp(jnp.sum(p['lambda_q2'][l].astype(jnp.float32) * p['lambda_k2'][l].astype(jnp.float32)))
           + lam_init)
    if k_past is None:
        o = prompt_attention(q, k, v, rel_table, lam)
    else:
        P = k_past.shape[1]
        k_all = jnp.concatenate([k_past.astype(k.dtype), k], axis=1)
        v_all = jnp.concatenate([v_past.astype(v.dtype), v], axis=1)
        q_pos = P + jnp.arange(S, dtype=jnp.int32)
        k_pos = jnp.arange(P + S, dtype=jnp.int32)
        o = diff_attend(q, k_all, v_all, q_pos, k_pos, rel_table, lam)
    o = (rms_norm(o, p['subln'][l]) * (1.0 - lam_init)).reshape(B, S, ATTN_WIDTH)
    r_out, h_last, conv_new = rglru_group(xr, xg, p, l, h0, conv0)
    x = x + jnp.concatenate([o, r_out], axis=-1) @ p['w_out'][l]
    x = x + 0.5 * swiglu(rms_norm(x, p['norm_ffn2'][l]), p['ffn2_gate'][l], p['ffn2_up'][l], p['ffn2_down'][l])
    return x, k, v, h_last, conv_new


def setup_inputs(seed: int = 0) -> dict:
    key = jax.random.key(seed)
    ks = jax.random.split(key, 40)

    def nrm(k, shape, scale):
        return jax.random.normal(k, shape, jnp.float32) * scale

    def gain(k, shape):
        return 1.0 + 0.01 * jax.random.normal(k, shape, jnp.float32)

    u = jax.random.uniform(ks[27], (DEPTH, LRU_WIDTH), jnp.float32, minval=0.9, maxval=0.999)
    a0 = u ** (1.0 / LRU_C)
    lru_L = jnp.log(a0) - jnp.log1p(-a0)
    return {
        "x_prompt": nrm(ks[0], (BATCH, SEQ, D_MODEL), 1.0),
        "x_sample": nrm(ks[1], (DEC_BATCH, DEC_SEQ, D_MODEL), 1.0),
        "cache_k": nrm(ks[2], (DEPTH, DEC_BATCH, PAST_LEN, N_ATTN_HEADS, 2, HEAD_DIM), 1.0),
        "cache_v": nrm(ks[3], (DEPTH, DEC_BATCH, PAST_LEN, N_ATTN_HEADS, V_DIM), 1.0),
        "state_lru": nrm(ks[4], (DEPTH, DEC_BATCH, LRU_WIDTH), 0.5),
        "state_conv": nrm(ks[5], (DEPTH, DEC_BATCH, CONV_WIDTH - 1, LRU_WIDTH), 1.0),
        "rel_bias": nrm(ks[6], (NUM_BUCKETS, N_ATTN_HEADS), 0.2),
        "norm_ffn1": gain(ks[7], (DEPTH, D_MODEL)),
        "ffn1_gate": nrm(ks[8], (DEPTH, D_MODEL, D_FF), D_MODEL ** -0.5),
        "ffn1_up": nrm(ks[9], (DEPTH, D_MODEL, D_FF), D_MODEL ** -0.5),
        "ffn1_down": nrm(ks[10], (DEPTH, D_FF, D_MODEL), D_FF ** -0.5),
        "norm_mix": gain(ks[11], (DEPTH, D_MODEL)),
        "w_in": nrm(ks[12], (DEPTH, D_MODEL, IN_WIDTH), D_MODEL ** -0.5),
        "q_norm": gain(ks[13], (DEPTH, HEAD_DIM)),
        "k_norm": gain(ks[14], (DEPTH, HEAD_DIM)),
        "lambda_q1": nrm(ks[15], (DEPTH, HEAD_DIM), 0.1),
        "lambda_k1": nrm(ks[16], (DEPTH, HEAD_DIM), 0.1),
        "lambda_q2": nrm(ks[17], (DEPTH, HEAD_DIM), 0.1),
        "lambda_k2": nrm(ks[18], (DEPTH, HEAD_DIM), 0.1),
        "subln": gain(ks[19], (DEPTH, V_DIM)),
        "conv_w": nrm(ks[20], (DEPTH, CONV_WIDTH, LRU_WIDTH), CONV_WIDTH ** -0.5),
        "conv_b": nrm(ks[21], (DEPTH, LRU_WIDTH), 0.01),
        "gate_a_w": nrm(ks[22], (DEPTH, LRU_BLOCKS, LRU_BLOCK_W, LRU_BLOCK_W), LRU_BLOCK_W ** -0.5),
        "gate_a_b": nrm(ks[23], (DEPTH, LRU_WIDTH), 0.01),
        "gate_x_w": nrm(ks[24], (DEPTH, LRU_BLOCKS, LRU_BLOCK_W, LRU_BLOCK_W), LRU_BLOCK_W ** -0.5),
        "gate_x_b": nrm(ks[25], (DEPTH, LRU_WIDTH), 0.01),
        "lru_L": lru_L,
        "lru_out_norm": gain(ks[26], (DEPTH, LRU_WIDTH)),
        "w_out": nrm(ks[28], (DEPTH, D_MODEL, D_MODEL), D_MODEL ** -0.5),
        "norm_ffn2": gain(ks[29], (DEPTH, D_MODEL)),
        "ffn2_gate": nrm(ks[30], (DEPTH, D_MODEL, D_FF), D_MODEL ** -0.5),
        "ffn2_up": nrm(ks[31], (DEPTH, D_MODEL, D_FF), D_MODEL ** -0.5),
        "ffn2_down": nrm(ks[32], (DEPTH, D_FF, D_MODEL), D_FF ** -0.5),
    }


def reference(x_prompt, x_sample, cache_k, cache_v, state_lru, state_conv, rel_bias,
              norm_ffn1, ffn1_gate, ffn1_up, ffn1_down, norm_mix, w_in, q_norm, k_norm,
              lambda_q1, lambda_k1, lambda_q2, lambda_k2, subln, conv_w, conv_b,
              gate_a_w, gate_a_b, gate_x_w, gate_x_b, lru_L, lru_out_norm, w_out,
              norm_ffn2, ffn2_gate, ffn2_up, ffn2_down):
    p = dict(norm_ffn1=norm_ffn1, ffn1_gate=ffn1_gate, ffn1_up=ffn1_up, ffn1_down=ffn1_down,
             norm_mix=norm_mix, w_in=w_in, q_norm=q_norm, k_norm=k_norm,
             lambda_q1=lambda_q1, lambda_k1=lambda_k1, lambda_q2=lambda_q2, lambda_k2=lambda_k2,
             subln=subln, conv_w=conv_w, conv_b=conv_b, gate_a_w=gate_a_w, gate_a_b=gate_a_b,
             gate_x_w=gate_x_w, gate_x_b=gate_x_b, lru_L=lru_L, lru_out_norm=lru_out_norm,
             w_out=w_out, norm_ffn2=norm_ffn2, ffn2_gate=ffn2_gate, ffn2_up=ffn2_up, ffn2_down=ffn2_down)
    B = x_prompt.shape[0]
    yp, ys = x_prompt, x_sample
    kp_l, vp_l, hp_l, cp_l, ks_l, vs_l, hs_l, cs_l = [], [], [], [], [], [], [], []
    for l in range(DEPTH):
        h0_p = jnp.zeros((B, LRU_WIDTH), state_lru.dtype)
        c0_p = jnp.zeros((B, CONV_WIDTH - 1, LRU_WIDTH), state_conv.dtype)
        yp, kp, vp, hp, cp = encoder_layer(yp, p, l, rel_bias, None, None, h0_p, c0_p)
        ys, kn, vn, hn, cn = encoder_layer(ys, p, l, rel_bias, cache_k[l], cache_v[l], state_lru[l], state_conv[l])
        kp_l.append(kp); vp_l.append(vp); hp_l.append(hp); cp_l.append(cp)
        ks_l.append(kn); vs_l.append(vn); hs_l.append(hn); cs_l.append(cn)
    new_k_prompt = jnp.stack(kp_l, axis=0)
    new_v_prompt = jnp.stack(vp_l, axis=0)
    new_lru_prompt = jnp.stack(hp_l, axis=0)
    new_conv_prompt = jnp.stack(cp_l, axis=0)
    new_k_sample = jnp.stack(ks_l, axis=0)
    new_v_sample = jnp.stack(vs_l, axis=0)
    new_lru_sample = jnp.stack(hs_l, axis=0)
    new_conv_sample = jnp.stack(cs_l, axis=0)
    return (yp, ys, new_k_prompt, new_v_prompt, new_lru_prompt, new_conv_prompt,
            new_k_sample, new_v_sample, new_lru_sample, new_conv_sample)
```

```python
import math
from contextlib import ExitStack
import numpy as np
import concourse.bass as bass
import concourse.mybir as mybir
from concourse.bass_utils import run_bass_kernel_spmd

F32 = mybir.dt.float32
BF16 = mybir.dt.bfloat16
AF = mybir.ActivationFunctionType
ALU = mybir.AluOpType
AX = mybir.AxisListType

ENGS = ("pe", "act", "dve", "pool", "sp")
INORDER_SAFE = ("pe", "sp")
EPS = 1e-6
NS = 3


class Op:
    __slots__ = ("eng", "fn", "waits", "inc_needed", "idx", "semval", "dma_inc")

    def __init__(self, eng, fn, waits):
        self.eng = eng
        self.fn = fn
        self.waits = waits
        self.inc_needed = False
        self.idx = -1
        self.semval = 0
        self.dma_inc = None


class Sched:
    def __init__(self, nc, same_engine_sync=True):
        self.nc = nc
        self.streams = {e: [] for e in ENGS}
        self.res = {}
        self.seen = {e: {} for e in ENGS}
        self.dma_tot = {}
        self.same_engine_sync = same_engine_sync

    def _need(self, eng, ev, waits):
        if ev[0] == "op":
            op = ev[1]
            if op.eng == eng and (eng in INORDER_SAFE or not self.same_engine_sync):
                return
            key = ("op", op.eng)
            if self.seen[eng].get(key, -1) >= op.idx:
                return
            self.seen[eng][key] = op.idx
            op.inc_needed = True
            waits.append(ev)
        else:
            key = ("dma", ev[1])
            if self.seen[eng].get(key, -1) >= ev[2]:
                return
            self.seen[eng][key] = ev[2]
            waits.append(ev)

    def _collect(self, eng, reads, writes):
        waits = []
        for r in reads:
            st = self.res.setdefault(r, [[], []])
            for ev in st[0]:
                self._need(eng, ev, waits)
        for w in writes:
            st = self.res.setdefault(w, [[], []])
            for ev in st[0]:
                self._need(eng, ev, waits)
            for ev in st[1]:
                self._need(eng, ev, waits)
        return waits

    def _commit(self, ev, reads, writes):
        for r in reads:
            self.res[r][1].append(ev)
        for w in writes:
            self.res[w] = [[ev], []]

    def op(self, eng, fn, reads=(), writes=()):
        waits = self._collect(eng, reads, writes)
        o = Op(eng, fn, waits)
        o.idx = len(self.streams[eng])
        self.streams[eng].append(o)
        self._commit(("op", o), reads, writes)
        return o

    def dma(self, q, pairs, reads, writes, semkey):
        waits = self._collect(q, reads, writes)
        tot = self.dma_tot.get(semkey, 0)
        first = True
        for pr in pairs:
            out_ap, in_ap = pr[0], pr[1]
            kw = pr[2] if len(pr) > 2 else {}
            tot += 16
            o = Op(q, (lambda e, a=out_ap, b=in_ap, k=kw: e.dma_start(out=a, in_=b, **k)), waits if first else [])
            first = False
            o.idx = len(self.streams[q])
            o.dma_inc = semkey
            self.streams[q].append(o)
        self.dma_tot[semkey] = tot
        self._commit(("dma", semkey, tot), reads, writes)

    def wait_all_dma(self, eng="sp"):
        waits = [("dma", k, v) for k, v in self.dma_tot.items()]
        o = Op(eng, None, waits)
        o.idx = len(self.streams[eng])
        self.streams[eng].append(o)

    def emit(self):
        nc = self.nc
        with ExitStack() as es:
            esem = {e: es.enter_context(nc.semaphore("s_" + e)) for e in ENGS}
            dsem = {k: es.enter_context(nc.semaphore("d_" + str(k))) for k in self.dma_tot}
            for e, ops in self.streams.items():
                c = 0
                for o in ops:
                    if o.inc_needed:
                        c += 1
                        o.semval = c
            block = es.enter_context(nc.Block())

            def run(eng_name):
                def body(e):
                    for o in self.streams[eng_name]:
                        for ev in o.waits:
                            if ev[0] == "op":
                                e.wait_ge(esem[ev[1].eng], ev[1].semval)
                            else:
                                e.wait_ge(dsem[ev[1]], ev[2])
                        if o.fn is None:
                            continue
                        ins = o.fn(e)
                        if o.dma_inc is not None:
                            ins.then_inc(dsem[o.dma_inc], 16)
                        elif o.inc_needed:
                            ins.then_inc(esem[eng_name], 1)
                return body

            block.tensor(run("pe"))
            block.scalar(run("act"))
            block.vector(run("dve"))
            block.gpsimd(run("pool"))
            block.sync(run("sp"))
        return nc


W_SPECS = [
    ("rel_bias", (32, 4)), ("norm_ffn1", (1, 1024)), ("ffn1_gate", (1024, 2816)), ("ffn1_up", (1024, 2816)),
    ("ffn1_down", (2816, 1024)), ("norm_mix", (1, 1024)), ("w_in", (1024, 2560)), ("q_norm", (1, 64)),
    ("k_norm", (1, 64)), ("lambda_q1", (1, 64)), ("lambda_k1", (1, 64)), ("lambda_q2", (1, 64)),
    ("lambda_k2", (1, 64)), ("subln", (1, 128)), ("conv_w", (4, 512)), ("conv_b", (1, 512)),
    ("gate_a_w", (8, 64, 64)), ("gate_a_b", (1, 512)), ("gate_x_w", (8, 64, 64)), ("gate_x_b", (1, 512)),
    ("lru_L", (1, 512)), ("lru_out_norm", (1, 512)), ("w_out", (1024, 1024)), ("norm_ffn2", (1, 1024)),
    ("ffn2_gate", (1024, 2816)), ("ffn2_up", (1024, 2816)), ("ffn2_down", (2816, 1024)),
]


def t5_bucket_np(rel):
    n = 16
    max_exact = 8
    ret = np.where(rel > 0, n, 0)
    rel = np.abs(rel)
    relf = np.maximum(rel, 1).astype(np.float32)
    large = max_exact + (np.log(relf / np.float32(max_exact)) / np.float32(math.log(128 / max_exact))
                         * np.float32(n - max_exact)).astype(np.int32)
    large = np.minimum(large, n - 1)
    return ret + np.where(rel < max_exact, rel, large)


def host_consts():
    ident = np.eye(128, dtype=np.float32)
    jrev = np.ascontiguousarray(ident[::-1])
    m = np.arange(384)
    bk = t5_bucket_np(m - 255)
    oh = np.zeros((32, 384), np.float32)
    oh[bk, m] = 1.0
    oh[15, :] -= 1.0
    i = np.arange(128)[:, None]
    r = np.arange(256)[None, :]
    vis = np.where((r < 128) & ((i // 64) > (r // 64)), 0.0, 1.0).astype(np.float32)
    r2 = np.arange(128)[None, :]
    vis2 = ((i // 64) == (r2 // 64)).astype(np.float32)
    return dict(c_ident=ident, c_j=jrev, c_oh=oh, c_vis=vis, c_vis2=vis2)


PHASES = []
MARKERS = False


def build(SEQ, PAST, do_sample=True, half=True):
    NT = SEQ // 512
    NL = NT // 2 if half else 0
    OSEQ = SEQ - NL * 512
    NKB = SEQ // 128
    PKB = PAST // 128
    assert SEQ % 512 == 0 and PAST % 512 == 0
    nc = bass.Bass("TRN2", target_bir_lowering=False)
    S = Sched(nc)
    es = ExitStack()
    del PHASES[:]

    def mark(name):
        PHASES.append((name, len(S.streams["pe"])))
        if MARKERS:
            S.op("dve", lambda e: e.memset(ap=dummy[:, 3:4], constant=0.0), [], [])

    def din(n, s, d=F32):
        return nc.dram_tensor(n, list(s), d, kind="ExternalInput").ap()

    def dout(n, s, d=F32):
        return nc.dram_tensor(n, list(s), d, kind="ExternalOutput").ap()

    def dscr(n, s, d):
        return nc.dram_tensor(n, list(s), d, kind="Internal").ap()

    def sb(n, s, d):
        return es.enter_context(nc.sbuf_tensor(n, list(s), d))

    def A(eng, method, reads, writes, **kw):
        return S.op(eng, lambda e: getattr(e, method)(**kw), reads, writes)

    x_d = din("x", (SEQ, 1024))
    xs_d = din("xs", (512, 1024))
    NSR = 4
    ck_d = din("ck", (NSR, PAST, 512))
    cv_d = din("cv", (NSR, PAST, 512))
    slru_d = din("slru", (8, 512))
    sconv_d = din("sconv", (24, 512))
    W = {n: din(n, s) for n, s in W_SPECS}
    c_ident = din("c_ident", (128, 128))
    c_j = din("c_j", (128, 128))
    c_oh = din("c_oh", (32, 384))
    c_vis = din("c_vis", (128, 256))
    c_vis2 = din("c_vis2", (128, 128))

    flag_d = din("flag", (128, 1))
    y_d = dout("y", (OSEQ, 1024))
    ys_d = dout("ys", (512, 1024))
    nk_d = dout("nk", (OSEQ, 512))
    nv_d = dout("nv", (OSEQ, 512))
    nlru_d = dout("nlru", (1, 512))
    nconv_d = dout("nconv", (3, 512))
    nks_d = dout("nks", (512, 512))
    nvs_d = dout("nvs", (512, 512))
    nlrus_d = dout("nlrus", (8, 512))
    nconvs_d = dout("nconvs", (24, 512))

    wsc = dscr("wsc", (41, 128, 4096), BF16)
    KT_s = dscr("KT_s", (4, 128, SEQ), BF16)
    VA_s = dscr("VA_s", (4, 128, NKB, 130), BF16)
    eu_s = dscr("eu_s", (4, 384), F32)

    xt = sb("xt", (128, 4, 1024), F32)
    arena = sb("arena", (128, 11264), BF16)
    hidT = arena[:].rearrange("p (k n) -> p k n", k=22)
    arena_f = arena[:].bitcast(F32)
    lt = [arena_f[:, k * 512:(k + 1) * 512] for k in range(6)]
    rstdL = arena_f[:, 3072:3584]
    hg = arena_f[:, 3584:5632].rearrange("p (c n) -> p c n", c=4)
    wring = sb("wring", (128, NS, 4096), BF16)
    xnT_t = sb("xnT", (128, 8, 512), BF16)
    xnT = xnT_t[:]
    catT = xnT_t[:]
    xsb = sb("xsb", (128, 2, 1024), BF16)
    sg = sb("sg", (128, 2, 512), F32)
    q_tm = sb("q_tm", (128, 4, 512), BF16)
    k_tm = sb("k_tm", (128, 4, 512), F32)
    k_bf = sb("k_bf", (128, 4, 512), BF16)
    v_tm = sb("v_tm", (128, 4, 512), F32)
    tmpn = sb("tmpn", (128, 2, 512), F32)
    QT = sb("QT", (128, 4, 512), BF16)
    KTn = sb("KTn", (128, 4, 512), BF16)
    VAn = sb("VAn", (128, 4, 4, 130), BF16)
    att = sb("att", (128, 8256), BF16)
    KTc = att[:, 0:4096].rearrange("p (s n) -> p s n", s=2)
    VAc = att[:, 4096:8256].rearrange("p (s k e) -> p s k e", s=2, k=16)
    kst = att[:, 0:4096].bitcast(F32).rearrange("p (k d) -> p k d", k=4)
    vst = att[:, 4096:8192].bitcast(F32).rearrange("p (k d) -> p k d", k=4)
    att_f = att[:, 0:8192].bitcast(F32)
    la = [att_f[:, k * 512:(k + 1) * 512] for k in range(6)]
    LA = ["la%d" % k for k in range(6)]
    LTN = ["lt%d" % k for k in range(6)]
    sx = sb("sx", (128, 6176), BF16)
    kbf = sx[:, 0:2048].rearrange("p (k d) -> p k d", k=4)
    KTq = sx[:, 2048:4096].rearrange("p (h n) -> p h n", h=4)
    VAq = sx[:, 4096:6176].rearrange("p (k h e) -> p k h e", k=4, h=4)
    PT = sb("PT", (128, 3, 2, 512), BF16)
    o_tm = sb("o_tm", (128, 4, 512), BF16)
    osb = sb("osb", (128, 5, 128), F32)
    Osb = sb("Osb", (128, 3, 390), F32)
    xrT = sb("xrT", (128, 2144), F32)
    xrP = xrT[:, 0:2060].rearrange("p (c n) -> p c n", c=4)
    xrS = xrT[:, 0:2144].rearrange("p (c s n) -> p c s n", c=4, s=8)
    xgT = sb("xgT", (128, 4, 512), F32)
    identf = sb("identf", (128, 128), F32)
    identb = sb("identb", (128, 128), BF16)
    jrev = sb("jrev", (128, 128), F32)
    onesf = sb("onesf", (128, 128), F32)
    vis = sb("vis", (128, 256), F32)
    vis2 = sb("vis2", (128, 128), F32)
    ET = sb("ET", (128, 4, 256), F32)
    ETp = sb("ETp", (128, 4, 128), F32)
    Hk = sb("Hk", (128, 4, 2, 128), F32)
    gT = sb("gT", (128, 3, 8), F32)
    gq_rep = sb("gq_rep", (128, 8, 64), F32)
    gk_rep = sb("gk_rep", (128, 8, 64), F32)
    gsub_rep = sb("gsub_rep", (128, 128), F32)
    cw = sb("cw", (128, 4, 4), F32)
    vecs = sb("vecs", (128, 6, 4), F32)
    Wbd = sb("Wbd", (128, 2, 4, 128), F32)
    relb = sb("relb", (32, 4), F32)
    oh = sb("oh", (32, 384), F32)
    eu = sb("eu", (4, 384), F32)
    lamv = sb("lamv", (1, 4, 64), F32)
    lams = sb("lams", (1, 8), F32)
    neglam = sb("neglam", (128, 1), F32)
    st = sb("st", (128, 64), F32)
    hcar = sb("hcar", (128, 4), F32)
    h0T = sb("h0T", (128, 4, 8), F32)
    hlast = sb("hlast", (128, 4, 8), F32)
    tails = sb("tails", (128, 4, 24), F32)
    sm_tm = tmpn
    dummy = sb("dummyt", (128, 4), F32)
    flagt = sb("flagt", (128, 1), F32)
    cm05 = sb("cm05", (128, 2), F32)

    psall = es.enter_context(nc.psum_tensor("psall", [128, 8, 512], F32))

    class _PV:
        def __init__(self, i):
            self.i = i

        def __getitem__(self, key):
            return psall[:, self.i, :][key]

    ps = [_PV(i) for i in range(8)]
    psb = [psall[:, i, :].bitcast(BF16) for i in range(8)]
    print("sbuf bytes remaining", nc.sbuf_bytes_remaining)

    XT = ["xt0", "xt1", "xt2", "xt3"]
    HID = ["hid%d" % i for i in range(22)]
    LT = ["lt%d" % i for i in range(6)] + ["rstdL", "hg0", "hg1", "hg2", "hg3"]
    XN = ["xn0", "xn1", "xn2", "xn3"]
    CAT = ["cat%d" % i for i in range(8)]

    ALLA = HID + XN + LT + CAT

    def barrier(eng, writes=None):
        writes = ALLA
        k = ENGS.index(eng) % 4
        if eng == "act":
            A(eng, "copy", [], list(writes) + ["dummy_" + eng], out=dummy[:, k:k + 1], in_=onesf[:, 0:1])
        else:
            A(eng, "memset", [], list(writes) + ["dummy_" + eng], ap=dummy[:, k:k + 1], constant=0.0)

    slow = dict(allow_slow_non_contiguous=True)
    S.dma("sp", [(identf[:], c_ident), (jrev[:], c_j), (oh[:], c_oh), (vis[:], c_vis), (vis2[:], c_vis2),
                 (relb[:], W["rel_bias"]),
                 (gT[:, 0, :], W["norm_ffn1"].rearrange("o (k p) -> p (o k)", p=128), slow),
                 (gT[:, 1, :], W["norm_mix"].rearrange("o (k p) -> p (o k)", p=128), slow),
                 (gT[:, 2, :], W["norm_ffn2"].rearrange("o (k p) -> p (o k)", p=128), slow),
                 (gq_rep[:], bass.AP(W["q_norm"].tensor, 0, [[0, 128], [0, 8], [1, 64]])),
                 (gk_rep[:], bass.AP(W["k_norm"].tensor, 0, [[0, 128], [0, 8], [1, 64]])),
                 (gsub_rep[:], bass.AP(W["subln"].tensor, 0, [[0, 128], [1, 128]])),
                 (cw[:, 0, :], W["conv_w"][0:1, :].rearrange("o (c p) -> p (o c)", p=128), slow),
                 (cw[:, 1, :], W["conv_w"][1:2, :].rearrange("o (c p) -> p (o c)", p=128), slow),
                 (cw[:, 2, :], W["conv_w"][2:3, :].rearrange("o (c p) -> p (o c)", p=128), slow),
                 (cw[:, 3, :], W["conv_w"][3:4, :].rearrange("o (c p) -> p (o c)", p=128), slow),
                 (vecs[:, 0, :], W["conv_b"].rearrange("o (c p) -> p (o c)", p=128), slow),
                 (vecs[:, 1, :], W["gate_a_b"].rearrange("o (c p) -> p (o c)", p=128), slow),
                 (vecs[:, 2, :], W["gate_x_b"].rearrange("o (c p) -> p (o c)", p=128), slow),
                 (vecs[:, 3, :], W["lru_L"].rearrange("o (c p) -> p (o c)", p=128), slow),
                 (vecs[:, 4, :], W["lru_out_norm"].rearrange("o (c p) -> p (o c)", p=128), slow),
                 (lamv[:, 0, :], W["lambda_q1"]), (lamv[:, 1, :], W["lambda_q2"]),
                 (lamv[:, 2, :], W["lambda_k1"]), (lamv[:, 3, :], W["lambda_k2"]),
                 (flagt[:], flag_d),
                 ], [], ["const"], "const")
    A("pool", "memset", [], ["Wbd"], ap=Wbd[:], constant=0.0)
    A("pool", "memset", [], ["onesf"], ap=onesf[:], constant=1.0)
    A("pool", "memset", [], ["cm05"], ap=cm05[:, 0:1], constant=-0.5)
    A("pool", "memset", [], ["cm05"], ap=cm05[:, 1:2], constant=0.5)
    A("pool", "memset", [], ["VAn"], ap=VAn[:], constant=1.0)
    A("pool", "memset", [], ["VAq"], ap=VAq, constant=1.0)
    A("pool", "memset", [], ["hcar"], ap=hcar[:], constant=0.0)
    A("pool", "memset", [], ["xr0", "xr1", "xr2", "xr3"], ap=xrT[:], constant=0.0)
    wb_pairs = []
    for gi, gname in enumerate(("gate_a_w", "gate_x_w")):
        for n in range(8):
            o0 = 64 * (n % 2)
            wb_pairs.append((Wbd[o0:o0 + 64, gi, n // 2, o0:o0 + 64], W[gname][n, :, :]))
    S.dma("sp", wb_pairs, [], ["Wbd"], "const2")
    A("dve", "tensor_copy", ["const"], ["identb"], out=identb[:], in_=identf[:])
    A("dve", "tensor_scalar", ["const"], ["gsub"], out=gsub_rep[:], in0=gsub_rep[:], scalar1=0.8, scalar2=None, op0=ALU.mult)
    A("dve", "tensor_scalar", ["const"], ["const"], out=vecs[:, 1:3, :], in0=vecs[:, 1:3, :], scalar1=-1.0, scalar2=None, op0=ALU.mult)
    A("act", "activation", ["const"], ["nsp"], out=vecs[:, 5, :], in_=vecs[:, 3, :], func=AF.Exp, scale=-1.0)
    A("act", "activation", ["nsp"], ["nsp"], out=vecs[:, 5, :], in_=vecs[:, 5, :], func=AF.Ln, bias=1.0)
    A("dve", "tensor_scalar", ["nsp"], ["nsp"], out=vecs[:, 5, :], in0=vecs[:, 5, :], scalar1=-8.0, scalar2=None, op0=ALU.mult)
    A("dve", "tensor_tensor", ["const"], ["lamv"], out=lamv[:, 0:2, :], in0=lamv[:, 0:2, :], in1=lamv[:, 2:4, :], op=ALU.mult)
    A("dve", "tensor_reduce", ["lamv"], ["lams"], out=lams[:, 0:2], in_=lamv[:, 0:2, :], axis=AX.X, op=ALU.add)
    A("act", "activation", ["lams"], ["lams"], out=lams[:, 2:4], in_=lams[:, 0:2], func=AF.Exp)
    A("dve", "tensor_tensor", ["lams"], ["lams"], out=lams[:, 4:5], in0=lams[:, 3:4], in1=lams[:, 2:3], op=ALU.subtract)
    A("dve", "tensor_scalar", ["lams"], ["lams"], out=lams[:, 5:6], in0=lams[:, 4:5], scalar1=-0.2, scalar2=None, op0=ALU.add)
    A("pe", "matmul", ["lams", "onesf"], ["ps0"], out=ps[0][:, 0:1], lhsT=onesf[0:1, :], rhs=lams[:, 5:6], start=True, stop=True)
    A("dve", "tensor_copy", ["ps0"], ["neglam"], out=neglam[:], in_=ps[0][:, 0:1])
    A("pe", "matmul", ["const"], ["ps1"], out=ps[1][0:4, 0:384], lhsT=relb[:], rhs=oh[:], start=True, stop=True)
    A("act", "activation", ["ps1"], ["eu"], out=eu[:], in_=ps[1][0:4, 0:384], func=AF.Exp)
    S.dma("sp", [(eu_s, eu[:])], ["eu"], ["eu_s"], "st_eu")
    S.dma("sp", [(Hk[:, h, :, :], bass.AP(eu_s.tensor, h * 384, [[1, 128], [128, 2], [1, 128]])) for h in range(4)],
          ["eu_s"], ["Hk"], "ld_hk")
    for h in range(4):
        b = 2 + (h % 2)
        A("pe", "matmul", ["Hk", "const"], ["ps%d" % b], out=ps[b][:, 128:256], lhsT=Hk[:, h, 0, :], rhs=jrev[:], start=True, stop=True)
        A("pe", "matmul", ["Hk", "const"], ["ps%d" % b], out=ps[b][:, 0:128], lhsT=Hk[:, h, 1, :], rhs=jrev[:], start=True, stop=True)
        A("dve", "tensor_tensor", ["ps%d" % b, "const"], ["ET"], out=ET[:, h, :], in0=ps[b][:, 0:256], in1=vis[:], op=ALU.mult)
        A("dve", "tensor_tensor", ["ps%d" % b, "const"], ["ET"], out=ETp[:, h, :], in0=ps[b][:, 0:128], in1=vis2[:], op=ALU.mult)

    chunks = []
    for f in ("ffn1", "w_in", "w_out", "ffn2"):
        if f in ("ffn1", "ffn2"):
            for j in range(11):
                chunks.append([(W[f + "_gate"], 0, 8, 256 * j, 256), (W[f + "_up"], 0, 8, 256 * j, 256)])
            for half in range(2):
                for g in range(3):
                    chunks.append([(W[f + "_down"], g * 8, min(8, 22 - g * 8), 512 * half, 512)])
        elif f == "w_in":
            for b in range(5):
                chunks.append([(W["w_in"], 0, 8, 512 * b, 512)])
        else:
            for half in range(2):
                chunks.append([(W["w_out"], 0, 8, 512 * half, 512)])
    assert len(chunks) == 41
    chunk_len = [sum(nk * ncols for (_, _, nk, _, ncols) in pieces) for pieces in chunks]
    stage = [xt[:].rearrange("p a b -> p (a b)"), arena_f[:, 0:4096]]
    stage_res = [XT, HID + LT]
    cast_eng = ["dve", "pool", "act"]
    N_EAGER = 22 if (half and NL >= 4) else 41
    sx_f = sx[:, 0:4096].bitcast(F32)
    lz_stage = [sx_f[:, 0:1024], sx_f[:, 1024:2048]]
    lz_stage_res = [["kbf"], ["KTq0", "KTq1", "KTq2", "KTq3"]]
    lz_out = [sx[:, 4096:5120], sx[:, 5120:6144]]
    lz_out_res = [["cvA"], ["cvB"]]
    lazy_pieces = []
    lz_ctr = [0]

    def lazy_piece(ci, off, wap, kt0, nk, col0, ncols):
        k = lz_ctr[0] % 2
        lz_ctr[0] += 1
        n = nk * ncols
        S.dma("sp", [(lz_stage[k][:, 0:n].rearrange("p (k n) -> p k n", k=nk),
                      wap[kt0 * 128:(kt0 + nk) * 128, col0:col0 + ncols].rearrange("(k p) n -> p k n", p=128))], [], lz_stage_res[k], "lzs%d" % k)
        eng = ("dve", "act")[lz_ctr[0] % 2]
        if eng == "act":
            A("act", "activation", lz_stage_res[k], lz_out_res[k], out=lz_out[k][:, 0:n], in_=lz_stage[k][:, 0:n], func=AF.Copy)
        else:
            A("dve", "tensor_copy", lz_stage_res[k], lz_out_res[k], out=lz_out[k][:, 0:n], in_=lz_stage[k][:, 0:n])
        S.dma("pool", [(wsc[ci, :, off:off + n], lz_out[k][:, 0:n])], lz_out_res[k], ["wsc%d" % ci], "lzo%d" % k)

    for ci, pieces in enumerate(chunks):
        if ci >= N_EAGER:
            off = 0
            for (wap, kt0, nk, col0, ncols) in pieces:
                step = max(1, 1024 // ncols)
                for q0 in range(0, nk, step):
                    nq = min(step, nk - q0)
                    lazy_pieces.append((ci, off, wap, kt0 + q0, nq, col0, ncols))
                    off += nq * ncols
            continue
        sidx = ci % 2
        stg = stage[sidx]
        pairs = []
        off = 0
        for (wap, kt0, nk, col0, ncols) in pieces:
            pairs.append((stg[:, off:off + nk * ncols].rearrange("p (k n) -> p k n", k=nk),
                          wap[kt0 * 128:(kt0 + nk) * 128, col0:col0 + ncols].rearrange("(k p) n -> p k n", p=128)))
            off += nk * ncols
        S.dma("sp", pairs, [], stage_res[sidx], "pst%d" % sidx)
        cslot = 1 + sidx
        eng = cast_eng[ci % 3]
        if eng == "act":
            A("act", "activation", stage_res[sidx], ["w%d" % cslot], out=wring[:, cslot, 0:off], in_=stg[:, 0:off], func=AF.Copy)
        else:
            A(eng, "tensor_copy", stage_res[sidx], ["w%d" % cslot], out=wring[:, cslot, 0:off], in_=stg[:, 0:off])
        S.dma("pool", [(wsc[ci, :, 0:off], wring[:, cslot, 0:off])], ["w%d" % cslot], ["wsc%d" % ci], "stc%d" % sidx)

    chunk_seq = []
    for t_ in range(NT):
        chunk_seq += (list(range(17)) + [18, 19, 20, 21]) if t_ < NL else list(range(41))
    if do_sample:
        chunk_seq += list(range(41))
    total_chunks = len(chunk_seq)
    ws = dict(cur=0, issued=0)

    def next_chunk():
        i = ws["cur"]
        ws["cur"] += 1
        while ws["issued"] < min(total_chunks, i + NS):
            k = ws["issued"]
            slot = k % NS
            ci = chunk_seq[k]
            n = chunk_len[ci]
            S.dma("sp", [(wring[:, slot, 0:n], wsc[ci, :, 0:n])], ["wsc%d" % ci], ["w%d" % slot], "w%d" % slot)
            ws["issued"] += 1
        return i % NS

    def rsqrt_small(ap, n_inv, res, rows=slice(0, 128)):
        A("dve", "tensor_scalar", [res], [res], out=ap, in0=ap, scalar1=n_inv, scalar2=EPS, op0=ALU.mult, op1=ALU.add)
        A("act", "activation", [res], [res], out=ap, in_=ap, func=AF.Ln)
        A("act", "activation", [res], [res], out=ap, in_=ap, func=AF.Exp, scale=-0.5)

    def norm_T(gi):
        for sub in range(4):
            A("act", "activation", [XT[sub]], ["sg0", "sg1", "ssn"], out=sg[:].rearrange("p a b -> p (a b)"), in_=xt[:, sub, :], func=AF.Square,
              accum_out=st[:, sub:sub + 1])
        rsqrt_small(st[:, 0:4], 1.0 / 1024, "ssn")
        for sub in range(4):
            xb = "xsb%d" % (sub % 2)
            A("act", "activation", [XT[sub], "ssn"], [xb], out=xsb[:, sub % 2, :], in_=xt[:, sub, :], func=AF.Copy,
              scale=st[:, sub:sub + 1])
            bk = 6 + (sub % 2)
            for kt in range(8):
                A("pe", "transpose", [xb, "identb"], ["ps%d" % bk], out=psb[bk][:, kt * 128:(kt + 1) * 128],
                  in_=xsb[:, sub % 2, kt * 128:(kt + 1) * 128], identity=identb[:])
            A("dve", "tensor_tensor", ["ps%d" % bk, "const"], [XN[sub]], out=xnT[:, :, sub * 128:(sub + 1) * 128],
              in0=psb[bk][:, 0:1024].rearrange("p (k n) -> p k n", k=8),
              in1=gT[:, gi, :].unsqueeze(2).to_broadcast([128, 8, 128]), op=ALU.mult)

    pending = dict(ops=None)

    def ffn():
        pend = pending["ops"]
        rate = (len(pend) + 19) // 20 if pend else 0
        for grp in range(11):
            slot = next_chunk()
            wres = "w%d" % slot
            w = wring[:, slot, :].rearrange("p (g k n) -> p g k n", g=2, k=8)
            for j in range(2):
                nt = grp * 2 + j
                bg, bu = (0, 1) if nt % 2 == 0 else (2, 3)
                for kt in range(8):
                    A("pe", "matmul", [wres] + XN, ["ps%d" % bg], out=ps[bg][:], lhsT=w[:, 0, kt, j * 128:(j + 1) * 128],
                      rhs=xnT[:, kt, :], start=(kt == 0), stop=(kt == 7))
                for kt in range(8):
                    A("pe", "matmul", [wres] + XN, ["ps%d" % bu], out=ps[bu][:], lhsT=w[:, 1, kt, j * 128:(j + 1) * 128],
                      rhs=xnT[:, kt, :], start=(kt == 0), stop=(kt == 7))
                sgn = "sg%d" % (nt % 2)
                A("act", "activation", ["ps%d" % bg], [sgn], out=sg[:, nt % 2, :], in_=ps[bg][:], func=AF.Exp, scale=-1.0)
                A("act", "activation", [sgn], [sgn], out=sg[:, nt % 2, :], in_=sg[:, nt % 2, :], func=AF.Ln, bias=1.0)
                A("act", "activation", [sgn], [sgn], out=sg[:, nt % 2, :], in_=sg[:, nt % 2, :], func=AF.Exp, scale=-1.0)
                A("dve", "tensor_tensor", [sgn, "ps%d" % bg], [sgn], out=sg[:, nt % 2, :], in0=sg[:, nt % 2, :], in1=ps[bg][:], op=ALU.mult)
                A("dve", "tensor_tensor", [sgn, "ps%d" % bu], [HID[nt]], out=hidT[:, nt, :], in0=sg[:, nt % 2, :], in1=ps[bu][:], op=ALU.mult)
                if pend:
                    pump(pend, rate)
                if lazy_pieces:
                    lazy_piece(*lazy_pieces.pop(0))
        if pend is not None:
            pump(pend, len(pend))
            pending["ops"] = None
        for half in range(2):
            banks = [4, 5, 6, 7] if half == 0 else [0, 1, 2, 3]
            for g in range(3):
                slot = next_chunk()
                wres = "w%d" % slot
                w = wring[:, slot, :].rearrange("p (k n) -> p k n", k=8)
                for kti, kt in enumerate(range(g * 8, min(22, g * 8 + 8))):
                    for sub in range(4):
                        A("pe", "matmul", [wres, HID[kt]], ["ps%d" % banks[sub]], out=ps[banks[sub]][:],
                          lhsT=hidT[:, kt, sub * 128:(sub + 1) * 128], rhs=w[:, kti, :], start=(kt == 0), stop=(kt == 21))
            for sub in range(4):
                A("dve", "scalar_tensor_tensor", ["ps%d" % banks[sub], XT[sub]], [XT[sub]],
                  out=xt[:, sub, half * 512:(half + 1) * 512], in0=ps[banks[sub]][:], scalar=0.5,
                  in1=xt[:, sub, half * 512:(half + 1) * 512], op0=ALU.mult, op1=ALU.add)

    def w_in(sample, light=False):
        for blk in range(3):
            if light and blk == 0:
                continue
            slot = next_chunk()
            wres = "w%d" % slot
            w = wring[:, slot, :].rearrange("p (k n) -> p k n", k=8)
            banks = [0, 1, 2, 3] if blk % 2 == 0 else [4, 5, 6, 7]
            for sub in range(4):
                for kt in range(8):
                    A("pe", "matmul", [wres, XN[sub]], ["ps%d" % banks[sub]], out=ps[banks[sub]][:],
                      lhsT=xnT[:, kt, sub * 128:(sub + 1) * 128], rhs=w[:, kt, :], start=(kt == 0), stop=(kt == 7))
            for sub in range(4):
                pb = "ps%d" % banks[sub]
                pv = ps[banks[sub]][:]
                if blk < 2:
                    ti = sub % 2
                    tn = "tmpn%d" % ti
                    ssq = st[:, 8 + 8 * ti:16 + 8 * ti]
                    A("act", "activation", [pb], [tn], out=tmpn[:, ti, :], in_=pv, func=AF.Square)
                    A("dve", "tensor_reduce", [tn], ["ssq%d" % ti], out=ssq, in_=tmpn[:, ti, :].rearrange("p (g d) -> p g d", g=8),
                      axis=AX.X, op=ALU.add)
                    rsqrt_small(ssq, 1.0 / 64, "ssq%d" % ti)
                    A("dve", "tensor_tensor", [pb, "ssq%d" % ti, tn], [tn], out=tmpn[:, ti, :].rearrange("p (g d) -> p g d", g=8),
                      in0=pv.rearrange("p (g d) -> p g d", g=8), in1=ssq.unsqueeze(2).to_broadcast([128, 8, 64]), op=ALU.mult)
                    if blk == 0:
                        A("pool", "tensor_tensor", [tn, "const"], ["q_tm%d" % sub], out=q_tm[:, sub, :], in0=tmpn[:, ti, :],
                          in1=gq_rep[:].rearrange("p g d -> p (g d)"), op=ALU.mult)
                    else:
                        A("pool", "tensor_tensor", [tn, "const"], ["k_tm%d" % sub], out=k_tm[:, sub, :], in0=tmpn[:, ti, :],
                          in1=gk_rep[:].rearrange("p g d -> p (g d)"), op=ALU.mult)
                        A("pool", "tensor_copy", ["k_tm%d" % sub], ["k_bf%d" % sub], out=k_bf[:, sub, :], in_=k_tm[:, sub, :])
                else:
                    A("act", "activation", [pb], ["v_tm%d" % sub], out=v_tm[:, sub, :], in_=pv, func=AF.Copy)
                    if light:
                        A("dve", "tensor_scalar", ["v_tm%d" % sub, "const"], ["VAn"], out=VAn[:, sub, :, 0:128],
                          in0=v_tm[:, sub, :].rearrange("p (h e) -> p h e", h=4), scalar1=flagt[:, 0:1], scalar2=None, op0=ALU.mult)
                    else:
                        A("pool", "tensor_copy", ["v_tm%d" % sub], ["VAn"], out=VAn[:, sub, :, 0:128],
                          in_=v_tm[:, sub, :].rearrange("p (h e) -> p h e", h=4))
        for blk in (3, 4):
            slot = next_chunk()
            wres = "w%d" % slot
            w = wring[:, slot, :].rearrange("p (k n) -> p k n", k=8)
            for c in range(4):
                bk = 4 + c if blk == 3 else c
                for kt in range(8):
                    A("pe", "matmul", [wres] + XN, ["ps%d" % bk], out=ps[bk][:], lhsT=w[:, kt, c * 128:(c + 1) * 128],
                      rhs=xnT[:, kt, :], start=(kt == 0), stop=(kt == 7))
                if blk == 3:
                    if sample:
                        A("act", "activation", ["ps%d" % bk], ["xr%d" % c], out=xrS[:, c, :, 3:67],
                          in_=ps[bk][:].rearrange("p (s n) -> p s n", s=8), func=AF.Copy)
                    else:
                        A("act", "activation", ["ps%d" % bk], ["xr%d" % c], out=xrP[:, c, 3:515], in_=ps[bk][:], func=AF.Copy)
                else:
                    A("dve", "tensor_copy", ["ps%d" % bk], ["xg%d" % c], out=xgT[:, c, :], in_=ps[bk][:])
        for (src, sname, dst, dname, bk) in ((q_tm, "q_tm", QT, "QT", 6), (k_bf, "k_bf", KTn, "KTn", 7)):
            if light and sname == "q_tm":
                continue
            for h in range(4):
                for sub in range(4):
                    A("pe", "transpose", ["%s%d" % (sname, sub), "identb"], ["ps%d" % bk], out=psb[bk][:, sub * 128:(sub + 1) * 128],
                      in_=src[:, sub, h * 128:(h + 1) * 128], identity=identb[:])
                A("dve" if h % 2 == 0 else "act", "tensor_copy" if h % 2 == 0 else "copy", ["ps%d" % bk], [dname],
                  out=dst[:, h, :], in_=psb[bk][:, 0:512])

    def Oacc(idx):
        b = 4 + idx // 3
        c0 = (idx % 3) * 130
        return b, c0

    pt_ctr = [0]

    def attn_prefetch(t, extra_writes=()):
        nprefix = 4 * t
        ch_list = [(c0, min(16, nprefix - c0)) for c0 in range(0, nprefix, 16)]
        stt = dict(t=t, nprefix=nprefix, ch_list=ch_list, loads=[(h, ci) for h in range(4) for ci in range(len(ch_list))],
                   issued=0, slot_of={}, extra=list(extra_writes))
        issue_load(stt)
        issue_load(stt)
        return stt

    def issue_load(stt):
        k = stt["issued"]
        if k >= len(stt["loads"]):
            return
        h, ci = stt["loads"][k]
        c0, n = stt["ch_list"][ci]
        slot = kv_ctr[0] % 2
        kv_ctr[0] += 1
        stt["slot_of"][(h, ci)] = slot
        tl = sorted(set((c0 + j) // 4 for j in range(n)))
        S.dma("sp", [(KTc[:, slot, 0:n * 128], KT_s[h, :, c0 * 128:(c0 + n) * 128])], ["kts%d" % tt for tt in tl],
              ["ktc%d" % slot] + stt["extra"], "ktc%d" % slot)
        S.dma("sp", [(VAc[:, slot, 0:n, :], VA_s[h, :, c0:c0 + n, :])], ["vas%d" % tt for tt in tl],
              ["vac%d" % slot] + stt["extra"], "vac%d" % slot)
        stt["issued"] += 1

    def attn_prompt(stt, side_ops=None):
        nprefix = stt["nprefix"]
        ch_list = stt["ch_list"]
        slot_of = stt["slot_of"]
        blocks = []
        for h in range(4):
            for ci, (c0, n) in enumerate(ch_list):
                for j in range(n):
                    blocks.append(dict(h=h, load=(h, ci), j=j, q_lo=0, bias=("pl" if c0 + j == nprefix - 1 else None), local=None))
            for jb in range(4):
                blocks.append(dict(h=h, load=None, j=jb, q_lo=128 * jb, bias="loc", local=jb))

        def qk(bl):
            h = bl["h"]
            pbi = pt_ctr[0] % 2
            ptb = pt_ctr[0] % 3
            pt_ctr[0] += 1
            bl["pb"] = ptb
            q_lo = bl["q_lo"]
            if bl["load"] is not None:
                slot = slot_of[bl["load"]]
                bl["slot"] = slot
                kt_ap = KTc[:, slot, bl["j"] * 128:(bl["j"] + 1) * 128]
                kres = ["ktc%d" % slot]
            else:
                kt_ap = KTn[:, h, bl["j"] * 128:(bl["j"] + 1) * 128]
                kres = ["KTn"]
            for c in range(2):
                bk = pbi * 2 + c
                A("pe", "matmul", kres + ["QT"], ["ps%d" % bk], out=ps[bk][:, q_lo:512], lhsT=kt_ap[64 * c:64 * c + 64, :],
                  rhs=QT[64 * c:64 * c + 64, h, q_lo:512], start=True, stop=True)
            ptn = ["pt%d0" % ptb, "pt%d1" % ptb]
            A("act", "activation", ["ps%d" % (pbi * 2), "ps%d" % (pbi * 2 + 1)], ptn, out=PT[:, ptb, :, q_lo:512],
              in_=psall[:, pbi * 2:pbi * 2 + 2, q_lo:512], func=AF.Exp, scale=0.125)
            if bl["bias"] == "pl":
                A("dve", "tensor_tensor", ptn + ["ET"], ptn, out=PT[:, ptb, :, 0:128], in0=PT[:, ptb, :, 0:128],
                  in1=ET[:, h, 128:256].unsqueeze(1).to_broadcast([128, 2, 128]), op=ALU.mult)
            elif bl["bias"] == "loc":
                hi = min(512, q_lo + 256)
                A("dve", "tensor_tensor", ptn + ["ET"], ptn, out=PT[:, ptb, :, q_lo:hi], in0=PT[:, ptb, :, q_lo:hi],
                  in1=ET[:, h, 0:hi - q_lo].unsqueeze(1).to_broadcast([128, 2, hi - q_lo]), op=ALU.mult)

        started = set()

        def pv(bl, first):
            h = bl["h"]
            if first:
                started.clear()
            pbi = bl["pb"]
            if bl["load"] is not None:
                va_ap = VAc[:, bl["slot"], bl["j"], 0:129]
                vres = ["vac%d" % bl["slot"]]
            else:
                va_ap = VAn[:, bl["j"], h, 0:129]
                vres = ["VAn"]
            for qs in range(bl["q_lo"] // 128, 4):
                last = (bl["local"] == qs)
                for c in range(2):
                    b, c0 = Oacc(c * 4 + qs)
                    st_flag = first and (b not in started)
                    started.add(b)
                    A("pe", "matmul", vres + ["pt%d%d" % (pbi, c)], ["ps%d" % b], out=ps[b][:, c0:c0 + 129],
                      lhsT=PT[:, pbi, c, qs * 128:(qs + 1) * 128], rhs=va_ap, start=st_flag, stop=last, skip_group_check=True)

        def epilogue(h):
            for k in range(3):
                A("dve", "tensor_copy", ["ps%d" % (4 + k)], ["Osb%d" % k], out=Osb[:, k, :], in_=ps[4 + k][:, 0:390])
            for qs in range(4):
                b0, c0 = Oacc(qs)
                b1, c1 = Oacc(4 + qs)
                O0 = Osb[:, b0 - 4, c0:c0 + 129]
                O1 = Osb[:, b1 - 4, c1:c1 + 129]
                r0, r1 = "Osb%d" % (b0 - 4), "Osb%d" % (b1 - 4)
                A("dve", "reciprocal", [r0], ["rec0"], out=st[:, 32:33], in_=O0[:, 128:129])
                A("dve", "reciprocal", [r1], ["rec1"], out=st[:, 33:34], in_=O1[:, 128:129])
                A("dve", "tensor_scalar", [r1, "rec1", "neglam"], ["osb4"], out=osb[:, 4, :], in0=O1[:, 0:128],
                  scalar1=st[:, 33:34], scalar2=neglam[:, 0:1], op0=ALU.mult, op1=ALU.mult)
                A("dve", "scalar_tensor_tensor", [r0, "rec0", "osb4"], ["osb%d" % qs], out=osb[:, qs, :], in0=O0[:, 0:128],
                  scalar=st[:, 32:33], in1=osb[:, 4, :], op0=ALU.mult, op1=ALU.add)
                A("dve", "tensor_tensor", ["osb%d" % qs], ["osb4"], out=osb[:, 4, :], in0=osb[:, qs, :], in1=osb[:, qs, :], op=ALU.mult)
                A("dve", "tensor_reduce", ["osb4"], ["ssh"], out=st[:, 36 + qs:37 + qs], in_=osb[:, 4, :], axis=AX.X, op=ALU.add)
            A("dve", "tensor_scalar", ["ssh"], ["ssh"], out=st[:, 36:40], in0=st[:, 36:40], scalar1=1.0 / 128, scalar2=EPS, op0=ALU.mult, op1=ALU.add)
            A("pool", "tensor_tensor", ["ssh", "cm05"], ["ssh"], out=st[:, 36:40], in0=st[:, 36:40], in1=cm05[:, 0:1].to_broadcast([128, 4]), op=ALU.pow)
            for qs in range(4):
                A("dve", "scalar_tensor_tensor", ["osb%d" % qs, "ssh", "gsub"], ["o_tm%d" % qs], out=o_tm[:, qs, h * 128:(h + 1) * 128],
                  in0=osb[:, qs, :], scalar=st[:, 36 + qs:37 + qs], in1=gsub_rep[:], op0=ALU.mult, op1=ALU.mult)

        nb = len(blocks)
        side_rate = (len(side_ops) + nb - 9) // max(1, nb - 8) if side_ops else 0
        qk(blocks[0])
        if nb > 1:
            qk(blocks[1])
        for i, bl in enumerate(blocks):
            if i + 2 < nb:
                qk(blocks[i + 2])
            first = (i == 0) or (blocks[i - 1]["h"] != bl["h"])
            pv(bl, first)
            if side_ops:
                pump(side_ops, side_rate)
            if bl["load"] is not None and (i + 1 >= nb or blocks[i + 1]["load"] != bl["load"]):
                issue_load(stt)
            if i + 1 >= nb or blocks[i + 1]["h"] != bl["h"]:
                epilogue(bl["h"])

    kv_ctr = [0]

    def attn_sample(side_ops=None):
        nch = PKB // 4
        seq_ch = [(s_, ch_) for s_ in range(NSR) for ch_ in range(nch)]

        def load_chunk(k):
            if k >= len(seq_ch):
                return
            s_, ch_ = seq_ch[k]
            S.dma("sp", [(kst, ck_d[s_, ch_ * 512:(ch_ + 1) * 512, :].rearrange("(k p) d -> p k d", p=128))], [],
                  ["ktc0", "ktc1"], "kst")
            S.dma("sp", [(vst, cv_d[s_, ch_ * 512:(ch_ + 1) * 512, :].rearrange("(k p) d -> p k d", p=128))], [],
                  ["vac0", "vac1"], "vst")

        side_rate = (len(side_ops) + len(seq_ch) * 8 - 9) // (len(seq_ch) * 8 - 8) if side_ops else 0
        load_chunk(0)
        for s in range(NSR):
            pair = s // 2
            started_s = set()
            qc0 = pair * 128
            R0 = (s % 2) * 64
            for ch in range(nch):
                A("pool", "tensor_copy", ["ktc0", "ktc1"], ["kbf"], out=kbf, in_=kst)
                A("dve", "tensor_copy", ["vac0", "vac1"], ["VAq"], out=VAq[:, :, :, 0:128],
                  in_=vst.rearrange("p k (h e) -> p k h e", h=4))
                load_chunk(s * nch + ch + 1)
                for h in range(4):
                    bk = 7
                    for kb in range(4):
                        A("pe", "transpose", ["kbf", "identb"], ["ps%d" % bk], out=psb[bk][:, kb * 128:(kb + 1) * 128],
                          in_=kbf[:, kb, h * 128:(h + 1) * 128], identity=identb[:])
                    A("dve", "tensor_copy", ["ps%d" % bk], ["KTq%d" % h], out=KTq[:, h, :], in_=psb[bk][:, 0:512])
                for h in range(4):
                    for c in range(2):
                        pbi = pt_ctr[0] % 2
                        pt_ctr[0] += 1
                        bk = pbi * 2 + c
                        ptn = "pt%d%d" % (pbi, c)
                        for kb in range(4):
                            A("pe", "matmul", ["KTq%d" % h, "QT"], ["ps%d" % bk], out=ps[bk][:, kb * 128:(kb + 1) * 128],
                              lhsT=KTq[64 * c:64 * c + 64, h, kb * 128:(kb + 1) * 128], rhs=QT[64 * c:64 * c + 64, h, qc0:qc0 + 128],
                              start=True, stop=True)
                        A("act", "activation", ["ps%d" % bk], [ptn], out=PT[:, pbi, c, :], in_=ps[bk][:], func=AF.Exp, scale=0.125)
                        if ch == nch - 1:
                            a0 = 3 * 128 + R0
                            A("dve", "tensor_tensor", [ptn, "ET"], [ptn], out=PT[:, pbi, c, a0:a0 + 64], in0=PT[:, pbi, c, a0:a0 + 64],
                              in1=ET[:, h, 128:192], op=ALU.mult)
                        b, c0 = Oacc(h * 2 + c)
                        for kb in range(4):
                            st_flag = (ch == 0 and kb == 0 and b not in started_s)
                            started_s.add(b)
                            A("pe", "matmul", ["VAq", ptn], ["ps%d" % b], out=ps[b][:, c0:c0 + 129],
                              lhsT=PT[:, pbi, c, kb * 128:(kb + 1) * 128], rhs=VAq[:, kb, h, 0:129],
                              start=st_flag, stop=False, skip_group_check=True)
                        if side_ops:
                            pump(side_ops, side_rate)
            for h in range(4):
                for c in range(2):
                    pbi = pt_ctr[0] % 2
                    pt_ctr[0] += 1
                    bk = pbi * 2 + c
                    ptn = "pt%d%d" % (pbi, c)
                    A("pe", "matmul", ["KTn", "QT"], ["ps%d" % bk], out=ps[bk][:, 0:128], lhsT=KTn[64 * c:64 * c + 64, h, qc0:qc0 + 128],
                      rhs=QT[64 * c:64 * c + 64, h, qc0:qc0 + 128], start=True, stop=True)
                    A("act", "activation", ["ps%d" % bk], [ptn], out=PT[:, pbi, c, 0:128], in_=ps[bk][:, 0:128], func=AF.Exp, scale=0.125)
                    A("dve", "tensor_tensor", [ptn, "ET"], [ptn], out=PT[:, pbi, c, 0:128], in0=PT[:, pbi, c, 0:128],
                      in1=ETp[:, h, :], op=ALU.mult)
                    b, c0 = Oacc(h * 2 + c)
                    A("pe", "matmul", ["VAn", ptn], ["ps%d" % b], out=ps[b][:, c0:c0 + 129], lhsT=PT[:, pbi, c, 0:128],
                      rhs=VAn[:, pair, h, 0:129], start=False, stop=True, skip_group_check=True)
            R = slice(R0, R0 + 64)
            for h in range(4):
                b0, c0 = Oacc(h * 2)
                b1, c1 = Oacc(h * 2 + 1)
                O0 = ps[b0][R, c0:c0 + 129]
                O1 = ps[b1][R, c1:c1 + 129]
                A("dve", "reciprocal", ["ps%d" % b0], ["rec"], out=st[R, 32:33], in_=O0[:, 128:129])
                A("dve", "reciprocal", ["ps%d" % b1], ["rec"], out=st[R, 33:34], in_=O1[:, 128:129])
                A("dve", "tensor_scalar", ["ps%d" % b1, "rec", "neglam"], ["osb4"], out=osb[R, 4, :], in0=O1[:, 0:128],
                  scalar1=st[R, 33:34], scalar2=neglam[R, 0:1], op0=ALU.mult, op1=ALU.mult)
                A("dve", "scalar_tensor_tensor", ["ps%d" % b0, "rec", "osb4"], ["osb%d" % h], out=osb[R, h, :], in0=O0[:, 0:128],
                  scalar=st[R, 32:33], in1=osb[R, 4, :], op0=ALU.mult, op1=ALU.add)
                A("act", "activation", ["osb%d" % h], ["sg0", "ssh"], out=sg[R, 0, 0:128], in_=osb[R, h, :], func=AF.Square,
                  accum_out=st[R, 36 + h:37 + h])
            rsqrt_small(st[R, 36:40], 1.0 / 128, "ssh", rows=R)
            for h in range(4):
                A("dve", "scalar_tensor_tensor", ["osb%d" % h, "ssh", "gsub"], ["o_tm%d" % pair], out=o_tm[R, pair, h * 128:(h + 1) * 128],
                  in0=osb[R, h, :], scalar=st[R, 36 + h:37 + h], in1=gsub_rep[R, :], op0=ALU.mult, op1=ALU.mult)

    def o_transposes():
        for h in range(4):
            bk = 6 + (h % 2)
            for sub in range(4):
                A("pe", "transpose", ["o_tm%d" % sub, "identb"], ["ps%d" % bk], out=psb[bk][:, sub * 128:(sub + 1) * 128],
                  in_=o_tm[:, sub, h * 128:(h + 1) * 128], identity=identb[:])
            A("dve" if h % 2 == 0 else "act", "tensor_copy" if h % 2 == 0 else "copy", ["ps%d" % bk], [CAT[h]],
              out=catT[:, h, :], in_=psb[bk][:, 0:512])

    def lru_build(sample, light, lt, ltn, gb, nb):
        ops = []
        ops_c = [[] for _ in range(4)]
        cur = [ops]
        two_sets = isinstance(lt, tuple)
        lt_sets, ltn_sets = (lt, ltn) if two_sets else ((lt, lt), (ltn, ltn))

        def flat(lst):
            out = []
            for x in lst:
                if isinstance(x, (list, tuple)):
                    out += list(x)
                else:
                    out.append(x)
            return out

        def A(eng, method, reads, writes, **kw):
            cur[0].append((eng, method, flat(reads), flat(writes), kw))
        nsq, tl = (8, 64) if sample else (1, 512)

        def v3(ap):
            return ap.rearrange("p (s n) -> p s n", s=nsq)

        for c in range(4):
            xr = "xr%d" % c
            cur[0] = ops_c[c]
            lt, ltn = lt_sets[c % 2], ltn_sets[c % 2]

            def xp(j):
                if sample:
                    return xrS[:, c, :, j:j + 64]
                return xrP[:, c, j:j + 512].rearrange("p (s n) -> p s n", s=1)
            A("dve", "tensor_scalar", [xr, "const"], [ltn[0]], out=v3(lt[0]), in0=xp(3), scalar1=cw[:, 3, c:c + 1], scalar2=vecs[:, 0, c:c + 1],
              op0=ALU.mult, op1=ALU.add)
            for j in range(3):
                A("dve", "scalar_tensor_tensor", [xr, "const", ltn[0]], [ltn[0]], out=v3(lt[0]), in0=xp(j), scalar=cw[:, j, c:c + 1],
                  in1=v3(lt[0]), op0=ALU.mult, op1=ALU.add)
            ba_, bx_ = gb[c % len(gb)]
            A("pe", "matmul", [ltn[0], "Wbd"], ["ps%d" % ba_], out=ps[ba_][:], lhsT=Wbd[:, 0, c, :], rhs=lt[0], start=True, stop=True)
            A("act", "activation", ["ps%d" % ba_, "const"], [ltn[1]], out=lt[1], in_=ps[ba_][:], func=AF.Exp, bias=vecs[:, 1, c:c + 1], scale=-1.0)
            A("pe", "matmul", [ltn[0], "Wbd"], ["ps%d" % bx_], out=ps[bx_][:], lhsT=Wbd[:, 1, c, :], rhs=lt[0], start=True, stop=True)
            A("act", "activation", ["ps%d" % bx_, "const"], [ltn[2]], out=lt[2], in_=ps[bx_][:], func=AF.Exp, bias=vecs[:, 2, c:c + 1], scale=-1.0)
            for k_ in (1, 2):
                A("act", "activation", [ltn[k_]], [ltn[k_]], out=lt[k_], in_=lt[k_], func=AF.Ln, bias=1.0)
                A("act", "activation", [ltn[k_]], [ltn[k_]], out=lt[k_], in_=lt[k_], func=AF.Exp, scale=-1.0)
            if not light:
                A("pool", "tensor_tensor", ["xg%d" % c], [ltn[4]], out=lt[4], in0=xgT[:, c, :], in1=xgT[:, c, :], op=ALU.mult)
                A("dve", "tensor_scalar", [ltn[4]], [ltn[4]], out=lt[4], in0=lt[4], scalar1=0.044715 * 1.5957691216057308,
                  scalar2=1.5957691216057308, op0=ALU.mult, op1=ALU.add)
                A("pool", "tensor_tensor", [ltn[4], "xg%d" % c], [ltn[4]], out=lt[4], in0=lt[4], in1=xgT[:, c, :], op=ALU.mult)
                A("act", "activation", [ltn[4]], [ltn[4]], out=lt[4], in_=lt[4], func=AF.Exp, scale=-1.0)
                A("act", "activation", [ltn[4]], [ltn[4]], out=lt[4], in_=lt[4], func=AF.Ln, bias=1.0)
                A("act", "activation", [ltn[4]], [ltn[4]], out=lt[4], in_=lt[4], func=AF.Exp, scale=-1.0)
            A("act", "activation", [ltn[1], "nsp"], [ltn[1]], out=lt[1], in_=lt[1], func=AF.Exp, scale=vecs[:, 5, c:c + 1])
            A("dve", "tensor_tensor", [ltn[1]], [ltn[3]], out=lt[3], in0=lt[1], in1=lt[1], op=ALU.mult)
            A("dve", "tensor_scalar", [ltn[3]], [ltn[3]], out=lt[3], in0=lt[3], scalar1=-1.0, scalar2=1.0, op0=ALU.mult, op1=ALU.add)
            A("act", "activation", [ltn[3]], [ltn[3]], out=lt[3], in_=lt[3], func=AF.Ln)
            A("act", "activation", [ltn[3]], [ltn[3]], out=lt[3], in_=lt[3], func=AF.Exp, scale=0.5)
            A("pool", "tensor_tensor", [ltn[2], ltn[3]], [ltn[2]], out=lt[2], in0=lt[2], in1=lt[3], op=ALU.mult)
            A("pool", "tensor_tensor", [ltn[2], ltn[0]], [ltn[2]], out=lt[2], in0=lt[2], in1=lt[0], op=ALU.mult)
            if sample:
                for s in range(8):
                    A("dve", "tensor_tensor_scan", [ltn[1], ltn[2], "h0T"], [ltn[5]], out=lt[5][:, s * 64:(s + 1) * 64],
                      data0=lt[1][:, s * 64:(s + 1) * 64], data1=lt[2][:, s * 64:(s + 1) * 64], initial=h0T[:, c, s:s + 1],
                      op0=ALU.mult, op1=ALU.add)
                A("dve", "tensor_copy", [ltn[5]], ["hlast"], out=hlast[:, c, :], in_=v3(lt[5])[:, :, 63])
                A("pool", "tensor_copy", [xr], ["tails"], out=tails[:, c, :].rearrange("p (s j) -> p s j", s=8), in_=xrS[:, c, :, 64:67])
            else:
                A("dve", "tensor_tensor_scan", [ltn[1], ltn[2], "hcar"], [ltn[5]], out=lt[5], data0=lt[1], data1=lt[2],
                  initial=hcar[:, c:c + 1], op0=ALU.mult, op1=ALU.add)
                A("dve", "tensor_copy", [ltn[5]], ["hcar"], out=hcar[:, c:c + 1], in_=lt[5][:, 511:512])
                A("pool", "tensor_copy", [xr], [xr], out=xrP[:, c, 0:3], in_=xrP[:, c, 512:515])
            if not light:
                A("dve", "tensor_tensor", [ltn[5], "xg%d" % c], [ltn[3]], out=lt[3], in0=lt[5], in1=xgT[:, c, :], op=ALU.mult)
                A("dve", "tensor_tensor", [ltn[3], ltn[4]], ["hg%d" % c], out=hg[:, c, :], in0=lt[3], in1=lt[4], op=ALU.mult)
        cur[0] = ops
        lt, ltn = lt_sets[0], ltn_sets[0]
        if two_sets:
            for c0 in (0, 2):
                a_, b_ = ops_c[c0], ops_c[c0 + 1]
                for k_ in range(max(len(a_), len(b_))):
                    if k_ < len(a_):
                        ops.append(a_[k_])
                    if k_ < len(b_):
                        ops.append(b_[k_])
        else:
            for c in range(4):
                ops.extend(ops_c[c])
        if light:
            return ops
        for c in range(4):
            A("act", "activation", ["hg%d" % c], [ltn[c]], out=lt[c], in_=hg[:, c, :], func=AF.Square)
        for c in range(4):
            A("pe", "matmul", ["lt%d" % c, "onesf"], ["ps%d" % nb], out=ps[nb][:], lhsT=onesf[:], rhs=lt[c], start=(c == 0), stop=(c == 3))
        A("dve", "tensor_scalar", ["ps%d" % nb], ["rstdL"], out=rstdL, in0=ps[nb][:], scalar1=1.0 / 512, scalar2=EPS, op0=ALU.mult, op1=ALU.add)
        A("act", "activation", ["rstdL"], ["rstdL"], out=rstdL, in_=rstdL, func=AF.Ln)
        A("act", "activation", ["rstdL"], ["rstdL"], out=rstdL, in_=rstdL, func=AF.Exp, scale=-0.5)
        for c in range(4):
            A("dve", "scalar_tensor_tensor", ["hg%d" % c, "rstdL", "const"], [CAT[4 + c]], out=catT[:, 4 + c, :], in0=hg[:, c, :],
              scalar=vecs[:, 4, c:c + 1], in1=rstdL, op0=ALU.mult, op1=ALU.mult)
        return ops

    def pump(ops, n):
        while n > 0 and ops:
            e_, m_, r_, w_, kw_ = ops.pop(0)
            A(e_, m_, r_, w_, **kw_)
            n -= 1

    def w_out():
        for half in range(2):
            slot = next_chunk()
            wres = "w%d" % slot
            w = wring[:, slot, :].rearrange("p (k n) -> p k n", k=8)
            banks = [0, 1, 2, 3] if half == 0 else [4, 5, 6, 7]
            for sub in range(4):
                for kt in range(8):
                    A("pe", "matmul", [wres, CAT[kt]], ["ps%d" % banks[sub]], out=ps[banks[sub]][:],
                      lhsT=catT[:, kt, sub * 128:(sub + 1) * 128], rhs=w[:, kt, :], start=(kt == 0), stop=(kt == 7))
            for sub in range(4):
                A("dve", "tensor_tensor", ["ps%d" % banks[sub], XT[sub]], [XT[sub]], out=xt[:, sub, half * 512:(half + 1) * 512],
                  in0=ps[banks[sub]][:], in1=xt[:, sub, half * 512:(half + 1) * 512], op=ALU.add)

    def tm_view(d_ap, r0):
        return d_ap[r0:r0 + 512, :].rearrange("(s p) d -> p s d", p=128)

    tiles = [("p", t) for t in range(NT)] + ([("s", 0)] if do_sample else [])
    for kind, t in tiles:
        sample = kind == "s"
        light = (not sample) and t < NL
        t0 = t * 512
        o0 = (t - NL) * 512
        xsrc = tm_view(xs_d if sample else x_d, 0 if sample else t0)
        for sub in range(4):
            S.dma("sp", [(xt[:, sub, :], xsrc[:, sub, :])], [], [XT[sub]], "ld_x%d" % sub)
        att_state = None
        if (not sample) and (not light) and not (half and t == NL):
            att_state = attn_prefetch(t)
        if sample:
            S.dma("sp", [(sm_tm[0:24, 0, :], sconv_d), (sm_tm[0:8, 1, :], slru_d)], [], ["tmpn0", "tmpn1"], "ld_sm")
            for c in range(4):
                A("pe", "transpose", ["tmpn0", "tmpn1", "const"], ["ps6"], out=ps[6][:, c * 32:c * 32 + 24], in_=sm_tm[0:24, 0, c * 128:(c + 1) * 128],
                  identity=identf[0:24, 0:24])
                A("pe", "transpose", ["tmpn0", "tmpn1", "const"], ["ps6"], out=ps[6][:, 128 + c * 8:128 + c * 8 + 8], in_=sm_tm[0:8, 1, c * 128:(c + 1) * 128],
                  identity=identf[0:8, 0:8])
            for c in range(4):
                A("dve", "tensor_copy", ["ps6"], ["xr%d" % c], out=xrS[:, c, :, 0:3],
                  in_=ps[6][:, c * 32:c * 32 + 24].rearrange("p (s j) -> p s j", s=8))
            A("dve", "tensor_copy", ["ps6"], ["h0T"], out=h0T[:], in_=ps[6][:, 128:160].rearrange("p (c s) -> p c s", c=4))
        if half and (not sample) and t == 0:
            A("pool", "tensor_copy", ["const"], ["VAn"], out=VAn[:, :, :, 128:130], in_=flagt[:, 0:1].unsqueeze(1).unsqueeze(1).to_broadcast([128, 4, 4, 2]))
        if half and (not sample) and t == NL:
            A("pool", "memset", [], ["VAn"], ap=VAn[:, :, :, 128:130], constant=1.0)
        mark("%s%d norm1" % (kind, t))
        barrier("dve")
        norm_T(0)
        mark("%s%d ffn1" % (kind, t))
        ffn()
        if half and (not sample) and t == NL:
            A("dve", "tensor_scalar", ["hcar", "const"], ["hcar"], out=hcar[:], in0=hcar[:], scalar1=flagt[:, 0:1], scalar2=None, op0=ALU.mult)
            for c in range(4):
                A("dve", "tensor_scalar", ["xr%d" % c, "const"], ["xr%d" % c], out=xrP[:, c, 0:3], in0=xrP[:, c, 0:3], scalar1=flagt[:, 0:1],
                  scalar2=None, op0=ALU.mult)
            att_state = attn_prefetch(t, extra_writes=LA)
        mark("%s%d norm2" % (kind, t))
        norm_T(1)
        mark("%s%d w_in" % (kind, t))
        w_in(sample, light)
        if sample:
            S.dma("pool", [(tm_view(nks_d, 0), k_tm[:])], ["k_tm%d" % i for i in range(4)], ["nk_out"], "st_k")
            S.dma("pool", [(tm_view(nvs_d, 0), v_tm[:])], ["v_tm%d" % i for i in range(4)], ["nv_out"], "st_v")
        else:
            if not light:
                S.dma("pool", [(tm_view(nk_d, o0), k_tm[:])], ["k_tm%d" % i for i in range(4)], ["nk_out"], "st_k")
                S.dma("pool", [(tm_view(nv_d, o0), v_tm[:])], ["v_tm%d" % i for i in range(4)], ["nv_out"], "st_v")
            if t + 1 < NT:
                S.dma("pool", [(KT_s.rearrange("h p s -> p h s")[:, :, t0:t0 + 512], KTn[:])], ["KTn"], ["kts%d" % t], "st_kt")
                S.dma("pool", [(VA_s[h, :, 4 * t:4 * t + 4, :], VAn[:, :, h, :]) for h in range(4)], ["VAn"], ["vas%d" % t], "st_va")
        if light:
            mark("%s%d lru" % (kind, t))
            pt_f = [PT[:, k, :, :].rearrange("p a b -> p (a b)").bitcast(F32) for k in range(3)]
            q_f = q_tm[:].rearrange("p a b -> p (a b)").bitcast(F32)
            QT_f = QT[:].rearrange("p a b -> p (a b)").bitcast(F32)
            lb = pt_f + [q_f[:, 0:512], q_f[:, 512:1024], QT_f[:, 0:512]]
            LB = [["pt00", "pt01"], ["pt10", "pt11"], ["pt20", "pt21"], ["q_tm0", "q_tm1"], ["q_tm2", "q_tm3"], ["QT"]]
            pending["ops"] = lru_build(False, True, (la, lb), (LA, LB), [(4, 5), (6, 7)], 7)
            mark("%s%d end" % (kind, t))
            continue
        for e_ in ("act", "dve", "pool"):
            barrier(e_)
        mark("%s%d attn" % (kind, t))
        if sample:
            A("pool", "memset", [], ["VAq", "cvA", "cvB"], ap=VAq[:, :, :, 128:130], constant=1.0)
            lops = lru_build(True, False, lt, LTN, [(7, 7)], 7)
            attn_sample(lops)
            mark("%s%d lru" % (kind, t))
            pump(lops, len(lops))
            mark("%s%d otr" % (kind, t))
            o_transposes()
        else:
            lops = lru_build(False, False, lt, LTN, [(7, 7)], 7)
            attn_prompt(att_state, lops)
            mark("%s%d lru" % (kind, t))
            pump(lops, len(lops))
            mark("%s%d otr" % (kind, t))
            o_transposes()
        mark("%s%d w_out" % (kind, t))
        w_out()
        barrier("dve")
        mark("%s%d norm3" % (kind, t))
        norm_T(2)
        mark("%s%d ffn2" % (kind, t))
        ffn()
        mark("%s%d end" % (kind, t))
        ydst = tm_view(ys_d if sample else y_d, 0 if sample else o0)
        for sub in range(4):
            S.dma("pool", [(ydst[:, sub, :], xt[:, sub, :])], [XT[sub]], ["y_out%d" % sub], "st_y%d" % sub)
        if sample:
            for c in range(4):
                A("pe", "transpose", ["tails", "const"], ["ps6"], out=ps[6][0:24, c * 128:(c + 1) * 128], in_=tails[:, c, :], identity=identf[:])
                A("pe", "transpose", ["hlast", "const"], ["ps7"], out=ps[7][0:8, c * 128:(c + 1) * 128], in_=hlast[:, c, :], identity=identf[:])
            A("dve", "tensor_copy", ["ps6"], ["tmpn0", "tmpn1"], out=sm_tm[0:24, 0, :], in_=ps[6][0:24, :])
            A("dve", "tensor_copy", ["ps7"], ["tmpn0", "tmpn1"], out=sm_tm[0:8, 1, :], in_=ps[7][0:8, :])
            S.dma("pool", [(nconvs_d, sm_tm[0:24, 0, :]), (nlrus_d, sm_tm[0:8, 1, :])], ["tmpn0", "tmpn1"], ["st_out"], "st_sm")
        elif t == NT - 1:
            for c in range(4):
                A("pe", "transpose", ["xr%d" % c, "const"], ["ps6"], out=ps[6][0:3, c * 128:(c + 1) * 128], in_=xrP[:, c, 0:3], identity=identf[:])
                A("pe", "transpose", ["hcar", "const"], ["ps7"], out=ps[7][0:1, c * 128:(c + 1) * 128], in_=hcar[:, c:c + 1], identity=identf[:])
            A("dve", "tensor_copy", ["ps6"], ["tmpn0", "tmpn1"], out=sm_tm[0:3, 0, :], in_=ps[6][0:3, :])
            A("dve", "tensor_copy", ["ps7"], ["tmpn0", "tmpn1"], out=sm_tm[0:1, 1, :], in_=ps[7][0:1, :])
            S.dma("pool", [(nconv_d, sm_tm[0:3, 0, :]), (nlru_d, sm_tm[0:1, 1, :])], ["tmpn0", "tmpn1"], ["st_out"], "st_sm")
    S.wait_all_dma("sp")
    S.emit()
    es.close()
    return nc


_NC_CACHE = {}


def _get_nc(SEQ, PAST):
    key = (SEQ, PAST)
    if key not in _NC_CACHE:
        _NC_CACHE[key] = build(SEQ, PAST)
    return _NC_CACHE[key]


def make_in_maps(inputs, n_prompt):
    consts = host_consts()
    wmap = {}
    for n, s in W_SPECS:
        a = np.asarray(inputs[n], dtype=np.float32)
        if n != "rel_bias":
            a = a[0]
        wmap[n] = np.ascontiguousarray(a.reshape(s))
    SEQ = inputs["x_prompt"].shape[1]
    H = SEQ // 2
    maps = []
    for c in range(2 * n_prompt):
        b = c // 2
        m = dict(wmap)
        m.update(consts)
        xb = np.asarray(inputs["x_prompt"][b], dtype=np.float32)
        if c % 2 == 0:
            m["x"] = np.ascontiguousarray(np.concatenate([np.zeros((H, 1024), np.float32), xb[:H]], 0))
            m["flag"] = np.zeros((128, 1), np.float32)
        else:
            m["x"] = np.ascontiguousarray(xb)
            m["flag"] = np.ones((128, 1), np.float32)
        sl = slice(4 * c, 4 * c + 4)
        z = lambda *shape: np.zeros(shape, np.float32)
        m["xs"] = np.ascontiguousarray(np.concatenate([inputs["x_sample"][sl].reshape(256, 1024), z(256, 1024)], 0))
        past = inputs["cache_k"].shape[2]
        m["ck"] = np.ascontiguousarray(inputs["cache_k"][0, sl].reshape(4, past, 512))
        m["cv"] = np.ascontiguousarray(inputs["cache_v"][0, sl].reshape(4, past, 512))
        m["slru"] = np.ascontiguousarray(np.concatenate([inputs["state_lru"][0, sl], z(4, 512)], 0))
        m["sconv"] = np.ascontiguousarray(np.concatenate([inputs["state_conv"][0, sl].reshape(12, 512), z(12, 512)], 0))
        maps.append(m)
    return maps


def assemble(results, n_prompt, SEQ):
    B = n_prompt
    r = results
    H = SEQ // 2
    cat2 = lambda name, b: np.concatenate([r[2 * b][name], r[2 * b + 1][name]], 0)
    yp = np.stack([cat2("y", b) for b in range(B)], 0)
    NC_ = 2 * B
    ys = np.concatenate([r[c]["ys"].reshape(8, 64, 1024)[:4] for c in range(NC_)], 0)
    nk = np.stack([cat2("nk", b).reshape(SEQ, 4, 2, 64) for b in range(B)], 0)[None]
    nv = np.stack([cat2("nv", b).reshape(SEQ, 4, 128) for b in range(B)], 0)[None]
    nl = np.stack([r[2 * b + 1]["nlru"].reshape(512) for b in range(B)], 0)[None]
    ncv = np.stack([r[2 * b + 1]["nconv"].reshape(3, 512) for b in range(B)], 0)[None]
    nks = np.concatenate([r[c]["nks"].reshape(8, 64, 4, 2, 64)[:4] for c in range(NC_)], 0)[None]
    nvs = np.concatenate([r[c]["nvs"].reshape(8, 64, 4, 128)[:4] for c in range(NC_)], 0)[None]
    nls = np.concatenate([r[c]["nlrus"].reshape(8, 512)[:4] for c in range(NC_)], 0)[None]
    ncs = np.concatenate([r[c]["nconvs"].reshape(8, 3, 512)[:4] for c in range(NC_)], 0)[None]
    return tuple(np.ascontiguousarray(a, dtype=np.float32) for a in (yp, ys, nk, nv, nl, ncv, nks, nvs, nls, ncs))


def kernel(**inputs):
    inputs = {k: np.asarray(v) for k, v in inputs.items()}
    B, SEQ = inputs["x_prompt"].shape[0], inputs["x_prompt"].shape[1]
    PAST = inputs["cache_k"].shape[2]
    assert B == 4 and inputs["x_sample"].shape[0] == 32
    nc = _get_nc(SEQ, PAST)
    maps = make_in_maps(inputs, 4)
    res = run_bass_kernel_spmd(nc, maps, core_ids=list(range(8)))
    return assemble(res.results, 4, SEQ)
```

```python
import math
from contextlib import ExitStack
import numpy as np
import concourse.bass as bass
import concourse.mybir as mybir
from concourse.bass_utils import run_bass_kernel_spmd

F32 = mybir.dt.float32
BF16 = mybir.dt.bfloat16
AF = mybir.ActivationFunctionType
ALU = mybir.AluOpType
AX = mybir.AxisListType

ENGS = ("pe", "act", "dve", "pool", "sp")
INORDER_SAFE = ("pe", "sp")
EPS = 1e-6
NS = 3


class Op:
    __slots__ = ("eng", "fn", "waits", "inc_needed", "idx", "semval", "dma_inc")

    def __init__(self, eng, fn, waits):
        self.eng = eng
        self.fn = fn
        self.waits = waits
        self.inc_needed = False
        self.idx = -1
        self.semval = 0
        self.dma_inc = None


class Sched:
    def __init__(self, nc, same_engine_sync=True):
        self.nc = nc
        self.streams = {e: [] for e in ENGS}
        self.res = {}
        self.seen = {e: {} for e in ENGS}
        self.dma_tot = {}
        self.same_engine_sync = same_engine_sync

    def _need(self, eng, ev, waits):
        if ev[0] == "op":
            op = ev[1]
            if op.eng == eng and (eng in INORDER_SAFE or not self.same_engine_sync):
                return
            key = ("op", op.eng)
            if self.seen[eng].get(key, -1) >= op.idx:
                return
            self.seen[eng][key] = op.idx
            op.inc_needed = True
            waits.append(ev)
        else:
            key = ("dma", ev[1])
            if self.seen[eng].get(key, -1) >= ev[2]:
                return
            self.seen[eng][key] = ev[2]
            waits.append(ev)

    def _collect(self, eng, reads, writes):
        waits = []
        for r in reads:
            st = self.res.setdefault(r, [[], []])
            for ev in st[0]:
                self._need(eng, ev, waits)
        for w in writes:
            st = self.res.setdefault(w, [[], []])
            for ev in st[0]:
                self._need(eng, ev, waits)
            for ev in st[1]:
                self._need(eng, ev, waits)
        return waits

    def _commit(self, ev, reads, writes):
        for r in reads:
            self.res[r][1].append(ev)
        for w in writes:
            self.res[w] = [[ev], []]

    def op(self, eng, fn, reads=(), writes=()):
        waits = self._collect(eng, reads, writes)
        o = Op(eng, fn, waits)
        o.idx = len(self.streams[eng])
        self.streams[eng].append(o)
        self._commit(("op", o), reads, writes)
        return o

    def dma(self, q, pairs, reads, writes, semkey):
        waits = self._collect(q, reads, writes)
        tot = self.dma_tot.get(semkey, 0)
        first = True
        for pr in pairs:
            out_ap, in_ap = pr[0], pr[1]
            kw = pr[2] if len(pr) > 2 else {}
            tot += 16
            o = Op(q, (lambda e, a=out_ap, b=in_ap, k=kw: e.dma_start(out=a, in_=b, **k)), waits if first else [])
            first = False
            o.idx = len(self.streams[q])
            o.dma_inc = semkey
            self.streams[q].append(o)
        self.dma_tot[semkey] = tot
        self._commit(("dma", semkey, tot), reads, writes)

    def wait_all_dma(self, eng="sp"):
        waits = [("dma", k, v) for k, v in self.dma_tot.items()]
        o = Op(eng, None, waits)
        o.idx = len(self.streams[eng])
        self.streams[eng].append(o)

    def emit(self):
        nc = self.nc
        with ExitStack() as es:
            esem = {e: es.enter_context(nc.semaphore("s_" + e)) for e in ENGS}
            dsem = {k: es.enter_context(nc.semaphore("d_" + str(k))) for k in self.dma_tot}
            for e, ops in self.streams.items():
                c = 0
                for o in ops:
                    if o.inc_needed:
                        c += 1
                        o.semval = c
            block = es.enter_context(nc.Block())

            def run(eng_name):
                def body(e):
                    for o in self.streams[eng_name]:
                        for ev in o.waits:
                            if ev[0] == "op":
                                e.wait_ge(esem[ev[1].eng], ev[1].semval)
                            else:
                                e.wait_ge(dsem[ev[1]], ev[2])
                        if o.fn is None:
                            continue
                        ins = o.fn(e)
                        if o.dma_inc is not None:
                            ins.then_inc(dsem[o.dma_inc], 16)
                        elif o.inc_needed:
                            ins.then_inc(esem[eng_name], 1)
                return body

            block.tensor(run("pe"))
            block.scalar(run("act"))
            block.vector(run("dve"))
            block.gpsimd(run("pool"))
            block.sync(run("sp"))
        return nc


W_SPECS = [
    ("rel_bias", (32, 4)), ("norm_ffn1", (1, 1024)), ("ffn1_gate", (1024, 2816)), ("ffn1_up", (1024, 2816)),
    ("ffn1_down", (2816, 1024)), ("norm_mix", (1, 1024)), ("w_in", (1024, 2560)), ("q_norm", (1, 64)),
    ("k_norm", (1, 64)), ("lambda_q1", (1, 64)), ("lambda_k1", (1, 64)), ("lambda_q2", (1, 64)),
    ("lambda_k2", (1, 64)), ("subln", (1, 128)), ("conv_w", (4, 512)), ("conv_b", (1, 512)),
    ("gate_a_w", (8, 64, 64)), ("gate_a_b", (1, 512)), ("gate_x_w", (8, 64, 64)), ("gate_x_b", (1, 512)),
    ("lru_L", (1, 512)), ("lru_out_norm", (1, 512)), ("w_out", (1024, 1024)), ("norm_ffn2", (1, 1024)),
    ("ffn2_gate", (1024, 2816)), ("ffn2_up", (1024, 2816)), ("ffn2_down", (2816, 1024)),
]


def t5_bucket_np(rel):
    n = 16
    max_exact = 8
    ret = np.where(rel > 0, n, 0)
    rel = np.abs(rel)
    relf = np.maximum(rel, 1).astype(np.float32)
    large = max_exact + (np.log(relf / np.float32(max_exact)) / np.float32(math.log(128 / max_exact))
                         * np.float32(n - max_exact)).astype(np.int32)
    large = np.minimum(large, n - 1)
    return ret + np.where(rel < max_exact, rel, large)


def host_consts():
    ident = np.eye(128, dtype=np.float32)
    jrev = np.ascontiguousarray(ident[::-1])
    m = np.arange(384)
    bk = t5_bucket_np(m - 255)
    oh = np.zeros((32, 384), np.float32)
    oh[bk, m] = 1.0
    oh[15, :] -= 1.0
    i = np.arange(128)[:, None]
    r = np.arange(256)[None, :]
    vis = np.where((r < 128) & ((i // 64) > (r // 64)), 0.0, 1.0).astype(np.float32)
    r2 = np.arange(128)[None, :]
    vis2 = ((i // 64) == (r2 // 64)).astype(np.float32)
    return dict(c_ident=ident, c_j=jrev, c_oh=oh, c_vis=vis, c_vis2=vis2)


PHASES = []
MARKERS = False


def build(SEQ, PAST, do_sample=True, half=True):
    NT = SEQ // 512
    NL = NT // 2 if half else 0
    OSEQ = SEQ - NL * 512
    NKB = SEQ // 128
    PKB = PAST // 128
    assert SEQ % 512 == 0 and PAST % 512 == 0
    nc = bass.Bass("TRN2", target_bir_lowering=False)
    S = Sched(nc)
    es = ExitStack()
    del PHASES[:]

    def mark(name):
        PHASES.append((name, len(S.streams["pe"])))
        if MARKERS:
            S.op("dve", lambda e: e.memset(ap=dummy[:, 3:4], constant=0.0), [], [])

    def din(n, s, d=F32):
        return nc.dram_tensor(n, list(s), d, kind="ExternalInput").ap()

    def dout(n, s, d=F32):
        return nc.dram_tensor(n, list(s), d, kind="ExternalOutput").ap()

    def dscr(n, s, d):
        return nc.dram_tensor(n, list(s), d, kind="Internal").ap()

    def sb(n, s, d):
        return es.enter_context(nc.sbuf_tensor(n, list(s), d))

    def A(eng, method, reads, writes, **kw):
        return S.op(eng, lambda e: getattr(e, method)(**kw), reads, writes)

    x_d = din("x", (SEQ, 1024))
    xs_d = din("xs", (512, 1024))
    NSR = 4
    ck_d = din("ck", (NSR, PAST, 512))
    cv_d = din("cv", (NSR, PAST, 512))
    slru_d = din("slru", (8, 512))
    sconv_d = din("sconv", (24, 512))
    W = {n: din(n, s) for n, s in W_SPECS}
    c_ident = din("c_ident", (128, 128))
    c_j = din("c_j", (128, 128))
    c_oh = din("c_oh", (32, 384))
    c_vis = din("c_vis", (128, 256))
    c_vis2 = din("c_vis2", (128, 128))

    flag_d = din("flag", (128, 1))
    y_d = dout("y", (OSEQ, 1024))
    ys_d = dout("ys", (512, 1024))
    nk_d = dout("nk", (OSEQ, 512))
    nv_d = dout("nv", (OSEQ, 512))
    nlru_d = dout("nlru", (1, 512))
    nconv_d = dout("nconv", (3, 512))
    nks_d = dout("nks", (512, 512))
    nvs_d = dout("nvs", (512, 512))
    nlrus_d = dout("nlrus", (8, 512))
    nconvs_d = dout("nconvs", (24, 512))

    wsc = dscr("wsc", (41, 128, 4096), BF16)
    KT_s = dscr("KT_s", (4, 128, SEQ), BF16)
    VA_s = dscr("VA_s", (4, 128, NKB, 130), BF16)
    eu_s = dscr("eu_s", (4, 384), F32)

    xt = sb("xt", (128, 4, 1024), F32)
    arena = sb("arena", (128, 11264), BF16)
    hidT = arena[:].rearrange("p (k n) -> p k n", k=22)
    arena_f = arena[:].bitcast(F32)
    lt = [arena_f[:, k * 512:(k + 1) * 512] for k in range(6)]
    rstdL = arena_f[:, 3072:3584]
    hg = arena_f[:, 3584:5632].rearrange("p (c n) -> p c n", c=4)
    wring = sb("wring", (128, NS, 4096), BF16)
    xnT_t = sb("xnT", (128, 8, 512), BF16)
    xnT = xnT_t[:]
    catT = xnT_t[:]
    xsb = sb("xsb", (128, 2, 1024), BF16)
    sg = sb("sg", (128, 2, 512), F32)
    q_tm = sb("q_tm", (128, 4, 512), BF16)
    k_tm = sb("k_tm", (128, 4, 512), F32)
    k_bf = sb("k_bf", (128, 4, 512), BF16)
    v_tm = sb("v_tm", (128, 4, 512), F32)
    tmpn = sb("tmpn", (128, 2, 512), F32)
    QT = sb("QT", (128, 4, 512), BF16)
    KTn = sb("KTn", (128, 4, 512), BF16)
    VAn = sb("VAn", (128, 4, 4, 130), BF16)
    att = sb("att", (128, 8256), BF16)
    KTc = att[:, 0:4096].rearrange("p (s n) -> p s n", s=2)
    VAc = att[:, 4096:8256].rearrange("p (s k e) -> p s k e", s=2, k=16)
    kst = att[:, 0:4096].bitcast(F32).rearrange("p (k d) -> p k d", k=4)
    vst = att[:, 4096:8192].bitcast(F32).rearrange("p (k d) -> p k d", k=4)
    att_f = att[:, 0:8192].bitcast(F32)
    la = [att_f[:, k * 512:(k + 1) * 512] for k in range(6)]
    LA = ["la%d" % k for k in range(6)]
    LTN = ["lt%d" % k for k in range(6)]
    sx = sb("sx", (128, 6176), BF16)
    kbf = sx[:, 0:2048].rearrange("p (k d) -> p k d", k=4)
    KTq = sx[:, 2048:4096].rearrange("p (h n) -> p h n", h=4)
    VAq = sx[:, 4096:6176].rearrange("p (k h e) -> p k h e", k=4, h=4)
    PT = sb("PT", (128, 3, 2, 512), BF16)
    o_tm = sb("o_tm", (128, 4, 512), BF16)
    osb = sb("osb", (128, 5, 128), F32)
    Osb = sb("Osb", (128, 3, 390), F32)
    xrT = sb("xrT", (128, 2144), F32)
    xrP = xrT[:, 0:2060].rearrange("p (c n) -> p c n", c=4)
    xrS = xrT[:, 0:2144].rearrange("p (c s n) -> p c s n", c=4, s=8)
    xgT = sb("xgT", (128, 4, 512), F32)
    identf = sb("identf", (128, 128), F32)
    identb = sb("identb", (128, 128), BF16)
    jrev = sb("jrev", (128, 128), F32)
    onesf = sb("onesf", (128, 128), F32)
    vis = sb("vis", (128, 256), F32)
    vis2 = sb("vis2", (128, 128), F32)
    ET = sb("ET", (128, 4, 256), F32)
    ETp = sb("ETp", (128, 4, 128), F32)
    Hk = sb("Hk", (128, 4, 2, 128), F32)
    gT = sb("gT", (128, 3, 8), F32)
    gq_rep = sb("gq_rep", (128, 8, 64), F32)
    gk_rep = sb("gk_rep", (128, 8, 64), F32)
    gsub_rep = sb("gsub_rep", (128, 128), F32)
    cw = sb("cw", (128, 4, 4), F32)
    vecs = sb("vecs", (128, 6, 4), F32)
    Wbd = sb("Wbd", (128, 2, 4, 128), F32)
    relb = sb("relb", (32, 4), F32)
    oh = sb("oh", (32, 384), F32)
    eu = sb("eu", (4, 384), F32)
    lamv = sb("lamv", (1, 4, 64), F32)
    lams = sb("lams", (1, 8), F32)
    neglam = sb("neglam", (128, 1), F32)
    st = sb("st", (128, 64), F32)
    hcar = sb("hcar", (128, 4), F32)
    h0T = sb("h0T", (128, 4, 8), F32)
    hlast = sb("hlast", (128, 4, 8), F32)
    tails = sb("tails", (128, 4, 24), F32)
    sm_tm = tmpn
    dummy = sb("dummyt", (128, 4), F32)
    flagt = sb("flagt", (128, 1), F32)
    cm05 = sb("cm05", (128, 2), F32)

    psall = es.enter_context(nc.psum_tensor("psall", [128, 8, 512], F32))

    class _PV:
        def __init__(self, i):
            self.i = i

        def __getitem__(self, key):
            return psall[:, self.i, :][key]

    ps = [_PV(i) for i in range(8)]
    psb = [psall[:, i, :].bitcast(BF16) for i in range(8)]
    print("sbuf bytes remaining", nc.sbuf_bytes_remaining)

    XT = ["xt0", "xt1", "xt2", "xt3"]
    HID = ["hid%d" % i for i in range(22)]
    LT = ["lt%d" % i for i in range(6)] + ["rstdL", "hg0", "hg1", "hg2", "hg3"]
    XN = ["xn0", "xn1", "xn2", "xn3"]
    CAT = ["cat%d" % i for i in range(8)]

    ALLA = HID + XN + LT + CAT

    def barrier(eng, writes=None):
        writes = ALLA
        k = ENGS.index(eng) % 4
        if eng == "act":
            A(eng, "copy", [], list(writes) + ["dummy_" + eng], out=dummy[:, k:k + 1], in_=onesf[:, 0:1])
        else:
            A(eng, "memset", [], list(writes) + ["dummy_" + eng], ap=dummy[:, k:k + 1], constant=0.0)

    slow = dict(allow_slow_non_contiguous=True)
    S.dma("sp", [(identf[:], c_ident), (jrev[:], c_j), (oh[:], c_oh), (vis[:], c_vis), (vis2[:], c_vis2),
                 (relb[:], W["rel_bias"]),
                 (gT[:, 0, :], W["norm_ffn1"].rearrange("o (k p) -> p (o k)", p=128), slow),
                 (gT[:, 1, :], W["norm_mix"].rearrange("o (k p) -> p (o k)", p=128), slow),
                 (gT[:, 2, :], W["norm_ffn2"].rearrange("o (k p) -> p (o k)", p=128), slow),
                 (gq_rep[:], bass.AP(W["q_norm"].tensor, 0, [[0, 128], [0, 8], [1, 64]])),
                 (gk_rep[:], bass.AP(W["k_norm"].tensor, 0, [[0, 128], [0, 8], [1, 64]])),
                 (gsub_rep[:], bass.AP(W["subln"].tensor, 0, [[0, 128], [1, 128]])),
                 (cw[:, 0, :], W["conv_w"][0:1, :].rearrange("o (c p) -> p (o c)", p=128), slow),
                 (cw[:, 1, :], W["conv_w"][1:2, :].rearrange("o (c p) -> p (o c)", p=128), slow),
                 (cw[:, 2, :], W["conv_w"][2:3, :].rearrange("o (c p) -> p (o c)", p=128), slow),
                 (cw[:, 3, :], W["conv_w"][3:4, :].rearrange("o (c p) -> p (o c)", p=128), slow),
                 (vecs[:, 0, :], W["conv_b"].rearrange("o (c p) -> p (o c)", p=128), slow),
                 (vecs[:, 1, :], W["gate_a_b"].rearrange("o (c p) -> p (o c)", p=128), slow),
                 (vecs[:, 2, :], W["gate_x_b"].rearrange("o (c p) -> p (o c)", p=128), slow),
                 (vecs[:, 3, :], W["lru_L"].rearrange("o (c p) -> p (o c)", p=128), slow),
                 (vecs[:, 4, :], W["lru_out_norm"].rearrange("o (c p) -> p (o c)", p=128), slow),
                 (lamv[:, 0, :], W["lambda_q1"]), (lamv[:, 1, :], W["lambda_q2"]),
                 (lamv[:, 2, :], W["lambda_k1"]), (lamv[:, 3, :], W["lambda_k2"]),
                 (flagt[:], flag_d),
                 ], [], ["const"], "const")
    A("pool", "memset", [], ["Wbd"], ap=Wbd[:], constant=0.0)
    A("pool", "memset", [], ["onesf"], ap=onesf[:], constant=1.0)
    A("pool", "memset", [], ["cm05"], ap=cm05[:, 0:1], constant=-0.5)
    A("pool", "memset", [], ["cm05"], ap=cm05[:, 1:2], constant=0.5)
    A("pool", "memset", [], ["VAn"], ap=VAn[:], constant=1.0)
    A("pool", "memset", [], ["VAq"], ap=VAq, constant=1.0)
    A("pool", "memset", [], ["hcar"], ap=hcar[:], constant=0.0)
    A("pool", "memset", [], ["xr0", "xr1", "xr2", "xr3"], ap=xrT[:], constant=0.0)
    wb_pairs = []
    for gi, gname in enumerate(("gate_a_w", "gate_x_w")):
        for n in range(8):
            o0 = 64 * (n % 2)
            wb_pairs.append((Wbd[o0:o0 + 64, gi, n // 2, o0:o0 + 64], W[gname][n, :, :]))
    S.dma("sp", wb_pairs, [], ["Wbd"], "const2")
    A("dve", "tensor_copy", ["const"], ["identb"], out=identb[:], in_=identf[:])
    A("dve", "tensor_scalar", ["const"], ["gsub"], out=gsub_rep[:], in0=gsub_rep[:], scalar1=0.8, scalar2=None, op0=ALU.mult)
    A("dve", "tensor_scalar", ["const"], ["const"], out=vecs[:, 1:3, :], in0=vecs[:, 1:3, :], scalar1=-1.0, scalar2=None, op0=ALU.mult)
    A("act", "activation", ["const"], ["nsp"], out=vecs[:, 5, :], in_=vecs[:, 3, :], func=AF.Exp, scale=-1.0)
    A("act", "activation", ["nsp"], ["nsp"], out=vecs[:, 5, :], in_=vecs[:, 5, :], func=AF.Ln, bias=1.0)
    A("dve", "tensor_scalar", ["nsp"], ["nsp"], out=vecs[:, 5, :], in0=vecs[:, 5, :], scalar1=-8.0, scalar2=None, op0=ALU.mult)
    A("dve", "tensor_tensor", ["const"], ["lamv"], out=lamv[:, 0:2, :], in0=lamv[:, 0:2, :], in1=lamv[:, 2:4, :], op=ALU.mult)
    A("dve", "tensor_reduce", ["lamv"], ["lams"], out=lams[:, 0:2], in_=lamv[:, 0:2, :], axis=AX.X, op=ALU.add)
    A("act", "activation", ["lams"], ["lams"], out=lams[:, 2:4], in_=lams[:, 0:2], func=AF.Exp)
    A("dve", "tensor_tensor", ["lams"], ["lams"], out=lams[:, 4:5], in0=lams[:, 3:4], in1=lams[:, 2:3], op=ALU.subtract)
    A("dve", "tensor_scalar", ["lams"], ["lams"], out=lams[:, 5:6], in0=lams[:, 4:5], scalar1=-0.2, scalar2=None, op0=ALU.add)
    A("pe", "matmul", ["lams", "onesf"], ["ps0"], out=ps[0][:, 0:1], lhsT=onesf[0:1, :], rhs=lams[:, 5:6], start=True, stop=True)
    A("dve", "tensor_copy", ["ps0"], ["neglam"], out=neglam[:], in_=ps[0][:, 0:1])
    A("pe", "matmul", ["const"], ["ps1"], out=ps[1][0:4, 0:384], lhsT=relb[:], rhs=oh[:], start=True, stop=True)
    A("act", "activation", ["ps1"], ["eu"], out=eu[:], in_=ps[1][0:4, 0:384], func=AF.Exp)
    S.dma("sp", [(eu_s, eu[:])], ["eu"], ["eu_s"], "st_eu")
    S.dma("sp", [(Hk[:, h, :, :], bass.AP(eu_s.tensor, h * 384, [[1, 128], [128, 2], [1, 128]])) for h in range(4)],
          ["eu_s"], ["Hk"], "ld_hk")
    for h in range(4):
        b = 2 + (h % 2)
        A("pe", "matmul", ["Hk", "const"], ["ps%d" % b], out=ps[b][:, 128:256], lhsT=Hk[:, h, 0, :], rhs=jrev[:], start=True, stop=True)
        A("pe", "matmul", ["Hk", "const"], ["ps%d" % b], out=ps[b][:, 0:128], lhsT=Hk[:, h, 1, :], rhs=jrev[:], start=True, stop=True)
        A("dve", "tensor_tensor", ["ps%d" % b, "const"], ["ET"], out=ET[:, h, :], in0=ps[b][:, 0:256], in1=vis[:], op=ALU.mult)
        A("dve", "tensor_tensor", ["ps%d" % b, "const"], ["ET"], out=ETp[:, h, :], in0=ps[b][:, 0:128], in1=vis2[:], op=ALU.mult)

    chunks = []
    for f in ("ffn1", "w_in", "w_out", "ffn2"):
        if f in ("ffn1", "ffn2"):
            for j in range(11):
                chunks.append([(W[f + "_gate"], 0, 8, 256 * j, 256), (W[f + "_up"], 0, 8, 256 * j, 256)])
            for half in range(2):
                for g in range(3):
                    chunks.append([(W[f + "_down"], g * 8, min(8, 22 - g * 8), 512 * half, 512)])
        elif f == "w_in":
            for b in range(5):
                chunks.append([(W["w_in"], 0, 8, 512 * b, 512)])
        else:
            for half in range(2):
                chunks.append([(W["w_out"], 0, 8, 512 * half, 512)])
    assert len(chunks) == 41
    chunk_len = [sum(nk * ncols for (_, _, nk, _, ncols) in pieces) for pieces in chunks]
    stage = [xt[:].rearrange("p a b -> p (a b)"), arena_f[:, 0:4096]]
    stage_res = [XT, HID + LT]
    cast_eng = ["dve", "pool", "act"]
    N_EAGER = 17 if (half and NL >= 6) else 41
    sx_f = sx[:, 0:4096].bitcast(F32)
    lz_stage = [sx_f[:, 0:1024], sx_f[:, 1024:2048]]
    lz_stage_res = [["kbf"], ["KTq0", "KTq1", "KTq2", "KTq3"]]
    lz_out = [sx[:, 4096:5120], sx[:, 5120:6144]]
    lz_out_res = [["cvA"], ["cvB"]]
    lazy_pieces = []
    lz_ctr = [0]

    def lazy_piece(ci, off, wap, kt0, nk, col0, ncols):
        k = lz_ctr[0] % 2
        lz_ctr[0] += 1
        n = nk * ncols
        S.dma("sp", [(lz_stage[k][:, 0:n].rearrange("p (k n) -> p k n", k=nk),
                      wap[kt0 * 128:(kt0 + nk) * 128, col0:col0 + ncols].rearrange("(k p) n -> p k n", p=128))], [], lz_stage_res[k], "lzs%d" % k)
        eng = ("dve", "act")[lz_ctr[0] % 2]
        if eng == "act":
            A("act", "activation", lz_stage_res[k], lz_out_res[k], out=lz_out[k][:, 0:n], in_=lz_stage[k][:, 0:n], func=AF.Copy)
        else:
            A("dve", "tensor_copy", lz_stage_res[k], lz_out_res[k], out=lz_out[k][:, 0:n], in_=lz_stage[k][:, 0:n])
        S.dma("pool", [(wsc[ci, :, off:off + n], lz_out[k][:, 0:n])], lz_out_res[k], ["wsc%d" % ci], "lzo%d" % k)

    conv_order = list(range(N_EAGER)) + [c_ for c_ in (18, 19, 20, 21, 17) if c_ >= N_EAGER] + [c_ for c_ in range(22, 41) if c_ >= N_EAGER]
    for ci in conv_order:
        pieces = chunks[ci]
        if ci >= N_EAGER:
            off = 0
            for (wap, kt0, nk, col0, ncols) in pieces:
                step = max(1, 1024 // ncols)
                for q0 in range(0, nk, step):
                    nq = min(step, nk - q0)
                    lazy_pieces.append((ci, off, wap, kt0 + q0, nq, col0, ncols))
                    off += nq * ncols
            continue
        sidx = ci % 2
        stg = stage[sidx]
        pairs = []
        off = 0
        for (wap, kt0, nk, col0, ncols) in pieces:
            pairs.append((stg[:, off:off + nk * ncols].rearrange("p (k n) -> p k n", k=nk),
                          wap[kt0 * 128:(kt0 + nk) * 128, col0:col0 + ncols].rearrange("(k p) n -> p k n", p=128)))
            off += nk * ncols
        S.dma("sp", pairs, [], stage_res[sidx], "pst%d" % sidx)
        cslot = 1 + sidx
        eng = cast_eng[ci % 3]
        if eng == "act":
            A("act", "activation", stage_res[sidx], ["w%d" % cslot], out=wring[:, cslot, 0:off], in_=stg[:, 0:off], func=AF.Copy)
        else:
            A(eng, "tensor_copy", stage_res[sidx], ["w%d" % cslot], out=wring[:, cslot, 0:off], in_=stg[:, 0:off])
        S.dma("pool", [(wsc[ci, :, 0:off], wring[:, cslot, 0:off])], ["w%d" % cslot], ["wsc%d" % ci], "stc%d" % sidx)

    chunk_seq = []
    for t_ in range(NT):
        chunk_seq += (list(range(17)) + [18, 19, 20, 21]) if t_ < NL else list(range(41))
    if do_sample:
        chunk_seq += list(range(41))
    total_chunks = len(chunk_seq)
    ws = dict(cur=0, issued=0)

    def next_chunk():
        i = ws["cur"]
        ws["cur"] += 1
        while ws["issued"] < min(total_chunks, i + NS):
            k = ws["issued"]
            slot = k % NS
            ci = chunk_seq[k]
            n = chunk_len[ci]
            S.dma("sp", [(wring[:, slot, 0:n], wsc[ci, :, 0:n])], ["wsc%d" % ci], ["w%d" % slot], "w%d" % slot)
            ws["issued"] += 1
        return i % NS

    def rsqrt_small(ap, n_inv, res, rows=slice(0, 128)):
        A("dve", "tensor_scalar", [res], [res], out=ap, in0=ap, scalar1=n_inv, scalar2=EPS, op0=ALU.mult, op1=ALU.add)
        A("act", "activation", [res], [res], out=ap, in_=ap, func=AF.Ln)
        A("act", "activation", [res], [res], out=ap, in_=ap, func=AF.Exp, scale=-0.5)

    def norm_T(gi):
        for sub in range(4):
            A("act", "activation", [XT[sub]], ["sg0", "sg1", "ssn"], out=sg[:].rearrange("p a b -> p (a b)"), in_=xt[:, sub, :], func=AF.Square,
              accum_out=st[:, sub:sub + 1])
        rsqrt_small(st[:, 0:4], 1.0 / 1024, "ssn")
        for sub in range(4):
            xb = "xsb%d" % (sub % 2)
            A("act", "activation", [XT[sub], "ssn"], [xb], out=xsb[:, sub % 2, :], in_=xt[:, sub, :], func=AF.Copy,
              scale=st[:, sub:sub + 1])
            bk = 6 + (sub % 2)
            for kt in range(8):
                A("pe", "transpose", [xb, "identb"], ["ps%d" % bk], out=psb[bk][:, kt * 128:(kt + 1) * 128],
                  in_=xsb[:, sub % 2, kt * 128:(kt + 1) * 128], identity=identb[:])
            A("dve", "tensor_tensor", ["ps%d" % bk, "const"], [XN[sub]], out=xnT[:, :, sub * 128:(sub + 1) * 128],
              in0=psb[bk][:, 0:1024].rearrange("p (k n) -> p k n", k=8),
              in1=gT[:, gi, :].unsqueeze(2).to_broadcast([128, 8, 128]), op=ALU.mult)

    pending = dict(ops=None)

    def ffn():
        pend = pending["ops"]
        rate = (len(pend) + 19) // 20 if pend else 0
        for grp in range(11):
            slot = next_chunk()
            wres = "w%d" % slot
            w = wring[:, slot, :].rearrange("p (g k n) -> p g k n", g=2, k=8)
            for j in range(2):
                nt = grp * 2 + j
                bg, bu = (0, 1) if nt % 2 == 0 else (2, 3)
                for kt in range(8):
                    A("pe", "matmul", [wres] + XN, ["ps%d" % bg], out=ps[bg][:], lhsT=w[:, 0, kt, j * 128:(j + 1) * 128],
                      rhs=xnT[:, kt, :], start=(kt == 0), stop=(kt == 7))
                for kt in range(8):
                    A("pe", "matmul", [wres] + XN, ["ps%d" % bu], out=ps[bu][:], lhsT=w[:, 1, kt, j * 128:(j + 1) * 128],
                      rhs=xnT[:, kt, :], start=(kt == 0), stop=(kt == 7))
                sgn = "sg%d" % (nt % 2)
                A("act", "activation", ["ps%d" % bg], [sgn], out=sg[:, nt % 2, :], in_=ps[bg][:], func=AF.Exp, scale=-1.0)
                A("act", "activation", [sgn], [sgn], out=sg[:, nt % 2, :], in_=sg[:, nt % 2, :], func=AF.Ln, bias=1.0)
                A("act", "activation", [sgn], [sgn], out=sg[:, nt % 2, :], in_=sg[:, nt % 2, :], func=AF.Exp, scale=-1.0)
                A("dve", "tensor_tensor", [sgn, "ps%d" % bg], [sgn], out=sg[:, nt % 2, :], in0=sg[:, nt % 2, :], in1=ps[bg][:], op=ALU.mult)
                A("dve", "tensor_tensor", [sgn, "ps%d" % bu], [HID[nt]], out=hidT[:, nt, :], in0=sg[:, nt % 2, :], in1=ps[bu][:], op=ALU.mult)
                if pend:
                    pump(pend, rate)
                if lazy_pieces:
                    lazy_piece(*lazy_pieces.pop(0))
        if pend is not None:
            pump(pend, len(pend))
            pending["ops"] = None
        for half in range(2):
            banks = [4, 5, 6, 7] if half == 0 else [0, 1, 2, 3]
            for g in range(3):
                slot = next_chunk()
                wres = "w%d" % slot
                w = wring[:, slot, :].rearrange("p (k n) -> p k n", k=8)
                for kti, kt in enumerate(range(g * 8, min(22, g * 8 + 8))):
                    for sub in range(4):
                        A("pe", "matmul", [wres, HID[kt]], ["ps%d" % banks[sub]], out=ps[banks[sub]][:],
                          lhsT=hidT[:, kt, sub * 128:(sub + 1) * 128], rhs=w[:, kti, :], start=(kt == 0), stop=(kt == 21))
            for sub in range(4):
                A("dve", "scalar_tensor_tensor", ["ps%d" % banks[sub], XT[sub]], [XT[sub]],
                  out=xt[:, sub, half * 512:(half + 1) * 512], in0=ps[banks[sub]][:], scalar=0.5,
                  in1=xt[:, sub, half * 512:(half + 1) * 512], op0=ALU.mult, op1=ALU.add)

    def w_in(sample, light=False):
        for blk in range(3):
            if light and blk == 0:
                continue
            slot = next_chunk()
            wres = "w%d" % slot
            w = wring[:, slot, :].rearrange("p (k n) -> p k n", k=8)
            banks = [0, 1, 2, 3] if blk % 2 == 0 else [4, 5, 6, 7]
            for sub in range(4):
                for kt in range(8):
                    A("pe", "matmul", [wres, XN[sub]], ["ps%d" % banks[sub]], out=ps[banks[sub]][:],
                      lhsT=xnT[:, kt, sub * 128:(sub + 1) * 128], rhs=w[:, kt, :], start=(kt == 0), stop=(kt == 7))
            for sub in range(4):
                pb = "ps%d" % banks[sub]
                pv = ps[banks[sub]][:]
                if blk < 2:
                    ti = sub % 2
                    tn = "tmpn%d" % ti
                    ssq = st[:, 8 + 8 * ti:16 + 8 * ti]
                    A("act", "activation", [pb], [tn], out=tmpn[:, ti, :], in_=pv, func=AF.Square)
                    A("dve", "tensor_reduce", [tn], ["ssq%d" % ti], out=ssq, in_=tmpn[:, ti, :].rearrange("p (g d) -> p g d", g=8),
                      axis=AX.X, op=ALU.add)
                    rsqrt_small(ssq, 1.0 / 64, "ssq%d" % ti)
                    A("dve", "tensor_tensor", [pb, "ssq%d" % ti, tn], [tn], out=tmpn[:, ti, :].rearrange("p (g d) -> p g d", g=8),
                      in0=pv.rearrange("p (g d) -> p g d", g=8), in1=ssq.unsqueeze(2).to_broadcast([128, 8, 64]), op=ALU.mult)
                    if blk == 0:
                        A("pool", "tensor_tensor", [tn, "const"], ["q_tm%d" % sub], out=q_tm[:, sub, :], in0=tmpn[:, ti, :],
                          in1=gq_rep[:].rearrange("p g d -> p (g d)"), op=ALU.mult)
                    else:
                        A("pool", "tensor_tensor", [tn, "const"], ["k_tm%d" % sub], out=k_tm[:, sub, :], in0=tmpn[:, ti, :],
                          in1=gk_rep[:].rearrange("p g d -> p (g d)"), op=ALU.mult)
                        A("pool", "tensor_copy", ["k_tm%d" % sub], ["k_bf%d" % sub], out=k_bf[:, sub, :], in_=k_tm[:, sub, :])
                else:
                    A("act", "activation", [pb], ["v_tm%d" % sub], out=v_tm[:, sub, :], in_=pv, func=AF.Copy)
                    if light:
                        A("dve", "tensor_scalar", ["v_tm%d" % sub, "const"], ["VAn"], out=VAn[:, sub, :, 0:128],
                          in0=v_tm[:, sub, :].rearrange("p (h e) -> p h e", h=4), scalar1=flagt[:, 0:1], scalar2=None, op0=ALU.mult)
                    else:
                        A("pool", "tensor_copy", ["v_tm%d" % sub], ["VAn"], out=VAn[:, sub, :, 0:128],
                          in_=v_tm[:, sub, :].rearrange("p (h e) -> p h e", h=4))
        for blk in (3, 4):
            slot = next_chunk()
            wres = "w%d" % slot
            w = wring[:, slot, :].rearrange("p (k n) -> p k n", k=8)
            for c in range(4):
                bk = 4 + c if blk == 3 else c
                for kt in range(8):
                    A("pe", "matmul", [wres] + XN, ["ps%d" % bk], out=ps[bk][:], lhsT=w[:, kt, c * 128:(c + 1) * 128],
                      rhs=xnT[:, kt, :], start=(kt == 0), stop=(kt == 7))
                if blk == 3:
                    if sample:
                        A("act", "activation", ["ps%d" % bk], ["xr%d" % c], out=xrS[:, c, :, 3:67],
                          in_=ps[bk][:].rearrange("p (s n) -> p s n", s=8), func=AF.Copy)
                    else:
                        A("act", "activation", ["ps%d" % bk], ["xr%d" % c], out=xrP[:, c, 3:515], in_=ps[bk][:], func=AF.Copy)
                else:
                    A("dve", "tensor_copy", ["ps%d" % bk], ["xg%d" % c], out=xgT[:, c, :], in_=ps[bk][:])
        for (src, sname, dst, dname, bk) in ((q_tm, "q_tm", QT, "QT", 6), (k_bf, "k_bf", KTn, "KTn", 7)):
            if light and sname == "q_tm":
                continue
            for h in range(4):
                for sub in range(4):
                    A("pe", "transpose", ["%s%d" % (sname, sub), "identb"], ["ps%d" % bk], out=psb[bk][:, sub * 128:(sub + 1) * 128],
                      in_=src[:, sub, h * 128:(h + 1) * 128], identity=identb[:])
                A("dve" if h % 2 == 0 else "act", "tensor_copy" if h % 2 == 0 else "copy", ["ps%d" % bk], [dname],
                  out=dst[:, h, :], in_=psb[bk][:, 0:512])

    def Oacc(idx):
        b = 4 + idx // 3
        c0 = (idx % 3) * 130
        return b, c0

    pt_ctr = [0]

    def attn_prefetch(t, extra_writes=()):
        nprefix = 4 * t
        ch_list = [(c0, min(16, nprefix - c0)) for c0 in range(0, nprefix, 16)]
        stt = dict(t=t, nprefix=nprefix, ch_list=ch_list, loads=[(h, ci) for h in range(4) for ci in range(len(ch_list))],
                   issued=0, slot_of={}, extra=list(extra_writes))
        issue_load(stt)
        issue_load(stt)
        return stt

    def issue_load(stt):
        k = stt["issued"]
        if k >= len(stt["loads"]):
            return
        h, ci = stt["loads"][k]
        c0, n = stt["ch_list"][ci]
        slot = kv_ctr[0] % 2
        kv_ctr[0] += 1
        stt["slot_of"][(h, ci)] = slot
        tl = sorted(set((c0 + j) // 4 for j in range(n)))
        S.dma("sp", [(KTc[:, slot, 0:n * 128], KT_s[h, :, c0 * 128:(c0 + n) * 128])], ["kts%d" % tt for tt in tl],
              ["ktc%d" % slot] + stt["extra"], "ktc%d" % slot)
        S.dma("sp", [(VAc[:, slot, 0:n, :], VA_s[h, :, c0:c0 + n, :])], ["vas%d" % tt for tt in tl],
              ["vac%d" % slot] + stt["extra"], "vac%d" % slot)
        stt["issued"] += 1

    def attn_prompt(stt, side_ops=None):
        nprefix = stt["nprefix"]
        ch_list = stt["ch_list"]
        slot_of = stt["slot_of"]
        blocks = []
        for h in range(4):
            for ci, (c0, n) in enumerate(ch_list):
                for j in range(n):
                    blocks.append(dict(h=h, load=(h, ci), j=j, q_lo=0, bias=("pl" if c0 + j == nprefix - 1 else None), local=None))
            for jb in range(4):
                blocks.append(dict(h=h, load=None, j=jb, q_lo=128 * jb, bias="loc", local=jb))

        def qk(bl):
            h = bl["h"]
            pbi = pt_ctr[0] % 2
            ptb = pt_ctr[0] % 3
            pt_ctr[0] += 1
            bl["pb"] = ptb
            q_lo = bl["q_lo"]
            if bl["load"] is not None:
                slot = slot_of[bl["load"]]
                bl["slot"] = slot
                kt_ap = KTc[:, slot, bl["j"] * 128:(bl["j"] + 1) * 128]
                kres = ["ktc%d" % slot]
            else:
                kt_ap = KTn[:, h, bl["j"] * 128:(bl["j"] + 1) * 128]
                kres = ["KTn"]
            for c in range(2):
                bk = pbi * 2 + c
                A("pe", "matmul", kres + ["QT"], ["ps%d" % bk], out=ps[bk][:, q_lo:512], lhsT=kt_ap[64 * c:64 * c + 64, :],
                  rhs=QT[64 * c:64 * c + 64, h, q_lo:512], start=True, stop=True)
            ptn = ["pt%d0" % ptb, "pt%d1" % ptb]
            A("act", "activation", ["ps%d" % (pbi * 2), "ps%d" % (pbi * 2 + 1)], ptn, out=PT[:, ptb, :, q_lo:512],
              in_=psall[:, pbi * 2:pbi * 2 + 2, q_lo:512], func=AF.Exp, scale=0.125)
            if bl["bias"] == "pl":
                A("dve", "tensor_tensor", ptn + ["ET"], ptn, out=PT[:, ptb, :, 0:128], in0=PT[:, ptb, :, 0:128],
                  in1=ET[:, h, 128:256].unsqueeze(1).to_broadcast([128, 2, 128]), op=ALU.mult)
            elif bl["bias"] == "loc":
                hi = min(512, q_lo + 256)
                A("dve", "tensor_tensor", ptn + ["ET"], ptn, out=PT[:, ptb, :, q_lo:hi], in0=PT[:, ptb, :, q_lo:hi],
                  in1=ET[:, h, 0:hi - q_lo].unsqueeze(1).to_broadcast([128, 2, hi - q_lo]), op=ALU.mult)

        started = set()

        def pv(bl, first):
            h = bl["h"]
            if first:
                started.clear()
            pbi = bl["pb"]
            if bl["load"] is not None:
                va_ap = VAc[:, bl["slot"], bl["j"], 0:129]
                vres = ["vac%d" % bl["slot"]]
            else:
                va_ap = VAn[:, bl["j"], h, 0:129]
                vres = ["VAn"]
            for qs in range(bl["q_lo"] // 128, 4):
                last = (bl["local"] == qs)
                for c in range(2):
                    b, c0 = Oacc(c * 4 + qs)
                    st_flag = first and (b not in started)
                    started.add(b)
                    A("pe", "matmul", vres + ["pt%d%d" % (pbi, c)], ["ps%d" % b], out=ps[b][:, c0:c0 + 129],
                      lhsT=PT[:, pbi, c, qs * 128:(qs + 1) * 128], rhs=va_ap, start=st_flag, stop=last, skip_group_check=True)

        def epilogue(h):
            for k in range(3):
                A("dve", "tensor_copy", ["ps%d" % (4 + k)], ["Osb%d" % k], out=Osb[:, k, :], in_=ps[4 + k][:, 0:390])
            for qs in range(4):
                b0, c0 = Oacc(qs)
                b1, c1 = Oacc(4 + qs)
                O0 = Osb[:, b0 - 4, c0:c0 + 129]
                O1 = Osb[:, b1 - 4, c1:c1 + 129]
                r0, r1 = "Osb%d" % (b0 - 4), "Osb%d" % (b1 - 4)
                A("dve", "reciprocal", [r0], ["rec0"], out=st[:, 32:33], in_=O0[:, 128:129])
                A("dve", "reciprocal", [r1], ["rec1"], out=st[:, 33:34], in_=O1[:, 128:129])
                A("dve", "tensor_scalar", [r1, "rec1", "neglam"], ["osb4"], out=osb[:, 4, :], in0=O1[:, 0:128],
                  scalar1=st[:, 33:34], scalar2=neglam[:, 0:1], op0=ALU.mult, op1=ALU.mult)
                A("dve", "scalar_tensor_tensor", [r0, "rec0", "osb4"], ["osb%d" % qs], out=osb[:, qs, :], in0=O0[:, 0:128],
                  scalar=st[:, 32:33], in1=osb[:, 4, :], op0=ALU.mult, op1=ALU.add)
                A("dve", "tensor_tensor", ["osb%d" % qs], ["osb4"], out=osb[:, 4, :], in0=osb[:, qs, :], in1=osb[:, qs, :], op=ALU.mult)
                A("dve", "tensor_reduce", ["osb4"], ["ssh"], out=st[:, 36 + qs:37 + qs], in_=osb[:, 4, :], axis=AX.X, op=ALU.add)
            A("dve", "tensor_scalar", ["ssh"], ["ssh"], out=st[:, 36:40], in0=st[:, 36:40], scalar1=1.0 / 128, scalar2=EPS, op0=ALU.mult, op1=ALU.add)
            A("pool", "tensor_tensor", ["ssh", "cm05"], ["ssh"], out=st[:, 36:40], in0=st[:, 36:40], in1=cm05[:, 0:1].to_broadcast([128, 4]), op=ALU.pow)
            for qs in range(4):
                A("dve", "scalar_tensor_tensor", ["osb%d" % qs, "ssh", "gsub"], ["o_tm%d" % qs], out=o_tm[:, qs, h * 128:(h + 1) * 128],
                  in0=osb[:, qs, :], scalar=st[:, 36 + qs:37 + qs], in1=gsub_rep[:], op0=ALU.mult, op1=ALU.mult)

        nb = len(blocks)
        side_rate = (len(side_ops) + nb - 9) // max(1, nb - 8) if side_ops else 0
        qk(blocks[0])
        if nb > 1:
            qk(blocks[1])
        for i, bl in enumerate(blocks):
            if i + 2 < nb:
                qk(blocks[i + 2])
            first = (i == 0) or (blocks[i - 1]["h"] != bl["h"])
            pv(bl, first)
            if side_ops:
                pump(side_ops, side_rate)
            if bl["load"] is not None and (i + 1 >= nb or blocks[i + 1]["load"] != bl["load"]):
                issue_load(stt)
            if i + 1 >= nb or blocks[i + 1]["h"] != bl["h"]:
                epilogue(bl["h"])

    kv_ctr = [0]

    def attn_sample(side_ops=None):
        nch = PKB // 4
        seq_ch = [(s_, ch_) for s_ in range(NSR) for ch_ in range(nch)]

        def load_chunk(k):
            if k >= len(seq_ch):
                return
            s_, ch_ = seq_ch[k]
            S.dma("sp", [(kst, ck_d[s_, ch_ * 512:(ch_ + 1) * 512, :].rearrange("(k p) d -> p k d", p=128))], [],
                  ["ktc0", "ktc1"], "kst")
            S.dma("sp", [(vst, cv_d[s_, ch_ * 512:(ch_ + 1) * 512, :].rearrange("(k p) d -> p k d", p=128))], [],
                  ["vac0", "vac1"], "vst")

        side_rate = (len(side_ops) + len(seq_ch) * 8 - 9) // (len(seq_ch) * 8 - 8) if side_ops else 0
        load_chunk(0)
        for s in range(NSR):
            pair = s // 2
            started_s = set()
            qc0 = pair * 128
            R0 = (s % 2) * 64
            for ch in range(nch):
                A("pool", "tensor_copy", ["ktc0", "ktc1"], ["kbf"], out=kbf, in_=kst)
                A("dve", "tensor_copy", ["vac0", "vac1"], ["VAq"], out=VAq[:, :, :, 0:128],
                  in_=vst.rearrange("p k (h e) -> p k h e", h=4))
                load_chunk(s * nch + ch + 1)
                pend_pv = [None]

                def flush_pv():
                    if pend_pv[0] is not None:
                        h_, c_, gi_, bk_ = pend_pv[0]
                        pend_pv[0] = None
                        ptn_ = "pt%d%d" % (gi_ // 2, gi_ % 2)
                        b, c0 = Oacc(h_ * 2 + c_)
                        for kb in range(4):
                            st_flag = (ch == 0 and kb == 0 and b not in started_s)
                            started_s.add(b)
                            A("pe", "matmul", ["VAq", ptn_], ["ps%d" % b], out=ps[b][:, c0:c0 + 129],
                              lhsT=PT[:, gi_ // 2, gi_ % 2, kb * 128:(kb + 1) * 128], rhs=VAq[:, kb, h_, 0:129],
                              start=st_flag, stop=False, skip_group_check=True)

                for h in range(4):
                    for kb in range(4):
                        A("pe", "transpose", ["kbf", "identb"], ["ps7"], out=psb[7][:, kb * 128:(kb + 1) * 128],
                          in_=kbf[:, kb, h * 128:(h + 1) * 128], identity=identb[:])
                    A("dve", "tensor_copy", ["ps7"], ["KTq%d" % h], out=KTq[:, h, :], in_=psb[7][:, 0:512])
                    for c in range(2):
                        gi = pt_ctr[0] % 6
                        bk = pt_ctr[0] % 4
                        pt_ctr[0] += 1
                        ptn = "pt%d%d" % (gi // 2, gi % 2)
                        for kb in range(4):
                            A("pe", "matmul", ["KTq%d" % h, "QT"], ["ps%d" % bk], out=ps[bk][:, kb * 128:(kb + 1) * 128],
                              lhsT=KTq[64 * c:64 * c + 64, h, kb * 128:(kb + 1) * 128], rhs=QT[64 * c:64 * c + 64, h, qc0:qc0 + 128],
                              start=True, stop=True)
                        A("act", "activation", ["ps%d" % bk], [ptn], out=PT[:, gi // 2, gi % 2, :], in_=ps[bk][:], func=AF.Exp, scale=0.125)
                        if ch == nch - 1:
                            a0 = 3 * 128 + R0
                            A("dve", "tensor_tensor", [ptn, "ET"], [ptn], out=PT[:, gi // 2, gi % 2, a0:a0 + 64],
                              in0=PT[:, gi // 2, gi % 2, a0:a0 + 64], in1=ET[:, h, 128:192], op=ALU.mult)
                        flush_pv()
                        pend_pv[0] = (h, c, gi, bk)
                        if side_ops:
                            pump(side_ops, side_rate)
                flush_pv()
            for h in range(4):
                for c in range(2):
                    gi = pt_ctr[0] % 6
                    bk = pt_ctr[0] % 4
                    pt_ctr[0] += 1
                    ptn = "pt%d%d" % (gi // 2, gi % 2)
                    A("pe", "matmul", ["KTn", "QT"], ["ps%d" % bk], out=ps[bk][:, 0:128], lhsT=KTn[64 * c:64 * c + 64, h, qc0:qc0 + 128],
                      rhs=QT[64 * c:64 * c + 64, h, qc0:qc0 + 128], start=True, stop=True)
                    A("act", "activation", ["ps%d" % bk], [ptn], out=PT[:, gi // 2, gi % 2, 0:128], in_=ps[bk][:, 0:128], func=AF.Exp, scale=0.125)
                    A("dve", "tensor_tensor", [ptn, "ET"], [ptn], out=PT[:, gi // 2, gi % 2, 0:128], in0=PT[:, gi // 2, gi % 2, 0:128],
                      in1=ETp[:, h, :], op=ALU.mult)
                    b, c0 = Oacc(h * 2 + c)
                    A("pe", "matmul", ["VAn", ptn], ["ps%d" % b], out=ps[b][:, c0:c0 + 129], lhsT=PT[:, gi // 2, gi % 2, 0:128],
                      rhs=VAn[:, pair, h, 0:129], start=False, stop=True, skip_group_check=True)
            R = slice(R0, R0 + 64)
            for h in range(4):
                b0, c0 = Oacc(h * 2)
                b1, c1 = Oacc(h * 2 + 1)
                O0 = ps[b0][R, c0:c0 + 129]
                O1 = ps[b1][R, c1:c1 + 129]
                A("dve", "reciprocal", ["ps%d" % b0], ["rec"], out=st[R, 32:33], in_=O0[:, 128:129])
                A("dve", "reciprocal", ["ps%d" % b1], ["rec"], out=st[R, 33:34], in_=O1[:, 128:129])
                A("dve", "tensor_scalar", ["ps%d" % b1, "rec", "neglam"], ["osb4"], out=osb[R, 4, :], in0=O1[:, 0:128],
                  scalar1=st[R, 33:34], scalar2=neglam[R, 0:1], op0=ALU.mult, op1=ALU.mult)
                A("dve", "scalar_tensor_tensor", ["ps%d" % b0, "rec", "osb4"], ["osb%d" % h], out=osb[R, h, :], in0=O0[:, 0:128],
                  scalar=st[R, 32:33], in1=osb[R, 4, :], op0=ALU.mult, op1=ALU.add)
                A("act", "activation", ["osb%d" % h], ["sg0", "ssh"], out=sg[R, 0, 0:128], in_=osb[R, h, :], func=AF.Square,
                  accum_out=st[R, 36 + h:37 + h])
            rsqrt_small(st[R, 36:40], 1.0 / 128, "ssh", rows=R)
            for h in range(4):
                A("dve", "scalar_tensor_tensor", ["osb%d" % h, "ssh", "gsub"], ["o_tm%d" % pair], out=o_tm[R, pair, h * 128:(h + 1) * 128],
                  in0=osb[R, h, :], scalar=st[R, 36 + h:37 + h], in1=gsub_rep[R, :], op0=ALU.mult, op1=ALU.mult)

    def o_transposes():
        for h in range(4):
            bk = 6 + (h % 2)
            for sub in range(4):
                A("pe", "transpose", ["o_tm%d" % sub, "identb"], ["ps%d" % bk], out=psb[bk][:, sub * 128:(sub + 1) * 128],
                  in_=o_tm[:, sub, h * 128:(h + 1) * 128], identity=identb[:])
            A("dve" if h % 2 == 0 else "act", "tensor_copy" if h % 2 == 0 else "copy", ["ps%d" % bk], [CAT[h]],
              out=catT[:, h, :], in_=psb[bk][:, 0:512])

    def lru_build(sample, light, lt, ltn, gb, nb):
        ops = []
        ops_c = [[] for _ in range(4)]
        cur = [ops]
        two_sets = isinstance(lt, tuple)
        lt_sets, ltn_sets = (lt, ltn) if two_sets else ((lt, lt), (ltn, ltn))

        def flat(lst):
            out = []
            for x in lst:
                if isinstance(x, (list, tuple)):
                    out += list(x)
                else:
                    out.append(x)
            return out

        def A(eng, method, reads, writes, **kw):
            cur[0].append((eng, method, flat(reads), flat(writes), kw))
        nsq, tl = (8, 64) if sample else (1, 512)

        def v3(ap):
            return ap.rearrange("p (s n) -> p s n", s=nsq)

        for c in range(4):
            xr = "xr%d" % c
            cur[0] = ops_c[c]
            lt, ltn = lt_sets[c % 2], ltn_sets[c % 2]

            def xp(j):
                if sample:
                    return xrS[:, c, :, j:j + 64]
                return xrP[:, c, j:j + 512].rearrange("p (s n) -> p s n", s=1)
            A("dve", "tensor_scalar", [xr, "const"], [ltn[0]], out=v3(lt[0]), in0=xp(3), scalar1=cw[:, 3, c:c + 1], scalar2=vecs[:, 0, c:c + 1],
              op0=ALU.mult, op1=ALU.add)
            for j in range(3):
                A("dve", "scalar_tensor_tensor", [xr, "const", ltn[0]], [ltn[0]], out=v3(lt[0]), in0=xp(j), scalar=cw[:, j, c:c + 1],
                  in1=v3(lt[0]), op0=ALU.mult, op1=ALU.add)
            ba_, bx_ = gb[c % len(gb)]
            A("pe", "matmul", [ltn[0], "Wbd"], ["ps%d" % ba_], out=ps[ba_][:], lhsT=Wbd[:, 0, c, :], rhs=lt[0], start=True, stop=True)
            A("act", "activation", ["ps%d" % ba_, "const"], [ltn[1]], out=lt[1], in_=ps[ba_][:], func=AF.Exp, bias=vecs[:, 1, c:c + 1], scale=-1.0)
            A("pe", "matmul", [ltn[0], "Wbd"], ["ps%d" % bx_], out=ps[bx_][:], lhsT=Wbd[:, 1, c, :], rhs=lt[0], start=True, stop=True)
            A("act", "activation", ["ps%d" % bx_, "const"], [ltn[2]], out=lt[2], in_=ps[bx_][:], func=AF.Exp, bias=vecs[:, 2, c:c + 1], scale=-1.0)
            for k_ in (1, 2):
                A("act", "activation", [ltn[k_]], [ltn[k_]], out=lt[k_], in_=lt[k_], func=AF.Ln, bias=1.0)
                A("act", "activation", [ltn[k_]], [ltn[k_]], out=lt[k_], in_=lt[k_], func=AF.Exp, scale=-1.0)
            if not light:
                A("pool", "tensor_tensor", ["xg%d" % c], [ltn[4]], out=lt[4], in0=xgT[:, c, :], in1=xgT[:, c, :], op=ALU.mult)
                A("dve", "tensor_scalar", [ltn[4]], [ltn[4]], out=lt[4], in0=lt[4], scalar1=0.044715 * 1.5957691216057308,
                  scalar2=1.5957691216057308, op0=ALU.mult, op1=ALU.add)
                A("pool", "tensor_tensor", [ltn[4], "xg%d" % c], [ltn[4]], out=lt[4], in0=lt[4], in1=xgT[:, c, :], op=ALU.mult)
                A("act", "activation", [ltn[4]], [ltn[4]], out=lt[4], in_=lt[4], func=AF.Exp, scale=-1.0)
                A("act", "activation", [ltn[4]], [ltn[4]], out=lt[4], in_=lt[4], func=AF.Ln, bias=1.0)
                A("act", "activation", [ltn[4]], [ltn[4]], out=lt[4], in_=lt[4], func=AF.Exp, scale=-1.0)
            A("act", "activation", [ltn[1], "nsp"], [ltn[1]], out=lt[1], in_=lt[1], func=AF.Exp, scale=vecs[:, 5, c:c + 1])
            A("dve", "tensor_tensor", [ltn[1]], [ltn[3]], out=lt[3], in0=lt[1], in1=lt[1], op=ALU.mult)
            A("dve", "tensor_scalar", [ltn[3]], [ltn[3]], out=lt[3], in0=lt[3], scalar1=-1.0, scalar2=1.0, op0=ALU.mult, op1=ALU.add)
            A("act", "activation", [ltn[3]], [ltn[3]], out=lt[3], in_=lt[3], func=AF.Ln)
            A("act", "activation", [ltn[3]], [ltn[3]], out=lt[3], in_=lt[3], func=AF.Exp, scale=0.5)
            A("pool", "tensor_tensor", [ltn[2], ltn[3]], [ltn[2]], out=lt[2], in0=lt[2], in1=lt[3], op=ALU.mult)
            A("pool", "tensor_tensor", [ltn[2], ltn[0]], [ltn[2]], out=lt[2], in0=lt[2], in1=lt[0], op=ALU.mult)
            if sample:
                for s in range(8):
                    A("dve", "tensor_tensor_scan", [ltn[1], ltn[2], "h0T"], [ltn[5]], out=lt[5][:, s * 64:(s + 1) * 64],
                      data0=lt[1][:, s * 64:(s + 1) * 64], data1=lt[2][:, s * 64:(s + 1) * 64], initial=h0T[:, c, s:s + 1],
                      op0=ALU.mult, op1=ALU.add)
                A("dve", "tensor_copy", [ltn[5]], ["hlast"], out=hlast[:, c, :], in_=v3(lt[5])[:, :, 63])
                A("pool", "tensor_copy", [xr], ["tails"], out=tails[:, c, :].rearrange("p (s j) -> p s j", s=8), in_=xrS[:, c, :, 64:67])
            else:
                A("dve", "tensor_tensor_scan", [ltn[1], ltn[2], "hcar"], [ltn[5]], out=lt[5], data0=lt[1], data1=lt[2],
                  initial=hcar[:, c:c + 1], op0=ALU.mult, op1=ALU.add)
                A("dve", "tensor_copy", [ltn[5]], ["hcar"], out=hcar[:, c:c + 1], in_=lt[5][:, 511:512])
                A("pool", "tensor_copy", [xr], [xr], out=xrP[:, c, 0:3], in_=xrP[:, c, 512:515])
            if not light:
                A("dve", "tensor_tensor", [ltn[5], "xg%d" % c], [ltn[3]], out=lt[3], in0=lt[5], in1=xgT[:, c, :], op=ALU.mult)
                A("dve", "tensor_tensor", [ltn[3], ltn[4]], ["hg%d" % c], out=hg[:, c, :], in0=lt[3], in1=lt[4], op=ALU.mult)
        cur[0] = ops
        lt, ltn = lt_sets[0], ltn_sets[0]
        if two_sets:
            for c0 in (0, 2):
                a_, b_ = ops_c[c0], ops_c[c0 + 1]
                for k_ in range(max(len(a_), len(b_))):
                    if k_ < len(a_):
                        ops.append(a_[k_])
                    if k_ < len(b_):
                        ops.append(b_[k_])
        else:
            for c in range(4):
                ops.extend(ops_c[c])
        if light:
            return ops
        for c in range(4):
            A("act", "activation", ["hg%d" % c], [ltn[c]], out=lt[c], in_=hg[:, c, :], func=AF.Square)
        for c in range(4):
            A("pe", "matmul", ["lt%d" % c, "onesf"], ["ps%d" % nb], out=ps[nb][:], lhsT=onesf[:], rhs=lt[c], start=(c == 0), stop=(c == 3))
        A("dve", "tensor_scalar", ["ps%d" % nb], ["rstdL"], out=rstdL, in0=ps[nb][:], scalar1=1.0 / 512, scalar2=EPS, op0=ALU.mult, op1=ALU.add)
        A("act", "activation", ["rstdL"], ["rstdL"], out=rstdL, in_=rstdL, func=AF.Ln)
        A("act", "activation", ["rstdL"], ["rstdL"], out=rstdL, in_=rstdL, func=AF.Exp, scale=-0.5)
        for c in range(4):
            A("dve", "scalar_tensor_tensor", ["hg%d" % c, "rstdL", "const"], [CAT[4 + c]], out=catT[:, 4 + c, :], in0=hg[:, c, :],
              scalar=vecs[:, 4, c:c + 1], in1=rstdL, op0=ALU.mult, op1=ALU.mult)
        return ops

    def pump(ops, n):
        def emit1():
            e_, m_, r_, w_, kw_ = ops.pop(0)
            A(e_, m_, r_, w_, **kw_)
            return e_
        while n > 0 and ops:
            e_ = emit1()
            n -= 1
            if e_ == "pe":
                while ops and ops[0][0] == "pe" and ops[0][4].get("start") is False:
                    emit1()
                if ops:
                    emit1()

    def w_out():
        for half in range(2):
            slot = next_chunk()
            wres = "w%d" % slot
            w = wring[:, slot, :].rearrange("p (k n) -> p k n", k=8)
            banks = [0, 1, 2, 3] if half == 0 else [4, 5, 6, 7]
            for sub in range(4):
                for kt in range(8):
                    A("pe", "matmul", [wres, CAT[kt]], ["ps%d" % banks[sub]], out=ps[banks[sub]][:],
                      lhsT=catT[:, kt, sub * 128:(sub + 1) * 128], rhs=w[:, kt, :], start=(kt == 0), stop=(kt == 7))
            for sub in range(4):
                A("dve", "tensor_tensor", ["ps%d" % banks[sub], XT[sub]], [XT[sub]], out=xt[:, sub, half * 512:(half + 1) * 512],
                  in0=ps[banks[sub]][:], in1=xt[:, sub, half * 512:(half + 1) * 512], op=ALU.add)

    def tm_view(d_ap, r0):
        return d_ap[r0:r0 + 512, :].rearrange("(s p) d -> p s d", p=128)

    tiles = [("p", t) for t in range(NT)] + ([("s", 0)] if do_sample else [])
    for kind, t in tiles:
        sample = kind == "s"
        light = (not sample) and t < NL
        t0 = t * 512
        o0 = (t - NL) * 512
        xsrc = tm_view(xs_d if sample else x_d, 0 if sample else t0)
        for sub in range(4):
            S.dma("sp", [(xt[:, sub, :], xsrc[:, sub, :])], [], [XT[sub]], "ld_x%d" % sub)
        att_state = None
        if (not sample) and (not light) and not (half and t == NL):
            att_state = attn_prefetch(t)
        if sample:
            S.dma("sp", [(sm_tm[0:24, 0, :], sconv_d), (sm_tm[0:8, 1, :], slru_d)], [], ["tmpn0", "tmpn1"], "ld_sm")
            for c in range(4):
                A("pe", "transpose", ["tmpn0", "tmpn1", "const"], ["ps6"], out=ps[6][:, c * 32:c * 32 + 24], in_=sm_tm[0:24, 0, c * 128:(c + 1) * 128],
                  identity=identf[0:24, 0:24])
                A("pe", "transpose", ["tmpn0", "tmpn1", "const"], ["ps6"], out=ps[6][:, 128 + c * 8:128 + c * 8 + 8], in_=sm_tm[0:8, 1, c * 128:(c + 1) * 128],
                  identity=identf[0:8, 0:8])
            for c in range(4):
                A("dve", "tensor_copy", ["ps6"], ["xr%d" % c], out=xrS[:, c, :, 0:3],
                  in_=ps[6][:, c * 32:c * 32 + 24].rearrange("p (s j) -> p s j", s=8))
            A("dve", "tensor_copy", ["ps6"], ["h0T"], out=h0T[:], in_=ps[6][:, 128:160].rearrange("p (c s) -> p c s", c=4))
        if half and (not sample) and t == 0:
            A("pool", "tensor_copy", ["const"], ["VAn"], out=VAn[:, :, :, 128:130], in_=flagt[:, 0:1].unsqueeze(1).unsqueeze(1).to_broadcast([128, 4, 4, 2]))
        if half and (not sample) and t == NL:
            A("pool", "memset", [], ["VAn"], ap=VAn[:, :, :, 128:130], constant=1.0)
        mark("%s%d norm1" % (kind, t))
        barrier("dve")
        norm_T(0)
        mark("%s%d ffn1" % (kind, t))
        ffn()
        if half and (not sample) and t == NL:
            A("dve", "tensor_scalar", ["hcar", "const"], ["hcar"], out=hcar[:], in0=hcar[:], scalar1=flagt[:, 0:1], scalar2=None, op0=ALU.mult)
            for c in range(4):
                A("dve", "tensor_scalar", ["xr%d" % c, "const"], ["xr%d" % c], out=xrP[:, c, 0:3], in0=xrP[:, c, 0:3], scalar1=flagt[:, 0:1],
                  scalar2=None, op0=ALU.mult)
            att_state = attn_prefetch(t, extra_writes=LA)
        mark("%s%d norm2" % (kind, t))
        norm_T(1)
        mark("%s%d w_in" % (kind, t))
        w_in(sample, light)
        if sample:
            S.dma("pool", [(tm_view(nks_d, 0), k_tm[:])], ["k_tm%d" % i for i in range(4)], ["nk_out"], "st_k")
            S.dma("pool", [(tm_view(nvs_d, 0), v_tm[:])], ["v_tm%d" % i for i in range(4)], ["nv_out"], "st_v")
        else:
            if not light:
                S.dma("pool", [(tm_view(nk_d, o0), k_tm[:])], ["k_tm%d" % i for i in range(4)], ["nk_out"], "st_k")
                S.dma("pool", [(tm_view(nv_d, o0), v_tm[:])], ["v_tm%d" % i for i in range(4)], ["nv_out"], "st_v")
            if t + 1 < NT:
                S.dma("pool", [(KT_s.rearrange("h p s -> p h s")[:, :, t0:t0 + 512], KTn[:])], ["KTn"], ["kts%d" % t], "st_kt")
                S.dma("pool", [(VA_s[h, :, 4 * t:4 * t + 4, :], VAn[:, :, h, :]) for h in range(4)], ["VAn"], ["vas%d" % t], "st_va")
        if light:
            mark("%s%d lru" % (kind, t))
            pt_f = [PT[:, k, :, :].rearrange("p a b -> p (a b)").bitcast(F32) for k in range(3)]
            q_f = q_tm[:].rearrange("p a b -> p (a b)").bitcast(F32)
            QT_f = QT[:].rearrange("p a b -> p (a b)").bitcast(F32)
            lb = pt_f + [q_f[:, 0:512], q_f[:, 512:1024], QT_f[:, 0:512]]
            LB = [["pt00", "pt01"], ["pt10", "pt11"], ["pt20", "pt21"], ["q_tm0", "q_tm1"], ["q_tm2", "q_tm3"], ["QT"]]
            pending["ops"] = lru_build(False, True, (la, lb), (LA, LB), [(4, 5), (6, 7)], 7)
            mark("%s%d end" % (kind, t))
            continue
        for e_ in ("act", "dve", "pool"):
            barrier(e_)
        mark("%s%d attn" % (kind, t))
        if sample:
            A("pool", "memset", [], ["VAq", "cvA", "cvB"], ap=VAq[:, :, :, 128:130], constant=1.0)
            lops = lru_build(True, False, lt, LTN, [(7, 7)], 7)
            attn_sample(lops)
            mark("%s%d lru" % (kind, t))
            pump(lops, len(lops))
            mark("%s%d otr" % (kind, t))
            o_transposes()
        else:
            lops = lru_build(False, False, lt, LTN, [(7, 7)], 7)
            attn_prompt(att_state, lops)
            mark("%s%d lru" % (kind, t))
            pump(lops, len(lops))
            mark("%s%d otr" % (kind, t))
            o_transposes()
        mark("%s%d w_out" % (kind, t))
        w_out()
        barrier("dve")
        mark("%s%d norm3" % (kind, t))
        norm_T(2)
        mark("%s%d ffn2" % (kind, t))
        ffn()
        mark("%s%d end" % (kind, t))
        ydst = tm_view(ys_d if sample else y_d, 0 if sample else o0)
        for sub in range(4):
            S.dma("pool", [(ydst[:, sub, :], xt[:, sub, :])], [XT[sub]], ["y_out%d" % sub], "st_y%d" % sub)
        if sample:
            for c in range(4):
                A("pe", "transpose", ["tails", "const"], ["ps6"], out=ps[6][0:24, c * 128:(c + 1) * 128], in_=tails[:, c, :], identity=identf[:])
                A("pe", "transpose", ["hlast", "const"], ["ps7"], out=ps[7][0:8, c * 128:(c + 1) * 128], in_=hlast[:, c, :], identity=identf[:])
            A("dve", "tensor_copy", ["ps6"], ["tmpn0", "tmpn1"], out=sm_tm[0:24, 0, :], in_=ps[6][0:24, :])
            A("dve", "tensor_copy", ["ps7"], ["tmpn0", "tmpn1"], out=sm_tm[0:8, 1, :], in_=ps[7][0:8, :])
            S.dma("pool", [(nconvs_d, sm_tm[0:24, 0, :]), (nlrus_d, sm_tm[0:8, 1, :])], ["tmpn0", "tmpn1"], ["st_out"], "st_sm")
        elif t == NT - 1:
            for c in range(4):
                A("pe", "transpose", ["xr%d" % c, "const"], ["ps6"], out=ps[6][0:3, c * 128:(c + 1) * 128], in_=xrP[:, c, 0:3], identity=identf[:])
                A("pe", "transpose", ["hcar", "const"], ["ps7"], out=ps[7][0:1, c * 128:(c + 1) * 128], in_=hcar[:, c:c + 1], identity=identf[:])
            A("dve", "tensor_copy", ["ps6"], ["tmpn0", "tmpn1"], out=sm_tm[0:3, 0, :], in_=ps[6][0:3, :])
            A("dve", "tensor_copy", ["ps7"], ["tmpn0", "tmpn1"], out=sm_tm[0:1, 1, :], in_=ps[7][0:1, :])
            S.dma("pool", [(nconv_d, sm_tm[0:3, 0, :]), (nlru_d, sm_tm[0:1, 1, :])], ["tmpn0", "tmpn1"], ["st_out"], "st_sm")
    S.wait_all_dma("sp")
    S.emit()
    es.close()
    return nc


_NC_CACHE = {}


def _get_nc(SEQ, PAST):
    key = (SEQ, PAST)
    if key not in _NC_CACHE:
        _NC_CACHE[key] = build(SEQ, PAST)
    return _NC_CACHE[key]


def make_in_maps(inputs, n_prompt):
    consts = host_consts()
    wmap = {}
    for n, s in W_SPECS:
        a = np.asarray(inputs[n], dtype=np.float32)
        if n != "rel_bias":
            a = a[0]
        wmap[n] = np.ascontiguousarray(a.reshape(s))
    SEQ = inputs["x_prompt"].shape[1]
    H = SEQ // 2
    maps = []
    for c in range(2 * n_prompt):
        b = c // 2
        m = dict(wmap)
        m.update(consts)
        xb = np.asarray(inputs["x_prompt"][b], dtype=np.float32)
        if c % 2 == 0:
            m["x"] = np.ascontiguousarray(np.concatenate([np.zeros((H, 1024), np.float32), xb[:H]], 0))
            m["flag"] = np.zeros((128, 1), np.float32)
        else:
            m["x"] = np.ascontiguousarray(xb)
            m["flag"] = np.ones((128, 1), np.float32)
        sl = slice(4 * c, 4 * c + 4)
        z = lambda *shape: np.zeros(shape, np.float32)
        m["xs"] = np.ascontiguousarray(np.concatenate([inputs["x_sample"][sl].reshape(256, 1024), z(256, 1024)], 0))
        past = inputs["cache_k"].shape[2]
        m["ck"] = np.ascontiguousarray(inputs["cache_k"][0, sl].reshape(4, past, 512))
        m["cv"] = np.ascontiguousarray(inputs["cache_v"][0, sl].reshape(4, past, 512))
        m["slru"] = np.ascontiguousarray(np.concatenate([inputs["state_lru"][0, sl], z(4, 512)], 0))
        m["sconv"] = np.ascontiguousarray(np.concatenate([inputs["state_conv"][0, sl].reshape(12, 512), z(12, 512)], 0))
        maps.append(m)
    return maps


def assemble(results, n_prompt, SEQ):
    B = n_prompt
    r = results
    H = SEQ // 2
    cat2 = lambda name, b: np.concatenate([r[2 * b][name], r[2 * b + 1][name]], 0)
    yp = np.stack([cat2("y", b) for b in range(B)], 0)
    NC_ = 2 * B
    ys = np.concatenate([r[c]["ys"].reshape(8, 64, 1024)[:4] for c in range(NC_)], 0)
    nk = np.stack([cat2("nk", b).reshape(SEQ, 4, 2, 64) for b in range(B)], 0)[None]
    nv = np.stack([cat2("nv", b).reshape(SEQ, 4, 128) for b in range(B)], 0)[None]
    nl = np.stack([r[2 * b + 1]["nlru"].reshape(512) for b in range(B)], 0)[None]
    ncv = np.stack([r[2 * b + 1]["nconv"].reshape(3, 512) for b in range(B)], 0)[None]
    nks = np.concatenate([r[c]["nks"].reshape(8, 64, 4, 2, 64)[:4] for c in range(NC_)], 0)[None]
    nvs = np.concatenate([r[c]["nvs"].reshape(8, 64, 4, 128)[:4] for c in range(NC_)], 0)[None]
    nls = np.concatenate([r[c]["nlrus"].reshape(8, 512)[:4] for c in range(NC_)], 0)[None]
    ncs = np.concatenate([r[c]["nconvs"].reshape(8, 3, 512)[:4] for c in range(NC_)], 0)[None]
    return tuple(np.ascontiguousarray(a, dtype=np.float32) for a in (yp, ys, nk, nv, nl, ncv, nks, nvs, nls, ncs))


def kernel(**inputs):
    inputs = {k: np.asarray(v) for k, v in inputs.items()}
    B, SEQ = inputs["x_prompt"].shape[0], inputs["x_prompt"].shape[1]
    PAST = inputs["cache_k"].shape[2]
    assert B == 4 and inputs["x_sample"].shape[0] == 32
    nc = _get_nc(SEQ, PAST)
    maps = make_in_maps(inputs, 4)
    res = run_bass_kernel_spmd(nc, maps, core_ids=list(range(8)))
    return assemble(res.results, 4, SEQ)
```

```python
import math
from contextlib import ExitStack
import numpy as np
import concourse.bass as bass
import concourse.mybir as mybir
from concourse.bass_utils import run_bass_kernel_spmd

F32 = mybir.dt.float32
BF16 = mybir.dt.bfloat16
AF = mybir.ActivationFunctionType
ALU = mybir.AluOpType
AX = mybir.AxisListType

ENGS = ("pe", "act", "dve", "pool", "sp")
INORDER_SAFE = ("pe", "sp")
EPS = 1e-6
NS = 3


class Op:
    __slots__ = ("eng", "fn", "waits", "inc_needed", "idx", "semval", "dma_inc")

    def __init__(self, eng, fn, waits):
        self.eng = eng
        self.fn = fn
        self.waits = waits
        self.inc_needed = False
        self.idx = -1
        self.semval = 0
        self.dma_inc = None


class Sched:
    def __init__(self, nc, same_engine_sync=True):
        self.nc = nc
        self.streams = {e: [] for e in ENGS}
        self.res = {}
        self.seen = {e: {} for e in ENGS}
        self.dma_tot = {}
        self.same_engine_sync = same_engine_sync

    def _need(self, eng, ev, waits):
        if ev[0] == "op":
            op = ev[1]
            if op.eng == eng and (eng in INORDER_SAFE or not self.same_engine_sync):
                return
            key = ("op", op.eng)
            if self.seen[eng].get(key, -1) >= op.idx:
                return
            self.seen[eng][key] = op.idx
            op.inc_needed = True
            waits.append(ev)
        else:
            key = ("dma", ev[1])
            if self.seen[eng].get(key, -1) >= ev[2]:
                return
            self.seen[eng][key] = ev[2]
            waits.append(ev)

    def _collect(self, eng, reads, writes):
        waits = []
        for r in reads:
            st = self.res.setdefault(r, [[], []])
            for ev in st[0]:
                self._need(eng, ev, waits)
        for w in writes:
            st = self.res.setdefault(w, [[], []])
            for ev in st[0]:
                self._need(eng, ev, waits)
            for ev in st[1]:
                self._need(eng, ev, waits)
        return waits

    def _commit(self, ev, reads, writes):
        for r in reads:
            self.res[r][1].append(ev)
        for w in writes:
            self.res[w] = [[ev], []]

    def op(self, eng, fn, reads=(), writes=()):
        waits = self._collect(eng, reads, writes)
        o = Op(eng, fn, waits)
        o.idx = len(self.streams[eng])
        self.streams[eng].append(o)
        self._commit(("op", o), reads, writes)
        return o

    def dma(self, q, pairs, reads, writes, semkey):
        waits = self._collect(q, reads, writes)
        tot = self.dma_tot.get(semkey, 0)
        first = True
        for pr in pairs:
            out_ap, in_ap = pr[0], pr[1]
            kw = pr[2] if len(pr) > 2 else {}
            tot += 16
            o = Op(q, (lambda e, a=out_ap, b=in_ap, k=kw: e.dma_start(out=a, in_=b, **k)), waits if first else [])
            first = False
            o.idx = len(self.streams[q])
            o.dma_inc = semkey
            self.streams[q].append(o)
        self.dma_tot[semkey] = tot
        self._commit(("dma", semkey, tot), reads, writes)

    def wait_all_dma(self, eng="sp"):
        waits = [("dma", k, v) for k, v in self.dma_tot.items()]
        o = Op(eng, None, waits)
        o.idx = len(self.streams[eng])
        self.streams[eng].append(o)

    def emit(self):
        nc = self.nc
        with ExitStack() as es:
            esem = {e: es.enter_context(nc.semaphore("s_" + e)) for e in ENGS}
            dsem = {k: es.enter_context(nc.semaphore("d_" + str(k))) for k in self.dma_tot}
            for e, ops in self.streams.items():
                c = 0
                for o in ops:
                    if o.inc_needed:
                        c += 1
                        o.semval = c
            block = es.enter_context(nc.Block())

            def run(eng_name):
                def body(e):
                    for o in self.streams[eng_name]:
                        for ev in o.waits:
                            if ev[0] == "op":
                                e.wait_ge(esem[ev[1].eng], ev[1].semval)
                            else:
                                e.wait_ge(dsem[ev[1]], ev[2])
                        if o.fn is None:
                            continue
                        ins = o.fn(e)
                        if o.dma_inc is not None:
                            ins.then_inc(dsem[o.dma_inc], 16)
                        elif o.inc_needed:
                            ins.then_inc(esem[eng_name], 1)
                return body

            block.tensor(run("pe"))
            block.scalar(run("act"))
            block.vector(run("dve"))
            block.gpsimd(run("pool"))
            block.sync(run("sp"))
        return nc


W_SPECS = [
    ("rel_bias", (32, 4)), ("norm_ffn1", (1, 1024)), ("ffn1_gate", (1024, 2816)), ("ffn1_up", (1024, 2816)),
    ("ffn1_down", (2816, 1024)), ("norm_mix", (1, 1024)), ("w_in", (1024, 2560)), ("q_norm", (1, 64)),
    ("k_norm", (1, 64)), ("lambda_q1", (1, 64)), ("lambda_k1", (1, 64)), ("lambda_q2", (1, 64)),
    ("lambda_k2", (1, 64)), ("subln", (1, 128)), ("conv_w", (4, 512)), ("conv_b", (1, 512)),
    ("gate_a_w", (8, 64, 64)), ("gate_a_b", (1, 512)), ("gate_x_w", (8, 64, 64)), ("gate_x_b", (1, 512)),
    ("lru_L", (1, 512)), ("lru_out_norm", (1, 512)), ("w_out", (1024, 1024)), ("norm_ffn2", (1, 1024)),
    ("ffn2_gate", (1024, 2816)), ("ffn2_up", (1024, 2816)), ("ffn2_down", (2816, 1024)),
]


def t5_bucket_np(rel):
    n = 16
    max_exact = 8
    ret = np.where(rel > 0, n, 0)
    rel = np.abs(rel)
    relf = np.maximum(rel, 1).astype(np.float32)
    large = max_exact + (np.log(relf / np.float32(max_exact)) / np.float32(math.log(128 / max_exact))
                         * np.float32(n - max_exact)).astype(np.int32)
    large = np.minimum(large, n - 1)
    return ret + np.where(rel < max_exact, rel, large)


def host_consts():
    ident = np.eye(128, dtype=np.float32)
    jrev = np.ascontiguousarray(ident[::-1])
    m = np.arange(384)
    bk = t5_bucket_np(m - 255)
    oh = np.zeros((32, 384), np.float32)
    oh[bk, m] = 1.0
    oh[15, :] -= 1.0
    i = np.arange(128)[:, None]
    r = np.arange(256)[None, :]
    vis = np.where((r < 128) & ((i // 64) > (r // 64)), 0.0, 1.0).astype(np.float32)
    r2 = np.arange(128)[None, :]
    vis2 = ((i // 64) == (r2 // 64)).astype(np.float32)
    return dict(c_ident=ident, c_j=jrev, c_oh=oh, c_vis=vis, c_vis2=vis2)


PHASES = []
MARKERS = False


def build(SEQ, PAST, do_sample=True, half=True):
    NT = SEQ // 512
    NL = NT // 2 if half else 0
    OSEQ = SEQ - NL * 512
    NKB = SEQ // 128
    PKB = PAST // 128
    assert SEQ % 512 == 0 and PAST % 512 == 0
    nc = bass.Bass("TRN2", target_bir_lowering=False)
    S = Sched(nc)
    es = ExitStack()
    del PHASES[:]

    def mark(name):
        PHASES.append((name, len(S.streams["pe"])))
        if MARKERS:
            S.op("dve", lambda e: e.memset(ap=dummy[:, 3:4], constant=0.0), [], [])

    def din(n, s, d=F32):
        return nc.dram_tensor(n, list(s), d, kind="ExternalInput").ap()

    def dout(n, s, d=F32):
        return nc.dram_tensor(n, list(s), d, kind="ExternalOutput").ap()

    def dscr(n, s, d):
        return nc.dram_tensor(n, list(s), d, kind="Internal").ap()

    def sb(n, s, d):
        return es.enter_context(nc.sbuf_tensor(n, list(s), d))

    def A(eng, method, reads, writes, **kw):
        return S.op(eng, lambda e: getattr(e, method)(**kw), reads, writes)

    x_d = din("x", (SEQ, 1024))
    xs_d = din("xs", (512, 1024))
    NSR = 4
    ck_d = din("ck", (NSR, PAST, 512))
    cv_d = din("cv", (NSR, PAST, 512))
    slru_d = din("slru", (8, 512))
    sconv_d = din("sconv", (24, 512))
    W = {n: din(n, s) for n, s in W_SPECS}
    c_ident = din("c_ident", (128, 128))
    c_j = din("c_j", (128, 128))
    c_oh = din("c_oh", (32, 384))
    c_vis = din("c_vis", (128, 256))
    c_vis2 = din("c_vis2", (128, 128))

    flag_d = din("flag", (128, 1))
    y_d = dout("y", (OSEQ, 1024))
    ys_d = dout("ys", (512, 1024))
    nk_d = dout("nk", (OSEQ, 512))
    nv_d = dout("nv", (OSEQ, 512))
    nlru_d = dout("nlru", (1, 512))
    nconv_d = dout("nconv", (3, 512))
    nks_d = dout("nks", (512, 512))
    nvs_d = dout("nvs", (512, 512))
    nlrus_d = dout("nlrus", (8, 512))
    nconvs_d = dout("nconvs", (24, 512))

    wsc = dscr("wsc", (41, 128, 4096), BF16)
    KT_s = dscr("KT_s", (4, 128, SEQ), BF16)
    VA_s = dscr("VA_s", (4, 128, NKB, 130), BF16)
    eu_s = dscr("eu_s", (4, 384), F32)

    xt = sb("xt", (128, 4, 1024), F32)
    arena = sb("arena", (128, 11264), BF16)
    hidT = arena[:].rearrange("p (k n) -> p k n", k=22)
    arena_f = arena[:].bitcast(F32)
    lt = [arena_f[:, k * 512:(k + 1) * 512] for k in range(6)]
    rstdL = arena_f[:, 3072:3584]
    hg = arena_f[:, 3584:5632].rearrange("p (c n) -> p c n", c=4)
    wring = sb("wring", (128, NS, 4096), BF16)
    xnT_t = sb("xnT", (128, 8, 512), BF16)
    xnT = xnT_t[:]
    catT = xnT_t[:]
    xsb = sb("xsb", (128, 2, 1024), BF16)
    sg = sb("sg", (128, 2, 512), F32)
    q_tm = sb("q_tm", (128, 4, 512), BF16)
    k_tm = sb("k_tm", (128, 4, 512), F32)
    k_bf = sb("k_bf", (128, 4, 512), BF16)
    v_tm = sb("v_tm", (128, 4, 512), F32)
    tmpn = sb("tmpn", (128, 2, 512), F32)
    QT = sb("QT", (128, 4, 512), BF16)
    KTn = sb("KTn", (128, 4, 512), BF16)
    VAn = sb("VAn", (128, 4, 4, 130), BF16)
    att = sb("att", (128, 8256), BF16)
    KTc = att[:, 0:4096].rearrange("p (s n) -> p s n", s=2)
    VAc = att[:, 4096:8256].rearrange("p (s k e) -> p s k e", s=2, k=16)
    kst = att[:, 0:4096].bitcast(F32).rearrange("p (k d) -> p k d", k=4)
    vst = att[:, 4096:8192].bitcast(F32).rearrange("p (k d) -> p k d", k=4)
    att_f = att[:, 0:8192].bitcast(F32)
    la = [att_f[:, k * 512:(k + 1) * 512] for k in range(6)]
    LA = ["la%d" % k for k in range(6)]
    LTN = ["lt%d" % k for k in range(6)]
    sx = sb("sx", (128, 6176), BF16)
    kbf = sx[:, 0:2048].rearrange("p (k d) -> p k d", k=4)
    KTq = sx[:, 2048:4096].rearrange("p (h n) -> p h n", h=4)
    VAq = sx[:, 4096:6176].rearrange("p (k h e) -> p k h e", k=4, h=4)
    PT = sb("PT", (128, 3, 2, 512), BF16)
    o_tm = sb("o_tm", (128, 4, 512), BF16)
    osb = sb("osb", (128, 5, 128), F32)
    Osb = sb("Osb", (128, 3, 390), F32)
    xrT = sb("xrT", (128, 2144), F32)
    xrP = xrT[:, 0:2060].rearrange("p (c n) -> p c n", c=4)
    xrS = xrT[:, 0:2144].rearrange("p (c s n) -> p c s n", c=4, s=8)
    xgT = sb("xgT", (128, 4, 512), F32)
    identf = sb("identf", (128, 128), F32)
    identb = sb("identb", (128, 128), BF16)
    jrev = sb("jrev", (128, 128), F32)
    onesf = sb("onesf", (128, 128), F32)
    vis = sb("vis", (128, 256), F32)
    vis2 = sb("vis2", (128, 128), F32)
    ET = sb("ET", (128, 4, 256), F32)
    ETp = sb("ETp", (128, 4, 128), F32)
    Hk = sb("Hk", (128, 4, 2, 128), F32)
    gT = sb("gT", (128, 3, 8), F32)
    gq_rep = sb("gq_rep", (128, 8, 64), F32)
    gk_rep = sb("gk_rep", (128, 8, 64), F32)
    gsub_rep = sb("gsub_rep", (128, 128), F32)
    cw = sb("cw", (128, 4, 4), F32)
    vecs = sb("vecs", (128, 6, 4), F32)
    Wbd = sb("Wbd", (128, 2, 4, 128), F32)
    relb = sb("relb", (32, 4), F32)
    oh = sb("oh", (32, 384), F32)
    eu = sb("eu", (4, 384), F32)
    lamv = sb("lamv", (1, 4, 64), F32)
    lams = sb("lams", (1, 8), F32)
    neglam = sb("neglam", (128, 1), F32)
    st = sb("st", (128, 64), F32)
    hcar = sb("hcar", (128, 4), F32)
    h0T = sb("h0T", (128, 4, 8), F32)
    hlast = sb("hlast", (128, 4, 8), F32)
    tails = sb("tails", (128, 4, 24), F32)
    sm_tm = tmpn
    dummy = sb("dummyt", (128, 4), F32)
    flagt = sb("flagt", (128, 1), F32)
    cm05 = sb("cm05", (128, 2), F32)

    psall = es.enter_context(nc.psum_tensor("psall", [128, 8, 512], F32))

    class _PV:
        def __init__(self, i):
            self.i = i

        def __getitem__(self, key):
            return psall[:, self.i, :][key]

    ps = [_PV(i) for i in range(8)]
    psb = [psall[:, i, :].bitcast(BF16) for i in range(8)]
    print("sbuf bytes remaining", nc.sbuf_bytes_remaining)

    XT = ["xt0", "xt1", "xt2", "xt3"]
    HID = ["hid%d" % i for i in range(22)]
    LT = ["lt%d" % i for i in range(6)] + ["rstdL", "hg0", "hg1", "hg2", "hg3"]
    XN = ["xn0", "xn1", "xn2", "xn3"]
    CAT = ["cat%d" % i for i in range(8)]

    ALLA = HID + XN + LT + CAT

    def barrier(eng, writes=None):
        writes = ALLA
        k = ENGS.index(eng) % 4
        if eng == "act":
            A(eng, "copy", [], list(writes) + ["dummy_" + eng], out=dummy[:, k:k + 1], in_=onesf[:, 0:1])
        else:
            A(eng, "memset", [], list(writes) + ["dummy_" + eng], ap=dummy[:, k:k + 1], constant=0.0)

    slow = dict(allow_slow_non_contiguous=True)
    S.dma("sp", [(identf[:], c_ident), (jrev[:], c_j), (oh[:], c_oh), (vis[:], c_vis), (vis2[:], c_vis2),
                 (relb[:], W["rel_bias"]),
                 (gT[:, 0, :], W["norm_ffn1"].rearrange("o (k p) -> p (o k)", p=128), slow),
                 (gT[:, 1, :], W["norm_mix"].rearrange("o (k p) -> p (o k)", p=128), slow),
                 (gT[:, 2, :], W["norm_ffn2"].rearrange("o (k p) -> p (o k)", p=128), slow),
                 (gq_rep[:], bass.AP(W["q_norm"].tensor, 0, [[0, 128], [0, 8], [1, 64]])),
                 (gk_rep[:], bass.AP(W["k_norm"].tensor, 0, [[0, 128], [0, 8], [1, 64]])),
                 (gsub_rep[:], bass.AP(W["subln"].tensor, 0, [[0, 128], [1, 128]])),
                 (cw[:, 0, :], W["conv_w"][0:1, :].rearrange("o (c p) -> p (o c)", p=128), slow),
                 (cw[:, 1, :], W["conv_w"][1:2, :].rearrange("o (c p) -> p (o c)", p=128), slow),
                 (cw[:, 2, :], W["conv_w"][2:3, :].rearrange("o (c p) -> p (o c)", p=128), slow),
                 (cw[:, 3, :], W["conv_w"][3:4, :].rearrange("o (c p) -> p (o c)", p=128), slow),
                 (vecs[:, 0, :], W["conv_b"].rearrange("o (c p) -> p (o c)", p=128), slow),
                 (vecs[:, 1, :], W["gate_a_b"].rearrange("o (c p) -> p (o c)", p=128), slow),
                 (vecs[:, 2, :], W["gate_x_b"].rearrange("o (c p) -> p (o c)", p=128), slow),
                 (vecs[:, 3, :], W["lru_L"].rearrange("o (c p) -> p (o c)", p=128), slow),
                 (vecs[:, 4, :], W["lru_out_norm"].rearrange("o (c p) -> p (o c)", p=128), slow),
                 (lamv[:, 0, :], W["lambda_q1"]), (lamv[:, 1, :], W["lambda_q2"]),
                 (lamv[:, 2, :], W["lambda_k1"]), (lamv[:, 3, :], W["lambda_k2"]),
                 (flagt[:], flag_d),
                 ], [], ["const"], "const")
    A("pool", "memset", [], ["Wbd"], ap=Wbd[:], constant=0.0)
    A("pool", "memset", [], ["onesf"], ap=onesf[:], constant=1.0)
    A("pool", "memset", [], ["cm05"], ap=cm05[:, 0:1], constant=-0.5)
    A("pool", "memset", [], ["cm05"], ap=cm05[:, 1:2], constant=0.5)
    A("pool", "memset", [], ["VAn"], ap=VAn[:], constant=1.0)
    A("pool", "memset", [], ["VAq"], ap=VAq, constant=1.0)
    A("pool", "memset", [], ["hcar"], ap=hcar[:], constant=0.0)
    A("pool", "memset", [], ["xr0", "xr1", "xr2", "xr3"], ap=xrT[:], constant=0.0)
    wb_pairs = []
    for gi, gname in enumerate(("gate_a_w", "gate_x_w")):
        for n in range(8):
            o0 = 64 * (n % 2)
            wb_pairs.append((Wbd[o0:o0 + 64, gi, n // 2, o0:o0 + 64], W[gname][n, :, :]))
    S.dma("sp", wb_pairs, [], ["Wbd"], "const2")
    A("dve", "tensor_copy", ["const"], ["identb"], out=identb[:], in_=identf[:])
    A("dve", "tensor_scalar", ["const"], ["gsub"], out=gsub_rep[:], in0=gsub_rep[:], scalar1=0.8, scalar2=None, op0=ALU.mult)
    A("dve", "tensor_scalar", ["const"], ["const"], out=vecs[:, 1:3, :], in0=vecs[:, 1:3, :], scalar1=-1.0, scalar2=None, op0=ALU.mult)
    A("act", "activation", ["const"], ["nsp"], out=vecs[:, 5, :], in_=vecs[:, 3, :], func=AF.Exp, scale=-1.0)
    A("act", "activation", ["nsp"], ["nsp"], out=vecs[:, 5, :], in_=vecs[:, 5, :], func=AF.Ln, bias=1.0)
    A("dve", "tensor_scalar", ["nsp"], ["nsp"], out=vecs[:, 5, :], in0=vecs[:, 5, :], scalar1=-8.0, scalar2=None, op0=ALU.mult)
    A("dve", "tensor_tensor", ["const"], ["lamv"], out=lamv[:, 0:2, :], in0=lamv[:, 0:2, :], in1=lamv[:, 2:4, :], op=ALU.mult)
    A("dve", "tensor_reduce", ["lamv"], ["lams"], out=lams[:, 0:2], in_=lamv[:, 0:2, :], axis=AX.X, op=ALU.add)
    A("act", "activation", ["lams"], ["lams"], out=lams[:, 2:4], in_=lams[:, 0:2], func=AF.Exp)
    A("dve", "tensor_tensor", ["lams"], ["lams"], out=lams[:, 4:5], in0=lams[:, 3:4], in1=lams[:, 2:3], op=ALU.subtract)
    A("dve", "tensor_scalar", ["lams"], ["lams"], out=lams[:, 5:6], in0=lams[:, 4:5], scalar1=-0.2, scalar2=None, op0=ALU.add)
    A("pe", "matmul", ["lams", "onesf"], ["ps0"], out=ps[0][:, 0:1], lhsT=onesf[0:1, :], rhs=lams[:, 5:6], start=True, stop=True)
    A("dve", "tensor_copy", ["ps0"], ["neglam"], out=neglam[:], in_=ps[0][:, 0:1])
    A("pe", "matmul", ["const"], ["ps1"], out=ps[1][0:4, 0:384], lhsT=relb[:], rhs=oh[:], start=True, stop=True)
    A("act", "activation", ["ps1"], ["eu"], out=eu[:], in_=ps[1][0:4, 0:384], func=AF.Exp)
    S.dma("sp", [(eu_s, eu[:])], ["eu"], ["eu_s"], "st_eu")
    S.dma("sp", [(Hk[:, h, :, :], bass.AP(eu_s.tensor, h * 384, [[1, 128], [128, 2], [1, 128]])) for h in range(4)],
          ["eu_s"], ["Hk"], "ld_hk")
    for h in range(4):
        b = 2 + (h % 2)
        A("pe", "matmul", ["Hk", "const"], ["ps%d" % b], out=ps[b][:, 128:256], lhsT=Hk[:, h, 0, :], rhs=jrev[:], start=True, stop=True)
        A("pe", "matmul", ["Hk", "const"], ["ps%d" % b], out=ps[b][:, 0:128], lhsT=Hk[:, h, 1, :], rhs=jrev[:], start=True, stop=True)
        A("dve", "tensor_tensor", ["ps%d" % b, "const"], ["ET"], out=ET[:, h, :], in0=ps[b][:, 0:256], in1=vis[:], op=ALU.mult)
        A("dve", "tensor_tensor", ["ps%d" % b, "const"], ["ET"], out=ETp[:, h, :], in0=ps[b][:, 0:128], in1=vis2[:], op=ALU.mult)

    chunks = []
    for f in ("ffn1", "w_in", "w_out", "ffn2"):
        if f in ("ffn1", "ffn2"):
            for j in range(11):
                chunks.append([(W[f + "_gate"], 0, 8, 256 * j, 256), (W[f + "_up"], 0, 8, 256 * j, 256)])
            for half in range(2):
                for g in range(3):
                    chunks.append([(W[f + "_down"], g * 8, min(8, 22 - g * 8), 512 * half, 512)])
        elif f == "w_in":
            for b in range(5):
                chunks.append([(W["w_in"], 0, 8, 512 * b, 512)])
        else:
            for half in range(2):
                chunks.append([(W["w_out"], 0, 8, 512 * half, 512)])
    assert len(chunks) == 41
    chunk_len = [sum(nk * ncols for (_, _, nk, _, ncols) in pieces) for pieces in chunks]
    stage = [xt[:].rearrange("p a b -> p (a b)"), arena_f[:, 0:4096]]
    stage_res = [XT, HID + LT]
    cast_eng = ["dve", "pool", "act"]
    N_EAGER = 17 if (half and NL >= 6) else 41
    sx_f = sx[:, 0:4096].bitcast(F32)
    lz_stage = [sx_f[:, 0:1024], sx_f[:, 1024:2048]]
    lz_stage_res = [["kbf"], ["KTq0", "KTq1", "KTq2", "KTq3"]]
    lz_out = [sx[:, 4096:5120], sx[:, 5120:6144]]
    lz_out_res = [["cvA"], ["cvB"]]
    lazy_pieces = []
    lz_ctr = [0]

    def lazy_piece(ci, off, wap, kt0, nk, col0, ncols):
        k = lz_ctr[0] % 2
        lz_ctr[0] += 1
        n = nk * ncols
        S.dma("sp", [(lz_stage[k][:, 0:n].rearrange("p (k n) -> p k n", k=nk),
                      wap[kt0 * 128:(kt0 + nk) * 128, col0:col0 + ncols].rearrange("(k p) n -> p k n", p=128))], [], lz_stage_res[k], "lzs%d" % k)
        eng = ("dve", "act")[lz_ctr[0] % 2]
        if eng == "act":
            A("act", "activation", lz_stage_res[k], lz_out_res[k], out=lz_out[k][:, 0:n], in_=lz_stage[k][:, 0:n], func=AF.Copy)
        else:
            A("dve", "tensor_copy", lz_stage_res[k], lz_out_res[k], out=lz_out[k][:, 0:n], in_=lz_stage[k][:, 0:n])
        S.dma("pool", [(wsc[ci, :, off:off + n], lz_out[k][:, 0:n])], lz_out_res[k], ["wsc%d" % ci], "lzo%d" % k)

    conv_order = list(range(N_EAGER)) + [c_ for c_ in (18, 19, 20, 21, 17) if c_ >= N_EAGER] + [c_ for c_ in range(22, 41) if c_ >= N_EAGER]
    for ci in conv_order:
        pieces = chunks[ci]
        if ci >= N_EAGER:
            off = 0
            for (wap, kt0, nk, col0, ncols) in pieces:
                step = max(1, 1024 // ncols)
                for q0 in range(0, nk, step):
                    nq = min(step, nk - q0)
                    lazy_pieces.append((ci, off, wap, kt0 + q0, nq, col0, ncols))
                    off += nq * ncols
            continue
        sidx = ci % 2
        stg = stage[sidx]
        pairs = []
        off = 0
        for (wap, kt0, nk, col0, ncols) in pieces:
            pairs.append((stg[:, off:off + nk * ncols].rearrange("p (k n) -> p k n", k=nk),
                          wap[kt0 * 128:(kt0 + nk) * 128, col0:col0 + ncols].rearrange("(k p) n -> p k n", p=128)))
            off += nk * ncols
        S.dma("sp", pairs, [], stage_res[sidx], "pst%d" % sidx)
        cslot = 1 + sidx
        eng = cast_eng[ci % 3]
        if eng == "act":
            A("act", "activation", stage_res[sidx], ["w%d" % cslot], out=wring[:, cslot, 0:off], in_=stg[:, 0:off], func=AF.Copy)
        else:
            A(eng, "tensor_copy", stage_res[sidx], ["w%d" % cslot], out=wring[:, cslot, 0:off], in_=stg[:, 0:off])
        S.dma("pool", [(wsc[ci, :, 0:off], wring[:, cslot, 0:off])], ["w%d" % cslot], ["wsc%d" % ci], "stc%d" % sidx)

    chunk_seq = []
    for t_ in range(NT):
        chunk_seq += (list(range(17)) + [18, 19, 20, 21]) if t_ < NL else list(range(41))
    if do_sample:
        chunk_seq += list(range(41))
    total_chunks = len(chunk_seq)
    ws = dict(cur=0, issued=0)

    def next_chunk():
        i = ws["cur"]
        ws["cur"] += 1
        while ws["issued"] < min(total_chunks, i + NS):
            k = ws["issued"]
            slot = k % NS
            ci = chunk_seq[k]
            n = chunk_len[ci]
            S.dma("sp", [(wring[:, slot, 0:n], wsc[ci, :, 0:n])], ["wsc%d" % ci], ["w%d" % slot], "w%d" % slot)
            ws["issued"] += 1
        return i % NS

    def rsqrt_small(ap, n_inv, res, rows=slice(0, 128)):
        A("dve", "tensor_scalar", [res], [res], out=ap, in0=ap, scalar1=n_inv, scalar2=EPS, op0=ALU.mult, op1=ALU.add)
        A("act", "activation", [res], [res], out=ap, in_=ap, func=AF.Ln)
        A("act", "activation", [res], [res], out=ap, in_=ap, func=AF.Exp, scale=-0.5)

    def norm_T(gi):
        for sub in range(4):
            A("act", "activation", [XT[sub]], ["sg0", "sg1", "ssn"], out=sg[:].rearrange("p a b -> p (a b)"), in_=xt[:, sub, :], func=AF.Square,
              accum_out=st[:, sub:sub + 1])
        rsqrt_small(st[:, 0:4], 1.0 / 1024, "ssn")
        for sub in range(4):
            xb = "xsb%d" % (sub % 2)
            A("act", "activation", [XT[sub], "ssn"], [xb], out=xsb[:, sub % 2, :], in_=xt[:, sub, :], func=AF.Copy,
              scale=st[:, sub:sub + 1])
            bk = 6 + (sub % 2)
            for kt in range(8):
                A("pe", "transpose", [xb, "identb"], ["ps%d" % bk], out=psb[bk][:, kt * 128:(kt + 1) * 128],
                  in_=xsb[:, sub % 2, kt * 128:(kt + 1) * 128], identity=identb[:])
            A("dve", "tensor_tensor", ["ps%d" % bk, "const"], [XN[sub]], out=xnT[:, :, sub * 128:(sub + 1) * 128],
              in0=psb[bk][:, 0:1024].rearrange("p (k n) -> p k n", k=8),
              in1=gT[:, gi, :].unsqueeze(2).to_broadcast([128, 8, 128]), op=ALU.mult)

    pending = dict(ops=None)

    def ffn():
        pend = pending["ops"]
        rate = (len(pend) + 19) // 20 if pend else 0
        for grp in range(11):
            slot = next_chunk()
            wres = "w%d" % slot
            w = wring[:, slot, :].rearrange("p (g k n) -> p g k n", g=2, k=8)
            for j in range(2):
                nt = grp * 2 + j
                bg, bu = (0, 1) if nt % 2 == 0 else (2, 3)
                for kt in range(8):
                    A("pe", "matmul", [wres] + XN, ["ps%d" % bg], out=ps[bg][:], lhsT=w[:, 0, kt, j * 128:(j + 1) * 128],
                      rhs=xnT[:, kt, :], start=(kt == 0), stop=(kt == 7))
                if pend:
                    pump(pend, 1)
                for kt in range(8):
                    A("pe", "matmul", [wres] + XN, ["ps%d" % bu], out=ps[bu][:], lhsT=w[:, 1, kt, j * 128:(j + 1) * 128],
                      rhs=xnT[:, kt, :], start=(kt == 0), stop=(kt == 7))
                if pend:
                    pump(pend, 1)
                sgn = "sg%d" % (nt % 2)
                A("act", "activation", ["ps%d" % bg], [sgn], out=sg[:, nt % 2, :], in_=ps[bg][:], func=AF.Exp, scale=-1.0)
                A("act", "activation", [sgn], [sgn], out=sg[:, nt % 2, :], in_=sg[:, nt % 2, :], func=AF.Ln, bias=1.0)
                A("act", "activation", [sgn], [sgn], out=sg[:, nt % 2, :], in_=sg[:, nt % 2, :], func=AF.Exp, scale=-1.0)
                if pend:
                    pump(pend, 1)
                A("dve", "tensor_tensor", [sgn, "ps%d" % bg], [sgn], out=sg[:, nt % 2, :], in0=sg[:, nt % 2, :], in1=ps[bg][:], op=ALU.mult)
                A("dve", "tensor_tensor", [sgn, "ps%d" % bu], [HID[nt]], out=hidT[:, nt, :], in0=sg[:, nt % 2, :], in1=ps[bu][:], op=ALU.mult)
                if pend:
                    pump(pend, max(1, rate - 3))
                if lazy_pieces:
                    lazy_piece(*lazy_pieces.pop(0))
        if pend is not None:
            pump(pend, len(pend))
            pending["ops"] = None
        for half in range(2):
            banks = [4, 5, 6, 7] if half == 0 else [0, 1, 2, 3]
            for g in range(3):
                slot = next_chunk()
                wres = "w%d" % slot
                w = wring[:, slot, :].rearrange("p (k n) -> p k n", k=8)
                for kti, kt in enumerate(range(g * 8, min(22, g * 8 + 8))):
                    for sub in range(4):
                        A("pe", "matmul", [wres, HID[kt]], ["ps%d" % banks[sub]], out=ps[banks[sub]][:],
                          lhsT=hidT[:, kt, sub * 128:(sub + 1) * 128], rhs=w[:, kti, :], start=(kt == 0), stop=(kt == 21))
            for sub in range(4):
                A("dve", "scalar_tensor_tensor", ["ps%d" % banks[sub], XT[sub]], [XT[sub]],
                  out=xt[:, sub, half * 512:(half + 1) * 512], in0=ps[banks[sub]][:], scalar=0.5,
                  in1=xt[:, sub, half * 512:(half + 1) * 512], op0=ALU.mult, op1=ALU.add)

    def w_in(sample, light=False):
        for blk in range(3):
            if light and blk == 0:
                continue
            slot = next_chunk()
            wres = "w%d" % slot
            w = wring[:, slot, :].rearrange("p (k n) -> p k n", k=8)
            banks = [0, 1, 2, 3] if blk % 2 == 0 else [4, 5, 6, 7]
            for sub in range(4):
                for kt in range(8):
                    A("pe", "matmul", [wres, XN[sub]], ["ps%d" % banks[sub]], out=ps[banks[sub]][:],
                      lhsT=xnT[:, kt, sub * 128:(sub + 1) * 128], rhs=w[:, kt, :], start=(kt == 0), stop=(kt == 7))
            for sub in range(4):
                pb = "ps%d" % banks[sub]
                pv = ps[banks[sub]][:]
                if blk < 2:
                    ti = sub % 2
                    tn = "tmpn%d" % ti
                    ssq = st[:, 8 + 8 * ti:16 + 8 * ti]
                    A("act", "activation", [pb], [tn], out=tmpn[:, ti, :], in_=pv, func=AF.Square)
                    A("dve", "tensor_reduce", [tn], ["ssq%d" % ti], out=ssq, in_=tmpn[:, ti, :].rearrange("p (g d) -> p g d", g=8),
                      axis=AX.X, op=ALU.add)
                    rsqrt_small(ssq, 1.0 / 64, "ssq%d" % ti)
                    A("dve", "tensor_tensor", [pb, "ssq%d" % ti, tn], [tn], out=tmpn[:, ti, :].rearrange("p (g d) -> p g d", g=8),
                      in0=pv.rearrange("p (g d) -> p g d", g=8), in1=ssq.unsqueeze(2).to_broadcast([128, 8, 64]), op=ALU.mult)
                    if blk == 0:
                        A("pool", "tensor_tensor", [tn, "const"], ["q_tm%d" % sub], out=q_tm[:, sub, :], in0=tmpn[:, ti, :],
                          in1=gq_rep[:].rearrange("p g d -> p (g d)"), op=ALU.mult)
                    else:
                        A("pool", "tensor_tensor", [tn, "const"], ["k_tm%d" % sub], out=k_tm[:, sub, :], in0=tmpn[:, ti, :],
                          in1=gk_rep[:].rearrange("p g d -> p (g d)"), op=ALU.mult)
                        A("pool", "tensor_copy", ["k_tm%d" % sub], ["k_bf%d" % sub], out=k_bf[:, sub, :], in_=k_tm[:, sub, :])
                else:
                    A("act", "activation", [pb], ["v_tm%d" % sub], out=v_tm[:, sub, :], in_=pv, func=AF.Copy)
                    if light:
                        A("dve", "tensor_scalar", ["v_tm%d" % sub, "const"], ["VAn"], out=VAn[:, sub, :, 0:128],
                          in0=v_tm[:, sub, :].rearrange("p (h e) -> p h e", h=4), scalar1=flagt[:, 0:1], scalar2=None, op0=ALU.mult)
                    else:
                        A("pool", "tensor_copy", ["v_tm%d" % sub], ["VAn"], out=VAn[:, sub, :, 0:128],
                          in_=v_tm[:, sub, :].rearrange("p (h e) -> p h e", h=4))
        for blk in (3, 4):
            slot = next_chunk()
            wres = "w%d" % slot
            w = wring[:, slot, :].rearrange("p (k n) -> p k n", k=8)
            for c in range(4):
                bk = 4 + c if blk == 3 else c
                for kt in range(8):
                    A("pe", "matmul", [wres] + XN, ["ps%d" % bk], out=ps[bk][:], lhsT=w[:, kt, c * 128:(c + 1) * 128],
                      rhs=xnT[:, kt, :], start=(kt == 0), stop=(kt == 7))
                if blk == 3:
                    if sample:
                        A("act", "activation", ["ps%d" % bk], ["xr%d" % c], out=xrS[:, c, :, 3:67],
                          in_=ps[bk][:].rearrange("p (s n) -> p s n", s=8), func=AF.Copy)
                    else:
                        A("act", "activation", ["ps%d" % bk], ["xr%d" % c], out=xrP[:, c, 3:515], in_=ps[bk][:], func=AF.Copy)
                else:
                    A("dve", "tensor_copy", ["ps%d" % bk], ["xg%d" % c], out=xgT[:, c, :], in_=ps[bk][:])
        for (src, sname, dst, dname, bk) in ((q_tm, "q_tm", QT, "QT", 6), (k_bf, "k_bf", KTn, "KTn", 7)):
            if light and sname == "q_tm":
                continue
            for h in range(4):
                for sub in range(4):
                    A("pe", "transpose", ["%s%d" % (sname, sub), "identb"], ["ps%d" % bk], out=psb[bk][:, sub * 128:(sub + 1) * 128],
                      in_=src[:, sub, h * 128:(h + 1) * 128], identity=identb[:])
                A("dve" if h % 2 == 0 else "act", "tensor_copy" if h % 2 == 0 else "copy", ["ps%d" % bk], [dname],
                  out=dst[:, h, :], in_=psb[bk][:, 0:512])

    def Oacc(idx):
        b = 4 + idx // 3
        c0 = (idx % 3) * 130
        return b, c0

    pt_ctr = [0]

    def attn_prefetch(t, extra_writes=()):
        nprefix = 4 * t
        ch_list = [(c0, min(16, nprefix - c0)) for c0 in range(0, nprefix, 16)]
        stt = dict(t=t, nprefix=nprefix, ch_list=ch_list, loads=[(h, ci) for h in range(4) for ci in range(len(ch_list))],
                   issued=0, slot_of={}, extra=list(extra_writes))
        issue_load(stt)
        issue_load(stt)
        return stt

    def issue_load(stt):
        k = stt["issued"]
        if k >= len(stt["loads"]):
            return
        h, ci = stt["loads"][k]
        c0, n = stt["ch_list"][ci]
        slot = kv_ctr[0] % 2
        kv_ctr[0] += 1
        stt["slot_of"][(h, ci)] = slot
        tl = sorted(set((c0 + j) // 4 for j in range(n)))
        S.dma("sp", [(KTc[:, slot, 0:n * 128], KT_s[h, :, c0 * 128:(c0 + n) * 128])], ["kts%d" % tt for tt in tl],
              ["ktc%d" % slot] + stt["extra"], "ktc%d" % slot)
        S.dma("sp", [(VAc[:, slot, 0:n, :], VA_s[h, :, c0:c0 + n, :])], ["vas%d" % tt for tt in tl],
              ["vac%d" % slot] + stt["extra"], "vac%d" % slot)
        stt["issued"] += 1

    def attn_prompt(stt, side_ops=None):
        nprefix = stt["nprefix"]
        ch_list = stt["ch_list"]
        slot_of = stt["slot_of"]
        blocks = []
        for h in range(4):
            for ci, (c0, n) in enumerate(ch_list):
                for j in range(n):
                    blocks.append(dict(h=h, load=(h, ci), j=j, q_lo=0, bias=("pl" if c0 + j == nprefix - 1 else None), local=None))
            for jb in range(4):
                blocks.append(dict(h=h, load=None, j=jb, q_lo=128 * jb, bias="loc", local=jb))

        def qk(bl):
            h = bl["h"]
            pbi = pt_ctr[0] % 2
            ptb = pt_ctr[0] % 3
            pt_ctr[0] += 1
            bl["pb"] = ptb
            q_lo = bl["q_lo"]
            if bl["load"] is not None:
                slot = slot_of[bl["load"]]
                bl["slot"] = slot
                kt_ap = KTc[:, slot, bl["j"] * 128:(bl["j"] + 1) * 128]
                kres = ["ktc%d" % slot]
            else:
                kt_ap = KTn[:, h, bl["j"] * 128:(bl["j"] + 1) * 128]
                kres = ["KTn"]
            for c in range(2):
                bk = pbi * 2 + c
                A("pe", "matmul", kres + ["QT"], ["ps%d" % bk], out=ps[bk][:, q_lo:512], lhsT=kt_ap[64 * c:64 * c + 64, :],
                  rhs=QT[64 * c:64 * c + 64, h, q_lo:512], start=True, stop=True)
            ptn = ["pt%d0" % ptb, "pt%d1" % ptb]
            A("act", "activation", ["ps%d" % (pbi * 2), "ps%d" % (pbi * 2 + 1)], ptn, out=PT[:, ptb, :, q_lo:512],
              in_=psall[:, pbi * 2:pbi * 2 + 2, q_lo:512], func=AF.Exp, scale=0.125)
            if bl["bias"] == "pl":
                A("dve", "tensor_tensor", ptn + ["ET"], ptn, out=PT[:, ptb, :, 0:128], in0=PT[:, ptb, :, 0:128],
                  in1=ET[:, h, 128:256].unsqueeze(1).to_broadcast([128, 2, 128]), op=ALU.mult)
            elif bl["bias"] == "loc":
                hi = min(512, q_lo + 256)
                A("dve", "tensor_tensor", ptn + ["ET"], ptn, out=PT[:, ptb, :, q_lo:hi], in0=PT[:, ptb, :, q_lo:hi],
                  in1=ET[:, h, 0:hi - q_lo].unsqueeze(1).to_broadcast([128, 2, hi - q_lo]), op=ALU.mult)

        started = set()

        def pv(bl, first):
            h = bl["h"]
            if first:
                started.clear()
            pbi = bl["pb"]
            if bl["load"] is not None:
                va_ap = VAc[:, bl["slot"], bl["j"], 0:129]
                vres = ["vac%d" % bl["slot"]]
            else:
                va_ap = VAn[:, bl["j"], h, 0:129]
                vres = ["VAn"]
            for qs in range(bl["q_lo"] // 128, 4):
                last = (bl["local"] == qs)
                for c in range(2):
                    b, c0 = Oacc(c * 4 + qs)
                    st_flag = first and (b not in started)
                    started.add(b)
                    A("pe", "matmul", vres + ["pt%d%d" % (pbi, c)], ["ps%d" % b], out=ps[b][:, c0:c0 + 129],
                      lhsT=PT[:, pbi, c, qs * 128:(qs + 1) * 128], rhs=va_ap, start=st_flag, stop=last, skip_group_check=True)

        def epilogue(h):
            for k in range(3):
                A("dve", "tensor_copy", ["ps%d" % (4 + k)], ["Osb%d" % k], out=Osb[:, k, :], in_=ps[4 + k][:, 0:390])
            for qs in range(4):
                b0, c0 = Oacc(qs)
                b1, c1 = Oacc(4 + qs)
                O0 = Osb[:, b0 - 4, c0:c0 + 129]
                O1 = Osb[:, b1 - 4, c1:c1 + 129]
                r0, r1 = "Osb%d" % (b0 - 4), "Osb%d" % (b1 - 4)
                A("dve", "reciprocal", [r0], ["rec0"], out=st[:, 32:33], in_=O0[:, 128:129])
                A("dve", "reciprocal", [r1], ["rec1"], out=st[:, 33:34], in_=O1[:, 128:129])
                A("dve", "tensor_scalar", [r1, "rec1", "neglam"], ["osb4"], out=osb[:, 4, :], in0=O1[:, 0:128],
                  scalar1=st[:, 33:34], scalar2=neglam[:, 0:1], op0=ALU.mult, op1=ALU.mult)
                A("dve", "scalar_tensor_tensor", [r0, "rec0", "osb4"], ["osb%d" % qs], out=osb[:, qs, :], in0=O0[:, 0:128],
                  scalar=st[:, 32:33], in1=osb[:, 4, :], op0=ALU.mult, op1=ALU.add)
                A("dve", "tensor_tensor", ["osb%d" % qs], ["osb4"], out=osb[:, 4, :], in0=osb[:, qs, :], in1=osb[:, qs, :], op=ALU.mult)
                A("dve", "tensor_reduce", ["osb4"], ["ssh"], out=st[:, 36 + qs:37 + qs], in_=osb[:, 4, :], axis=AX.X, op=ALU.add)
            A("dve", "tensor_scalar", ["ssh"], ["ssh"], out=st[:, 36:40], in0=st[:, 36:40], scalar1=1.0 / 128, scalar2=EPS, op0=ALU.mult, op1=ALU.add)
            A("pool", "tensor_tensor", ["ssh", "cm05"], ["ssh"], out=st[:, 36:40], in0=st[:, 36:40], in1=cm05[:, 0:1].to_broadcast([128, 4]), op=ALU.pow)
            for qs in range(4):
                A("dve", "scalar_tensor_tensor", ["osb%d" % qs, "ssh", "gsub"], ["o_tm%d" % qs], out=o_tm[:, qs, h * 128:(h + 1) * 128],
                  in0=osb[:, qs, :], scalar=st[:, 36 + qs:37 + qs], in1=gsub_rep[:], op0=ALU.mult, op1=ALU.mult)

        nb = len(blocks)
        side_rate = (len(side_ops) + nb - 9) // max(1, nb - 8) if side_ops else 0
        qk(blocks[0])
        if nb > 1:
            qk(blocks[1])
        for i, bl in enumerate(blocks):
            if i + 2 < nb:
                qk(blocks[i + 2])
            first = (i == 0) or (blocks[i - 1]["h"] != bl["h"])
            pv(bl, first)
            if side_ops:
                pump(side_ops, side_rate)
            if bl["load"] is not None and (i + 1 >= nb or blocks[i + 1]["load"] != bl["load"]):
                issue_load(stt)
            if i + 1 >= nb or blocks[i + 1]["h"] != bl["h"]:
                epilogue(bl["h"])

    kv_ctr = [0]

    def attn_sample(side_ops=None):
        nch = PKB // 4
        seq_ch = [(s_, ch_) for s_ in range(NSR) for ch_ in range(nch)]

        def load_chunk(k):
            if k >= len(seq_ch):
                return
            s_, ch_ = seq_ch[k]
            S.dma("sp", [(kst, ck_d[s_, ch_ * 512:(ch_ + 1) * 512, :].rearrange("(k p) d -> p k d", p=128))], [],
                  ["ktc0", "ktc1"], "kst")
            S.dma("sp", [(vst, cv_d[s_, ch_ * 512:(ch_ + 1) * 512, :].rearrange("(k p) d -> p k d", p=128))], [],
                  ["vac0", "vac1"], "vst")

        side_rate = (len(side_ops) + len(seq_ch) * 8 - 9) // (len(seq_ch) * 8 - 8) if side_ops else 0
        load_chunk(0)
        for s in range(NSR):
            pair = s // 2
            started_s = set()
            qc0 = pair * 128
            R0 = (s % 2) * 64
            for ch in range(nch):
                A("pool", "tensor_copy", ["ktc0", "ktc1"], ["kbf"], out=kbf, in_=kst)
                A("dve", "tensor_copy", ["vac0", "vac1"], ["VAq"], out=VAq[:, :, :, 0:128],
                  in_=vst.rearrange("p k (h e) -> p k h e", h=4))
                load_chunk(s * nch + ch + 1)
                pend_pv = [None]

                def flush_pv():
                    if pend_pv[0] is not None:
                        h_, c_, gi_, bk_ = pend_pv[0]
                        pend_pv[0] = None
                        ptn_ = "pt%d%d" % (gi_ // 2, gi_ % 2)
                        b, c0 = Oacc(h_ * 2 + c_)
                        for kb in range(4):
                            st_flag = (ch == 0 and kb == 0 and b not in started_s)
                            started_s.add(b)
                            A("pe", "matmul", ["VAq", ptn_], ["ps%d" % b], out=ps[b][:, c0:c0 + 129],
                              lhsT=PT[:, gi_ // 2, gi_ % 2, kb * 128:(kb + 1) * 128], rhs=VAq[:, kb, h_, 0:129],
                              start=st_flag, stop=False, skip_group_check=True)

                for h in range(4):
                    for kb in range(4):
                        A("pe", "transpose", ["kbf", "identb"], ["ps7"], out=psb[7][:, kb * 128:(kb + 1) * 128],
                          in_=kbf[:, kb, h * 128:(h + 1) * 128], identity=identb[:])
                    A("dve", "tensor_copy", ["ps7"], ["KTq%d" % h], out=KTq[:, h, :], in_=psb[7][:, 0:512])
                    for c in range(2):
                        gi = pt_ctr[0] % 6
                        bk = pt_ctr[0] % 4
                        pt_ctr[0] += 1
                        ptn = "pt%d%d" % (gi // 2, gi % 2)
                        for kb in range(4):
                            A("pe", "matmul", ["KTq%d" % h, "QT"], ["ps%d" % bk], out=ps[bk][:, kb * 128:(kb + 1) * 128],
                              lhsT=KTq[64 * c:64 * c + 64, h, kb * 128:(kb + 1) * 128], rhs=QT[64 * c:64 * c + 64, h, qc0:qc0 + 128],
                              start=True, stop=True)
                        A("act", "activation", ["ps%d" % bk], [ptn], out=PT[:, gi // 2, gi % 2, :], in_=ps[bk][:], func=AF.Exp, scale=0.125)
                        if ch == nch - 1:
                            a0 = 3 * 128 + R0
                            A("dve", "tensor_tensor", [ptn, "ET"], [ptn], out=PT[:, gi // 2, gi % 2, a0:a0 + 64],
                              in0=PT[:, gi // 2, gi % 2, a0:a0 + 64], in1=ET[:, h, 128:192], op=ALU.mult)
                        flush_pv()
                        pend_pv[0] = (h, c, gi, bk)
                        if side_ops:
                            pump(side_ops, side_rate)
                flush_pv()
            for h in range(4):
                for c in range(2):
                    gi = pt_ctr[0] % 6
                    bk = pt_ctr[0] % 4
                    pt_ctr[0] += 1
                    ptn = "pt%d%d" % (gi // 2, gi % 2)
                    A("pe", "matmul", ["KTn", "QT"], ["ps%d" % bk], out=ps[bk][:, 0:128], lhsT=KTn[64 * c:64 * c + 64, h, qc0:qc0 + 128],
                      rhs=QT[64 * c:64 * c + 64, h, qc0:qc0 + 128], start=True, stop=True)
                    A("act", "activation", ["ps%d" % bk], [ptn], out=PT[:, gi // 2, gi % 2, 0:128], in_=ps[bk][:, 0:128], func=AF.Exp, scale=0.125)
                    A("dve", "tensor_tensor", [ptn, "ET"], [ptn], out=PT[:, gi // 2, gi % 2, 0:128], in0=PT[:, gi // 2, gi % 2, 0:128],
                      in1=ETp[:, h, :], op=ALU.mult)
                    b, c0 = Oacc(h * 2 + c)
                    A("pe", "matmul", ["VAn", ptn], ["ps%d" % b], out=ps[b][:, c0:c0 + 129], lhsT=PT[:, gi // 2, gi % 2, 0:128],
                      rhs=VAn[:, pair, h, 0:129], start=False, stop=True, skip_group_check=True)
            R = slice(R0, R0 + 64)
            for h in range(4):
                b0, c0 = Oacc(h * 2)
                b1, c1 = Oacc(h * 2 + 1)
                O0 = ps[b0][R, c0:c0 + 129]
                O1 = ps[b1][R, c1:c1 + 129]
                A("dve", "reciprocal", ["ps%d" % b0], ["rec"], out=st[R, 32:33], in_=O0[:, 128:129])
                A("dve", "reciprocal", ["ps%d" % b1], ["rec"], out=st[R, 33:34], in_=O1[:, 128:129])
                A("dve", "tensor_scalar", ["ps%d" % b1, "rec", "neglam"], ["osb4"], out=osb[R, 4, :], in0=O1[:, 0:128],
                  scalar1=st[R, 33:34], scalar2=neglam[R, 0:1], op0=ALU.mult, op1=ALU.mult)
                A("dve", "scalar_tensor_tensor", ["ps%d" % b0, "rec", "osb4"], ["osb%d" % h], out=osb[R, h, :], in0=O0[:, 0:128],
                  scalar=st[R, 32:33], in1=osb[R, 4, :], op0=ALU.mult, op1=ALU.add)
                A("act", "activation", ["osb%d" % h], ["sg0", "ssh"], out=sg[R, 0, 0:128], in_=osb[R, h, :], func=AF.Square,
                  accum_out=st[R, 36 + h:37 + h])
            rsqrt_small(st[R, 36:40], 1.0 / 128, "ssh", rows=R)
            for h in range(4):
                A("dve", "scalar_tensor_tensor", ["osb%d" % h, "ssh", "gsub"], ["o_tm%d" % pair], out=o_tm[R, pair, h * 128:(h + 1) * 128],
                  in0=osb[R, h, :], scalar=st[R, 36 + h:37 + h], in1=gsub_rep[R, :], op0=ALU.mult, op1=ALU.mult)

    def o_transposes():
        for h in range(4):
            bk = 6 + (h % 2)
            for sub in range(4):
                A("pe", "transpose", ["o_tm%d" % sub, "identb"], ["ps%d" % bk], out=psb[bk][:, sub * 128:(sub + 1) * 128],
                  in_=o_tm[:, sub, h * 128:(h + 1) * 128], identity=identb[:])
            A("dve" if h % 2 == 0 else "act", "tensor_copy" if h % 2 == 0 else "copy", ["ps%d" % bk], [CAT[h]],
              out=catT[:, h, :], in_=psb[bk][:, 0:512])

    def lru_build(sample, light, lt, ltn, gb, nb):
        ops = []
        ops_c = [[] for _ in range(4)]
        cur = [ops]
        two_sets = isinstance(lt, tuple)
        lt_sets, ltn_sets = (lt, ltn) if two_sets else ((lt, lt), (ltn, ltn))

        def flat(lst):
            out = []
            for x in lst:
                if isinstance(x, (list, tuple)):
                    out += list(x)
                else:
                    out.append(x)
            return out

        def A(eng, method, reads, writes, **kw):
            cur[0].append((eng, method, flat(reads), flat(writes), kw))
        nsq, tl = (8, 64) if sample else (1, 512)

        def v3(ap):
            return ap.rearrange("p (s n) -> p s n", s=nsq)

        for c in range(4):
            xr = "xr%d" % c
            cur[0] = ops_c[c]
            lt, ltn = lt_sets[c % 2], ltn_sets[c % 2]

            def xp(j):
                if sample:
                    return xrS[:, c, :, j:j + 64]
                return xrP[:, c, j:j + 512].rearrange("p (s n) -> p s n", s=1)
            A("dve", "tensor_scalar", [xr, "const"], [ltn[0]], out=v3(lt[0]), in0=xp(3), scalar1=cw[:, 3, c:c + 1], scalar2=vecs[:, 0, c:c + 1],
              op0=ALU.mult, op1=ALU.add)
            for j in range(3):
                A("dve", "scalar_tensor_tensor", [xr, "const", ltn[0]], [ltn[0]], out=v3(lt[0]), in0=xp(j), scalar=cw[:, j, c:c + 1],
                  in1=v3(lt[0]), op0=ALU.mult, op1=ALU.add)
            ba_, bx_ = gb[c % len(gb)]
            A("pe", "matmul", [ltn[0], "Wbd"], ["ps%d" % ba_], out=ps[ba_][:], lhsT=Wbd[:, 0, c, :], rhs=lt[0], start=True, stop=True)
            A("act", "activation", ["ps%d" % ba_, "const"], [ltn[1]], out=lt[1], in_=ps[ba_][:], func=AF.Exp, bias=vecs[:, 1, c:c + 1], scale=-1.0)
            A("pe", "matmul", [ltn[0], "Wbd"], ["ps%d" % bx_], out=ps[bx_][:], lhsT=Wbd[:, 1, c, :], rhs=lt[0], start=True, stop=True)
            A("act", "activation", ["ps%d" % bx_, "const"], [ltn[2]], out=lt[2], in_=ps[bx_][:], func=AF.Exp, bias=vecs[:, 2, c:c + 1], scale=-1.0)
            for k_ in (1, 2):
                A("act", "activation", [ltn[k_]], [ltn[k_]], out=lt[k_], in_=lt[k_], func=AF.Ln, bias=1.0)
                A("act", "activation", [ltn[k_]], [ltn[k_]], out=lt[k_], in_=lt[k_], func=AF.Exp, scale=-1.0)
            if not light:
                A("pool", "tensor_tensor", ["xg%d" % c], [ltn[4]], out=lt[4], in0=xgT[:, c, :], in1=xgT[:, c, :], op=ALU.mult)
                A("dve", "tensor_scalar", [ltn[4]], [ltn[4]], out=lt[4], in0=lt[4], scalar1=0.044715 * 1.5957691216057308,
                  scalar2=1.5957691216057308, op0=ALU.mult, op1=ALU.add)
                A("pool", "tensor_tensor", [ltn[4], "xg%d" % c], [ltn[4]], out=lt[4], in0=lt[4], in1=xgT[:, c, :], op=ALU.mult)
                A("act", "activation", [ltn[4]], [ltn[4]], out=lt[4], in_=lt[4], func=AF.Exp, scale=-1.0)
                A("act", "activation", [ltn[4]], [ltn[4]], out=lt[4], in_=lt[4], func=AF.Ln, bias=1.0)
                A("act", "activation", [ltn[4]], [ltn[4]], out=lt[4], in_=lt[4], func=AF.Exp, scale=-1.0)
            A("act", "activation", [ltn[1], "nsp"], [ltn[1]], out=lt[1], in_=lt[1], func=AF.Exp, scale=vecs[:, 5, c:c + 1])
            A("dve", "tensor_tensor", [ltn[1]], [ltn[3]], out=lt[3], in0=lt[1], in1=lt[1], op=ALU.mult)
            A("dve", "tensor_scalar", [ltn[3]], [ltn[3]], out=lt[3], in0=lt[3], scalar1=-1.0, scalar2=1.0, op0=ALU.mult, op1=ALU.add)
            A("act", "activation", [ltn[3]], [ltn[3]], out=lt[3], in_=lt[3], func=AF.Ln)
            A("act", "activation", [ltn[3]], [ltn[3]], out=lt[3], in_=lt[3], func=AF.Exp, scale=0.5)
            A("pool", "tensor_tensor", [ltn[2], ltn[3]], [ltn[2]], out=lt[2], in0=lt[2], in1=lt[3], op=ALU.mult)
            A("pool", "tensor_tensor", [ltn[2], ltn[0]], [ltn[2]], out=lt[2], in0=lt[2], in1=lt[0], op=ALU.mult)
            if sample:
                for s in range(8):
                    A("dve", "tensor_tensor_scan", [ltn[1], ltn[2], "h0T"], [ltn[5]], out=lt[5][:, s * 64:(s + 1) * 64],
                      data0=lt[1][:, s * 64:(s + 1) * 64], data1=lt[2][:, s * 64:(s + 1) * 64], initial=h0T[:, c, s:s + 1],
                      op0=ALU.mult, op1=ALU.add)
                A("dve", "tensor_copy", [ltn[5]], ["hlast"], out=hlast[:, c, :], in_=v3(lt[5])[:, :, 63])
                A("pool", "tensor_copy", [xr], ["tails"], out=tails[:, c, :].rearrange("p (s j) -> p s j", s=8), in_=xrS[:, c, :, 64:67])
            else:
                A("dve", "tensor_tensor_scan", [ltn[1], ltn[2], "hcar"], [ltn[5]], out=lt[5], data0=lt[1], data1=lt[2],
                  initial=hcar[:, c:c + 1], op0=ALU.mult, op1=ALU.add)
                A("dve", "tensor_copy", [ltn[5]], ["hcar"], out=hcar[:, c:c + 1], in_=lt[5][:, 511:512])
                A("pool", "tensor_copy", [xr], [xr], out=xrP[:, c, 0:3], in_=xrP[:, c, 512:515])
            if not light:
                A("dve", "tensor_tensor", [ltn[5], "xg%d" % c], [ltn[3]], out=lt[3], in0=lt[5], in1=xgT[:, c, :], op=ALU.mult)
                A("dve", "tensor_tensor", [ltn[3], ltn[4]], ["hg%d" % c], out=hg[:, c, :], in0=lt[3], in1=lt[4], op=ALU.mult)
        cur[0] = ops
        lt, ltn = lt_sets[0], ltn_sets[0]
        if two_sets:
            for c0 in (0, 2):
                a_, b_ = ops_c[c0], ops_c[c0 + 1]
                for k_ in range(max(len(a_), len(b_))):
                    if k_ < len(a_):
                        ops.append(a_[k_])
                    if k_ < len(b_):
                        ops.append(b_[k_])
        else:
            for c in range(4):
                ops.extend(ops_c[c])
        if light:
            return ops
        for c in range(4):
            A("act", "activation", ["hg%d" % c], [ltn[c]], out=lt[c], in_=hg[:, c, :], func=AF.Square)
        for c in range(4):
            A("pe", "matmul", ["lt%d" % c, "onesf"], ["ps%d" % nb], out=ps[nb][:], lhsT=onesf[:], rhs=lt[c], start=(c == 0), stop=(c == 3))
        A("dve", "tensor_scalar", ["ps%d" % nb], ["rstdL"], out=rstdL, in0=ps[nb][:], scalar1=1.0 / 512, scalar2=EPS, op0=ALU.mult, op1=ALU.add)
        A("act", "activation", ["rstdL"], ["rstdL"], out=rstdL, in_=rstdL, func=AF.Ln)
        A("act", "activation", ["rstdL"], ["rstdL"], out=rstdL, in_=rstdL, func=AF.Exp, scale=-0.5)
        for c in range(4):
            A("dve", "scalar_tensor_tensor", ["hg%d" % c, "rstdL", "const"], [CAT[4 + c]], out=catT[:, 4 + c, :], in0=hg[:, c, :],
              scalar=vecs[:, 4, c:c + 1], in1=rstdL, op0=ALU.mult, op1=ALU.mult)
        return ops

    def pump(ops, n):
        def emit1():
            e_, m_, r_, w_, kw_ = ops.pop(0)
            A(e_, m_, r_, w_, **kw_)
            return e_
        while n > 0 and ops:
            e_ = emit1()
            n -= 1
            if e_ == "pe":
                while ops and ops[0][0] == "pe" and ops[0][4].get("start") is False:
                    emit1()
                if ops:
                    emit1()

    def w_out():
        for half in range(2):
            slot = next_chunk()
            wres = "w%d" % slot
            w = wring[:, slot, :].rearrange("p (k n) -> p k n", k=8)
            banks = [0, 1, 2, 3] if half == 0 else [4, 5, 6, 7]
            for sub in range(4):
                for kt in range(8):
                    A("pe", "matmul", [wres, CAT[kt]], ["ps%d" % banks[sub]], out=ps[banks[sub]][:],
                      lhsT=catT[:, kt, sub * 128:(sub + 1) * 128], rhs=w[:, kt, :], start=(kt == 0), stop=(kt == 7))
            for sub in range(4):
                A("dve", "tensor_tensor", ["ps%d" % banks[sub], XT[sub]], [XT[sub]], out=xt[:, sub, half * 512:(half + 1) * 512],
                  in0=ps[banks[sub]][:], in1=xt[:, sub, half * 512:(half + 1) * 512], op=ALU.add)

    def tm_view(d_ap, r0):
        return d_ap[r0:r0 + 512, :].rearrange("(s p) d -> p s d", p=128)

    tiles = [("p", t) for t in range(NT)] + ([("s", 0)] if do_sample else [])
    for kind, t in tiles:
        sample = kind == "s"
        light = (not sample) and t < NL
        t0 = t * 512
        o0 = (t - NL) * 512
        xsrc = tm_view(xs_d if sample else x_d, 0 if sample else t0)
        for sub in range(4):
            S.dma("sp", [(xt[:, sub, :], xsrc[:, sub, :])], [], [XT[sub]], "ld_x%d" % sub)
        att_state = None
        if (not sample) and (not light) and not (half and t == NL):
            att_state = attn_prefetch(t)
        if sample:
            S.dma("sp", [(sm_tm[0:24, 0, :], sconv_d), (sm_tm[0:8, 1, :], slru_d)], [], ["tmpn0", "tmpn1"], "ld_sm")
            for c in range(4):
                A("pe", "transpose", ["tmpn0", "tmpn1", "const"], ["ps6"], out=ps[6][:, c * 32:c * 32 + 24], in_=sm_tm[0:24, 0, c * 128:(c + 1) * 128],
                  identity=identf[0:24, 0:24])
                A("pe", "transpose", ["tmpn0", "tmpn1", "const"], ["ps6"], out=ps[6][:, 128 + c * 8:128 + c * 8 + 8], in_=sm_tm[0:8, 1, c * 128:(c + 1) * 128],
                  identity=identf[0:8, 0:8])
            for c in range(4):
                A("dve", "tensor_copy", ["ps6"], ["xr%d" % c], out=xrS[:, c, :, 0:3],
                  in_=ps[6][:, c * 32:c * 32 + 24].rearrange("p (s j) -> p s j", s=8))
            A("dve", "tensor_copy", ["ps6"], ["h0T"], out=h0T[:], in_=ps[6][:, 128:160].rearrange("p (c s) -> p c s", c=4))
        if half and (not sample) and t == 0:
            A("pool", "tensor_copy", ["const"], ["VAn"], out=VAn[:, :, :, 128:130], in_=flagt[:, 0:1].unsqueeze(1).unsqueeze(1).to_broadcast([128, 4, 4, 2]))
        if half and (not sample) and t == NL:
            A("pool", "memset", [], ["VAn"], ap=VAn[:, :, :, 128:130], constant=1.0)
        mark("%s%d norm1" % (kind, t))
        barrier("dve")
        norm_T(0)
        mark("%s%d ffn1" % (kind, t))
        ffn()
        if half and (not sample) and t == NL:
            A("dve", "tensor_scalar", ["hcar", "const"], ["hcar"], out=hcar[:], in0=hcar[:], scalar1=flagt[:, 0:1], scalar2=None, op0=ALU.mult)
            for c in range(4):
                A("dve", "tensor_scalar", ["xr%d" % c, "const"], ["xr%d" % c], out=xrP[:, c, 0:3], in0=xrP[:, c, 0:3], scalar1=flagt[:, 0:1],
                  scalar2=None, op0=ALU.mult)
            att_state = attn_prefetch(t, extra_writes=LA)
        mark("%s%d norm2" % (kind, t))
        norm_T(1)
        mark("%s%d w_in" % (kind, t))
        w_in(sample, light)
        if sample:
            S.dma("pool", [(tm_view(nks_d, 0), k_tm[:])], ["k_tm%d" % i for i in range(4)], ["nk_out"], "st_k")
            S.dma("pool", [(tm_view(nvs_d, 0), v_tm[:])], ["v_tm%d" % i for i in range(4)], ["nv_out"], "st_v")
        else:
            if not light:
                S.dma("pool", [(tm_view(nk_d, o0), k_tm[:])], ["k_tm%d" % i for i in range(4)], ["nk_out"], "st_k")
                S.dma("pool", [(tm_view(nv_d, o0), v_tm[:])], ["v_tm%d" % i for i in range(4)], ["nv_out"], "st_v")
            if t + 1 < NT:
                S.dma("pool", [(KT_s.rearrange("h p s -> p h s")[:, :, t0:t0 + 512], KTn[:])], ["KTn"], ["kts%d" % t], "st_kt")
                S.dma("pool", [(VA_s[h, :, 4 * t:4 * t + 4, :], VAn[:, :, h, :]) for h in range(4)], ["VAn"], ["vas%d" % t], "st_va")
        if light:
            mark("%s%d lru" % (kind, t))
            pt_f = [PT[:, k, :, :].rearrange("p a b -> p (a b)").bitcast(F32) for k in range(3)]
            q_f = q_tm[:].rearrange("p a b -> p (a b)").bitcast(F32)
            QT_f = QT[:].rearrange("p a b -> p (a b)").bitcast(F32)
            lb = pt_f + [q_f[:, 0:512], q_f[:, 512:1024], QT_f[:, 0:512]]
            LB = [["pt00", "pt01"], ["pt10", "pt11"], ["pt20", "pt21"], ["q_tm0", "q_tm1"], ["q_tm2", "q_tm3"], ["QT"]]
            pending["ops"] = lru_build(False, True, (la, lb), (LA, LB), [(4, 5), (6, 7)], 7)
            mark("%s%d end" % (kind, t))
            continue
        for e_ in ("act", "dve", "pool"):
            barrier(e_)
        mark("%s%d attn" % (kind, t))
        if sample:
            A("pool", "memset", [], ["VAq", "cvA", "cvB"], ap=VAq[:, :, :, 128:130], constant=1.0)
            lops = lru_build(True, False, lt, LTN, [(7, 7)], 7)
            attn_sample(lops)
            mark("%s%d lru" % (kind, t))
            pump(lops, len(lops))
            mark("%s%d otr" % (kind, t))
            o_transposes()
        else:
            lops = lru_build(False, False, lt, LTN, [(7, 7)], 7)
            attn_prompt(att_state, lops)
            mark("%s%d lru" % (kind, t))
            pump(lops, len(lops))
            mark("%s%d otr" % (kind, t))
            o_transposes()
        mark("%s%d w_out" % (kind, t))
        w_out()
        barrier("dve")
        mark("%s%d norm3" % (kind, t))
        norm_T(2)
        mark("%s%d ffn2" % (kind, t))
        ffn()
        mark("%s%d end" % (kind, t))
        ydst = tm_view(ys_d if sample else y_d, 0 if sample else o0)
        for sub in range(4):
            S.dma("pool", [(ydst[:, sub, :], xt[:, sub, :])], [XT[sub]], ["y_out%d" % sub], "st_y%d" % sub)
        if sample:
            for c in range(4):
                A("pe", "transpose", ["tails", "const"], ["ps6"], out=ps[6][0:24, c * 128:(c + 1) * 128], in_=tails[:, c, :], identity=identf[:])
                A("pe", "transpose", ["hlast", "const"], ["ps7"], out=ps[7][0:8, c * 128:(c + 1) * 128], in_=hlast[:, c, :], identity=identf[:])
            A("dve", "tensor_copy", ["ps6"], ["tmpn0", "tmpn1"], out=sm_tm[0:24, 0, :], in_=ps[6][0:24, :])
            A("dve", "tensor_copy", ["ps7"], ["tmpn0", "tmpn1"], out=sm_tm[0:8, 1, :], in_=ps[7][0:8, :])
            S.dma("pool", [(nconvs_d, sm_tm[0:24, 0, :]), (nlrus_d, sm_tm[0:8, 1, :])], ["tmpn0", "tmpn1"], ["st_out"], "st_sm")
        elif t == NT - 1:
            for c in range(4):
                A("pe", "transpose", ["xr%d" % c, "const"], ["ps6"], out=ps[6][0:3, c * 128:(c + 1) * 128], in_=xrP[:, c, 0:3], identity=identf[:])
                A("pe", "transpose", ["hcar", "const"], ["ps7"], out=ps[7][0:1, c * 128:(c + 1) * 128], in_=hcar[:, c:c + 1], identity=identf[:])
            A("dve", "tensor_copy", ["ps6"], ["tmpn0", "tmpn1"], out=sm_tm[0:3, 0, :], in_=ps[6][0:3, :])
            A("dve", "tensor_copy", ["ps7"], ["tmpn0", "tmpn1"], out=sm_tm[0:1, 1, :], in_=ps[7][0:1, :])
            S.dma("pool", [(nconv_d, sm_tm[0:3, 0, :]), (nlru_d, sm_tm[0:1, 1, :])], ["tmpn0", "tmpn1"], ["st_out"], "st_sm")
    S.wait_all_dma("sp")
    S.emit()
    es.close()
    return nc


_NC_CACHE = {}


def _get_nc(SEQ, PAST):
    key = (SEQ, PAST)
    if key not in _NC_CACHE:
        _NC_CACHE[key] = build(SEQ, PAST)
    return _NC_CACHE[key]


def make_in_maps(inputs, n_prompt):
    consts = host_consts()
    wmap = {}
    for n, s in W_SPECS:
        a = np.asarray(inputs[n], dtype=np.float32)
        if n != "rel_bias":
            a = a[0]
        wmap[n] = np.ascontiguousarray(a.reshape(s))
    SEQ = inputs["x_prompt"].shape[1]
    H = SEQ // 2
    maps = []
    for c in range(2 * n_prompt):
        b = c // 2
        m = dict(wmap)
        m.update(consts)
        xb = np.asarray(inputs["x_prompt"][b], dtype=np.float32)
        if c % 2 == 0:
            m["x"] = np.ascontiguousarray(np.concatenate([np.zeros((H, 1024), np.float32), xb[:H]], 0))
            m["flag"] = np.zeros((128, 1), np.float32)
        else:
            m["x"] = np.ascontiguousarray(xb)
            m["flag"] = np.ones((128, 1), np.float32)
        sl = slice(4 * c, 4 * c + 4)
        z = lambda *shape: np.zeros(shape, np.float32)
        m["xs"] = np.ascontiguousarray(np.concatenate([inputs["x_sample"][sl].reshape(256, 1024), z(256, 1024)], 0))
        past = inputs["cache_k"].shape[2]
        m["ck"] = np.ascontiguousarray(inputs["cache_k"][0, sl].reshape(4, past, 512))
        m["cv"] = np.ascontiguousarray(inputs["cache_v"][0, sl].reshape(4, past, 512))
        m["slru"] = np.ascontiguousarray(np.concatenate([inputs["state_lru"][0, sl], z(4, 512)], 0))
        m["sconv"] = np.ascontiguousarray(np.concatenate([inputs["state_conv"][0, sl].reshape(12, 512), z(12, 512)], 0))
        maps.append(m)
    return maps


def assemble(results, n_prompt, SEQ):
    B = n_prompt
    r = results
    H = SEQ // 2
    cat2 = lambda name, b: np.concatenate([r[2 * b][name], r[2 * b + 1][name]], 0)
    yp = np.stack([cat2("y", b) for b in range(B)], 0)
    NC_ = 2 * B
    ys = np.concatenate([r[c]["ys"].reshape(8, 64, 1024)[:4] for c in range(NC_)], 0)
    nk = np.stack([cat2("nk", b).reshape(SEQ, 4, 2, 64) for b in range(B)], 0)[None]
    nv = np.stack([cat2("nv", b).reshape(SEQ, 4, 128) for b in range(B)], 0)[None]
    nl = np.stack([r[2 * b + 1]["nlru"].reshape(512) for b in range(B)], 0)[None]
    ncv = np.stack([r[2 * b + 1]["nconv"].reshape(3, 512) for b in range(B)], 0)[None]
    nks = np.concatenate([r[c]["nks"].reshape(8, 64, 4, 2, 64)[:4] for c in range(NC_)], 0)[None]
    nvs = np.concatenate([r[c]["nvs"].reshape(8, 64, 4, 128)[:4] for c in range(NC_)], 0)[None]
    nls = np.concatenate([r[c]["nlrus"].reshape(8, 512)[:4] for c in range(NC_)], 0)[None]
    ncs = np.concatenate([r[c]["nconvs"].reshape(8, 3, 512)[:4] for c in range(NC_)], 0)[None]
    return tuple(np.ascontiguousarray(a, dtype=np.float32) for a in (yp, ys, nk, nv, nl, ncv, nks, nvs, nls, ncs))


def kernel(**inputs):
    inputs = {k: np.asarray(v) for k, v in inputs.items()}
    B, SEQ = inputs["x_prompt"].shape[0], inputs["x_prompt"].shape[1]
    PAST = inputs["cache_k"].shape[2]
    assert B == 4 and inputs["x_sample"].shape[0] == 32
    nc = _get_nc(SEQ, PAST)
    maps = make_in_maps(inputs, 4)
    res = run_bass_kernel_spmd(nc, maps, core_ids=list(range(8)))
    return assemble(res.results, 4, SEQ)
```

```python
import math
from contextlib import ExitStack
import numpy as np
import concourse.bass as bass
import concourse.mybir as mybir
from concourse.bass_utils import run_bass_kernel_spmd

F32 = mybir.dt.float32
BF16 = mybir.dt.bfloat16
AF = mybir.ActivationFunctionType
ALU = mybir.AluOpType
AX = mybir.AxisListType

ENGS = ("pe", "act", "dve", "pool", "sp")
INORDER_SAFE = ("pe", "sp")
EPS = 1e-6
NS = 3


class Op:
    __slots__ = ("eng", "fn", "waits", "inc_needed", "idx", "semval", "dma_inc")

    def __init__(self, eng, fn, waits):
        self.eng = eng
        self.fn = fn
        self.waits = waits
        self.inc_needed = False
        self.idx = -1
        self.semval = 0
        self.dma_inc = None


class Sched:
    def __init__(self, nc, same_engine_sync=True):
        self.nc = nc
        self.streams = {e: [] for e in ENGS}
        self.res = {}
        self.seen = {e: {} for e in ENGS}
        self.dma_tot = {}
        self.same_engine_sync = same_engine_sync

    def _need(self, eng, ev, waits):
        if ev[0] == "op":
            op = ev[1]
            if op.eng == eng and (eng in INORDER_SAFE or not self.same_engine_sync):
                return
            key = ("op", op.eng)
            if self.seen[eng].get(key, -1) >= op.idx:
                return
            self.seen[eng][key] = op.idx
            op.inc_needed = True
            waits.append(ev)
        else:
            key = ("dma", ev[1])
            if self.seen[eng].get(key, -1) >= ev[2]:
                return
            self.seen[eng][key] = ev[2]
            waits.append(ev)

    def _collect(self, eng, reads, writes):
        waits = []
        for r in reads:
            st = self.res.setdefault(r, [[], []])
            for ev in st[0]:
                self._need(eng, ev, waits)
        for w in writes:
            st = self.res.setdefault(w, [[], []])
            for ev in st[0]:
                self._need(eng, ev, waits)
            for ev in st[1]:
                self._need(eng, ev, waits)
        return waits

    def _commit(self, ev, reads, writes):
        for r in reads:
            self.res[r][1].append(ev)
        for w in writes:
            self.res[w] = [[ev], []]

    def op(self, eng, fn, reads=(), writes=()):
        waits = self._collect(eng, reads, writes)
        o = Op(eng, fn, waits)
        o.idx = len(self.streams[eng])
        self.streams[eng].append(o)
        self._commit(("op", o), reads, writes)
        return o

    def dma(self, q, pairs, reads, writes, semkey):
        waits = self._collect(q, reads, writes)
        tot = self.dma_tot.get(semkey, 0)
        first = True
        for pr in pairs:
            out_ap, in_ap = pr[0], pr[1]
            kw = pr[2] if len(pr) > 2 else {}
            tot += 16
            o = Op(q, (lambda e, a=out_ap, b=in_ap, k=kw: e.dma_start(out=a, in_=b, **k)), waits if first else [])
            first = False
            o.idx = len(self.streams[q])
            o.dma_inc = semkey
            self.streams[q].append(o)
        self.dma_tot[semkey] = tot
        self._commit(("dma", semkey, tot), reads, writes)

    def wait_all_dma(self, eng="sp"):
        waits = [("dma", k, v) for k, v in self.dma_tot.items()]
        o = Op(eng, None, waits)
        o.idx = len(self.streams[eng])
        self.streams[eng].append(o)

    def emit(self):
        nc = self.nc
        with ExitStack() as es:
            esem = {e: es.enter_context(nc.semaphore("s_" + e)) for e in ENGS}
            dsem = {k: es.enter_context(nc.semaphore("d_" + str(k))) for k in self.dma_tot}
            for e, ops in self.streams.items():
                c = 0
                for o in ops:
                    if o.inc_needed:
                        c += 1
                        o.semval = c
            block = es.enter_context(nc.Block())

            def run(eng_name):
                def body(e):
                    for o in self.streams[eng_name]:
                        for ev in o.waits:
                            if ev[0] == "op":
                                e.wait_ge(esem[ev[1].eng], ev[1].semval)
                            else:
                                e.wait_ge(dsem[ev[1]], ev[2])
                        if o.fn is None:
                            continue
                        ins = o.fn(e)
                        if o.dma_inc is not None:
                            ins.then_inc(dsem[o.dma_inc], 16)
                        elif o.inc_needed:
                            ins.then_inc(esem[eng_name], 1)
                return body

            block.tensor(run("pe"))
            block.scalar(run("act"))
            block.vector(run("dve"))
            block.gpsimd(run("pool"))
            block.sync(run("sp"))
        return nc


W_SPECS = [
    ("rel_bias", (32, 4)), ("norm_ffn1", (1, 1024)), ("ffn1_gate", (1024, 2816)), ("ffn1_up", (1024, 2816)),
    ("ffn1_down", (2816, 1024)), ("norm_mix", (1, 1024)), ("w_in", (1024, 2560)), ("q_norm", (1, 64)),
    ("k_norm", (1, 64)), ("lambda_q1", (1, 64)), ("lambda_k1", (1, 64)), ("lambda_q2", (1, 64)),
    ("lambda_k2", (1, 64)), ("subln", (1, 128)), ("conv_w", (4, 512)), ("conv_b", (1, 512)),
    ("gate_a_w", (8, 64, 64)), ("gate_a_b", (1, 512)), ("gate_x_w", (8, 64, 64)), ("gate_x_b", (1, 512)),
    ("lru_L", (1, 512)), ("lru_out_norm", (1, 512)), ("w_out", (1024, 1024)), ("norm_ffn2", (1, 1024)),
    ("ffn2_gate", (1024, 2816)), ("ffn2_up", (1024, 2816)), ("ffn2_down", (2816, 1024)),
]


def t5_bucket_np(rel):
    n = 16
    max_exact = 8
    ret = np.where(rel > 0, n, 0)
    rel = np.abs(rel)
    relf = np.maximum(rel, 1).astype(np.float32)
    large = max_exact + (np.log(relf / np.float32(max_exact)) / np.float32(math.log(128 / max_exact))
                         * np.float32(n - max_exact)).astype(np.int32)
    large = np.minimum(large, n - 1)
    return ret + np.where(rel < max_exact, rel, large)


def host_consts():
    ident = np.eye(128, dtype=np.float32)
    jrev = np.ascontiguousarray(ident[::-1])
    m = np.arange(384)
    bk = t5_bucket_np(m - 255)
    oh = np.zeros((32, 384), np.float32)
    oh[bk, m] = 1.0
    oh[15, :] -= 1.0
    i = np.arange(128)[:, None]
    r = np.arange(256)[None, :]
    vis = np.where((r < 128) & ((i // 64) > (r // 64)), 0.0, 1.0).astype(np.float32)
    r2 = np.arange(128)[None, :]
    vis2 = ((i // 64) == (r2 // 64)).astype(np.float32)
    return dict(c_ident=ident, c_j=jrev, c_oh=oh, c_vis=vis, c_vis2=vis2)


PHASES = []
MARKERS = False


def build(SEQ, PAST, do_sample=True, half=True):
    NT = SEQ // 512
    NL = NT // 2 if half else 0
    OSEQ = SEQ - NL * 512
    NKB = SEQ // 128
    PKB = PAST // 128
    assert SEQ % 512 == 0 and PAST % 512 == 0
    nc = bass.Bass("TRN2", target_bir_lowering=False)
    S = Sched(nc)
    es = ExitStack()
    del PHASES[:]

    def mark(name):
        PHASES.append((name, len(S.streams["pe"])))
        if MARKERS:
            S.op("dve", lambda e: e.memset(ap=dummy[:, 3:4], constant=0.0), [], [])

    def din(n, s, d=F32):
        return nc.dram_tensor(n, list(s), d, kind="ExternalInput").ap()

    def dout(n, s, d=F32):
        return nc.dram_tensor(n, list(s), d, kind="ExternalOutput").ap()

    def dscr(n, s, d):
        return nc.dram_tensor(n, list(s), d, kind="Internal").ap()

    def sb(n, s, d):
        return es.enter_context(nc.sbuf_tensor(n, list(s), d))

    def A(eng, method, reads, writes, **kw):
        return S.op(eng, lambda e: getattr(e, method)(**kw), reads, writes)

    x_d = din("x", (SEQ, 1024))
    xs_d = din("xs", (512, 1024))
    NSR = 4
    ck_d = din("ck", (NSR, PAST, 512))
    cv_d = din("cv", (NSR, PAST, 512))
    slru_d = din("slru", (8, 512))
    sconv_d = din("sconv", (24, 512))
    W = {n: din(n, s) for n, s in W_SPECS}
    c_ident = din("c_ident", (128, 128))
    c_j = din("c_j", (128, 128))
    c_oh = din("c_oh", (32, 384))
    c_vis = din("c_vis", (128, 256))
    c_vis2 = din("c_vis2", (128, 128))

    flag_d = din("flag", (128, 1))
    y_d = dout("y", (OSEQ, 1024))
    ys_d = dout("ys", (512, 1024))
    nk_d = dout("nk", (OSEQ, 512))
    nv_d = dout("nv", (OSEQ, 512))
    nlru_d = dout("nlru", (1, 512))
    nconv_d = dout("nconv", (3, 512))
    nks_d = dout("nks", (512, 512))
    nvs_d = dout("nvs", (512, 512))
    nlrus_d = dout("nlrus", (8, 512))
    nconvs_d = dout("nconvs", (24, 512))

    wsc = dscr("wsc", (41, 128, 4096), BF16)
    KT_s = dscr("KT_s", (4, 128, SEQ), BF16)
    VA_s = dscr("VA_s", (4, 128, NKB, 130), BF16)
    eu_s = dscr("eu_s", (4, 384), F32)

    xt = sb("xt", (128, 4, 1024), F32)
    arena = sb("arena", (128, 11264), BF16)
    hidT = arena[:].rearrange("p (k n) -> p k n", k=22)
    arena_f = arena[:].bitcast(F32)
    lt = [arena_f[:, k * 512:(k + 1) * 512] for k in range(6)]
    rstdL = arena_f[:, 3072:3584]
    hg = arena_f[:, 3584:5632].rearrange("p (c n) -> p c n", c=4)
    wring = sb("wring", (128, NS, 4096), BF16)
    xnT_t = sb("xnT", (128, 8, 512), BF16)
    xnT = xnT_t[:]
    catT = xnT_t[:]
    xsb = sb("xsb", (128, 2, 1024), BF16)
    sg = sb("sg", (128, 2, 512), F32)
    q_tm = sb("q_tm", (128, 4, 512), BF16)
    k_tm = sb("k_tm", (128, 4, 512), F32)
    k_bf = sb("k_bf", (128, 4, 512), BF16)
    v_tm = sb("v_tm", (128, 4, 512), F32)
    tmpn = sb("tmpn", (128, 2, 512), F32)
    QT = sb("QT", (128, 4, 512), BF16)
    KTn = sb("KTn", (128, 4, 512), BF16)
    VAn = sb("VAn", (128, 4, 4, 130), BF16)
    att = sb("att", (128, 8256), BF16)
    KTc = att[:, 0:4096].rearrange("p (s n) -> p s n", s=2)
    VAc = att[:, 4096:8256].rearrange("p (s k e) -> p s k e", s=2, k=16)
    kst = att[:, 0:4096].bitcast(F32).rearrange("p (k d) -> p k d", k=4)
    vst = att[:, 4096:8192].bitcast(F32).rearrange("p (k d) -> p k d", k=4)
    att_f = att[:, 0:8192].bitcast(F32)
    la = [att_f[:, k * 512:(k + 1) * 512] for k in range(6)]
    LA = ["la%d" % k for k in range(6)]
    LTN = ["lt%d" % k for k in range(6)]
    sx = sb("sx", (128, 6176), BF16)
    kbf = sx[:, 0:2048].rearrange("p (k d) -> p k d", k=4)
    KTq = sx[:, 2048:4096].rearrange("p (h n) -> p h n", h=4)
    VAq = sx[:, 4096:6176].rearrange("p (k h e) -> p k h e", k=4, h=4)
    PT = sb("PT", (128, 3, 2, 512), BF16)
    o_tm = sb("o_tm", (128, 4, 512), BF16)
    osb = sb("osb", (128, 5, 128), F32)
    Osb = sb("Osb", (128, 3, 390), F32)
    xrT = sb("xrT", (128, 2144), F32)
    xrP = xrT[:, 0:2060].rearrange("p (c n) -> p c n", c=4)
    xrS = xrT[:, 0:2144].rearrange("p (c s n) -> p c s n", c=4, s=8)
    xgT = sb("xgT", (128, 4, 512), F32)
    identf = sb("identf", (128, 128), F32)
    identb = sb("identb", (128, 128), BF16)
    jrev = sb("jrev", (128, 128), F32)
    onesf = sb("onesf", (128, 128), F32)
    vis = sb("vis", (128, 256), F32)
    vis2 = sb("vis2", (128, 128), F32)
    ET = sb("ET", (128, 4, 256), F32)
    ETp = sb("ETp", (128, 4, 128), F32)
    Hk = sb("Hk", (128, 4, 2, 128), F32)
    gT = sb("gT", (128, 3, 8), F32)
    gq_rep = sb("gq_rep", (128, 8, 64), F32)
    gk_rep = sb("gk_rep", (128, 8, 64), F32)
    gsub_rep = sb("gsub_rep", (128, 128), F32)
    cw = sb("cw", (128, 4, 4), F32)
    vecs = sb("vecs", (128, 6, 4), F32)
    Wbd = sb("Wbd", (128, 2, 4, 128), F32)
    relb = sb("relb", (32, 4), F32)
    oh = sb("oh", (32, 384), F32)
    eu = sb("eu", (4, 384), F32)
    lamv = sb("lamv", (1, 4, 64), F32)
    lams = sb("lams", (1, 8), F32)
    neglam = sb("neglam", (128, 1), F32)
    st = sb("st", (128, 64), F32)
    hcar = sb("hcar", (128, 4), F32)
    h0T = sb("h0T", (128, 4, 8), F32)
    hlast = sb("hlast", (128, 4, 8), F32)
    tails = sb("tails", (128, 4, 24), F32)
    sm_tm = tmpn
    dummy = sb("dummyt", (128, 4), F32)
    flagt = sb("flagt", (128, 1), F32)
    cm05 = sb("cm05", (128, 2), F32)

    psall = es.enter_context(nc.psum_tensor("psall", [128, 8, 512], F32))

    class _PV:
        def __init__(self, i):
            self.i = i

        def __getitem__(self, key):
            return psall[:, self.i, :][key]

    ps = [_PV(i) for i in range(8)]
    psb = [psall[:, i, :].bitcast(BF16) for i in range(8)]
    print("sbuf bytes remaining", nc.sbuf_bytes_remaining)

    XTH = [["xt%da" % i, "xt%db" % i] for i in range(4)]
    XT_ALL = [n_ for p_ in XTH for n_ in p_]
    HID = ["hid%d" % i for i in range(22)]
    LT = ["lt%d" % i for i in range(6)] + ["rstdL", "hg0", "hg1", "hg2", "hg3"]
    XN = ["xn0", "xn1", "xn2", "xn3"]
    CAT = ["cat%d" % i for i in range(8)]

    ALLA = HID + XN + LT + CAT

    def barrier(eng, writes=None):
        writes = ALLA
        k = ENGS.index(eng) % 4
        if eng == "act":
            A(eng, "copy", [], list(writes) + ["dummy_" + eng], out=dummy[:, k:k + 1], in_=onesf[:, 0:1])
        else:
            A(eng, "memset", [], list(writes) + ["dummy_" + eng], ap=dummy[:, k:k + 1], constant=0.0)

    slow = dict(allow_slow_non_contiguous=True)
    S.dma("sp", [(identf[:], c_ident), (jrev[:], c_j), (oh[:], c_oh), (vis[:], c_vis), (vis2[:], c_vis2),
                 (relb[:], W["rel_bias"]),
                 (gT[:, 0, :], W["norm_ffn1"].rearrange("o (k p) -> p (o k)", p=128), slow),
                 (gT[:, 1, :], W["norm_mix"].rearrange("o (k p) -> p (o k)", p=128), slow),
                 (gT[:, 2, :], W["norm_ffn2"].rearrange("o (k p) -> p (o k)", p=128), slow),
                 (gq_rep[:], bass.AP(W["q_norm"].tensor, 0, [[0, 128], [0, 8], [1, 64]])),
                 (gk_rep[:], bass.AP(W["k_norm"].tensor, 0, [[0, 128], [0, 8], [1, 64]])),
                 (gsub_rep[:], bass.AP(W["subln"].tensor, 0, [[0, 128], [1, 128]])),
                 (cw[:, 0, :], W["conv_w"][0:1, :].rearrange("o (c p) -> p (o c)", p=128), slow),
                 (cw[:, 1, :], W["conv_w"][1:2, :].rearrange("o (c p) -> p (o c)", p=128), slow),
                 (cw[:, 2, :], W["conv_w"][2:3, :].rearrange("o (c p) -> p (o c)", p=128), slow),
                 (cw[:, 3, :], W["conv_w"][3:4, :].rearrange("o (c p) -> p (o c)", p=128), slow),
                 (vecs[:, 0, :], W["conv_b"].rearrange("o (c p) -> p (o c)", p=128), slow),
                 (vecs[:, 1, :], W["gate_a_b"].rearrange("o (c p) -> p (o c)", p=128), slow),
                 (vecs[:, 2, :], W["gate_x_b"].rearrange("o (c p) -> p (o c)", p=128), slow),
                 (vecs[:, 3, :], W["lru_L"].rearrange("o (c p) -> p (o c)", p=128), slow),
                 (vecs[:, 4, :], W["lru_out_norm"].rearrange("o (c p) -> p (o c)", p=128), slow),
                 (lamv[:, 0, :], W["lambda_q1"]), (lamv[:, 1, :], W["lambda_q2"]),
                 (lamv[:, 2, :], W["lambda_k1"]), (lamv[:, 3, :], W["lambda_k2"]),
                 (flagt[:], flag_d),
                 ], [], ["const"], "const")
    A("pool", "memset", [], ["Wbd"], ap=Wbd[:], constant=0.0)
    A("pool", "memset", [], ["onesf"], ap=onesf[:], constant=1.0)
    A("pool", "memset", [], ["cm05"], ap=cm05[:, 0:1], constant=-0.5)
    A("pool", "memset", [], ["cm05"], ap=cm05[:, 1:2], constant=0.5)
    A("pool", "memset", [], ["VAn"], ap=VAn[:], constant=1.0)
    A("pool", "memset", [], ["VAq"], ap=VAq, constant=1.0)
    A("pool", "memset", [], ["hcar"], ap=hcar[:], constant=0.0)
    A("pool", "memset", [], ["xr0", "xr1", "xr2", "xr3"], ap=xrT[:], constant=0.0)
    wb_pairs = []
    for gi, gname in enumerate(("gate_a_w", "gate_x_w")):
        for n in range(8):
            o0 = 64 * (n % 2)
            wb_pairs.append((Wbd[o0:o0 + 64, gi, n // 2, o0:o0 + 64], W[gname][n, :, :]))
    S.dma("sp", wb_pairs, [], ["Wbd"], "const2")
    A("dve", "tensor_copy", ["const"], ["identb"], out=identb[:], in_=identf[:])
    A("dve", "tensor_scalar", ["const"], ["gsub"], out=gsub_rep[:], in0=gsub_rep[:], scalar1=0.8, scalar2=None, op0=ALU.mult)
    A("dve", "tensor_scalar", ["const"], ["const"], out=vecs[:, 1:3, :], in0=vecs[:, 1:3, :], scalar1=-1.0, scalar2=None, op0=ALU.mult)
    A("act", "activation", ["const"], ["nsp"], out=vecs[:, 5, :], in_=vecs[:, 3, :], func=AF.Exp, scale=-1.0)
    A("act", "activation", ["nsp"], ["nsp"], out=vecs[:, 5, :], in_=vecs[:, 5, :], func=AF.Ln, bias=1.0)
    A("dve", "tensor_scalar", ["nsp"], ["nsp"], out=vecs[:, 5, :], in0=vecs[:, 5, :], scalar1=-8.0, scalar2=None, op0=ALU.mult)
    A("dve", "tensor_tensor", ["const"], ["lamv"], out=lamv[:, 0:2, :], in0=lamv[:, 0:2, :], in1=lamv[:, 2:4, :], op=ALU.mult)
    A("dve", "tensor_reduce", ["lamv"], ["lams"], out=lams[:, 0:2], in_=lamv[:, 0:2, :], axis=AX.X, op=ALU.add)
    A("act", "activation", ["lams"], ["lams"], out=lams[:, 2:4], in_=lams[:, 0:2], func=AF.Exp)
    A("dve", "tensor_tensor", ["lams"], ["lams"], out=lams[:, 4:5], in0=lams[:, 3:4], in1=lams[:, 2:3], op=ALU.subtract)
    A("dve", "tensor_scalar", ["lams"], ["lams"], out=lams[:, 5:6], in0=lams[:, 4:5], scalar1=-0.2, scalar2=None, op0=ALU.add)
    A("pe", "matmul", ["lams", "onesf"], ["ps0"], out=ps[0][:, 0:1], lhsT=onesf[0:1, :], rhs=lams[:, 5:6], start=True, stop=True)
    A("dve", "tensor_copy", ["ps0"], ["neglam"], out=neglam[:], in_=ps[0][:, 0:1])
    A("pe", "matmul", ["const"], ["ps1"], out=ps[1][0:4, 0:384], lhsT=relb[:], rhs=oh[:], start=True, stop=True)
    A("act", "activation", ["ps1"], ["eu"], out=eu[:], in_=ps[1][0:4, 0:384], func=AF.Exp)
    S.dma("sp", [(eu_s, eu[:])], ["eu"], ["eu_s"], "st_eu")
    S.dma("sp", [(Hk[:, h, :, :], bass.AP(eu_s.tensor, h * 384, [[1, 128], [128, 2], [1, 128]])) for h in range(4)],
          ["eu_s"], ["Hk"], "ld_hk")
    for h in range(4):
        b = 2 + (h % 2)
        A("pe", "matmul", ["Hk", "const"], ["ps%d" % b], out=ps[b][:, 128:256], lhsT=Hk[:, h, 0, :], rhs=jrev[:], start=True, stop=True)
        A("pe", "matmul", ["Hk", "const"], ["ps%d" % b], out=ps[b][:, 0:128], lhsT=Hk[:, h, 1, :], rhs=jrev[:], start=True, stop=True)
        A("dve", "tensor_tensor", ["ps%d" % b, "const"], ["ET"], out=ET[:, h, :], in0=ps[b][:, 0:256], in1=vis[:], op=ALU.mult)
        A("dve", "tensor_tensor", ["ps%d" % b, "const"], ["ET"], out=ETp[:, h, :], in0=ps[b][:, 0:128], in1=vis2[:], op=ALU.mult)

    chunks = []
    for f in ("ffn1", "w_in", "w_out", "ffn2"):
        if f in ("ffn1", "ffn2"):
            for j in range(11):
                chunks.append([(W[f + "_gate"], 0, 8, 256 * j, 256), (W[f + "_up"], 0, 8, 256 * j, 256)])
            for half in range(2):
                for g in range(3):
                    chunks.append([(W[f + "_down"], g * 8, min(8, 22 - g * 8), 512 * half, 512)])
        elif f == "w_in":
            for b in range(5):
                chunks.append([(W["w_in"], 0, 8, 512 * b, 512)])
        else:
            for half in range(2):
                chunks.append([(W["w_out"], 0, 8, 512 * half, 512)])
    assert len(chunks) == 41
    chunk_len = [sum(nk * ncols for (_, _, nk, _, ncols) in pieces) for pieces in chunks]
    stage = [xt[:].rearrange("p a b -> p (a b)"), arena_f[:, 0:4096]]
    stage_res = [XT_ALL, HID + LT]
    cast_eng = ["dve", "pool", "act"]
    N_EAGER = 17 if (half and NL >= 6) else 41
    sx_f = sx[:, 0:4096].bitcast(F32)
    lz_stage = [sx_f[:, 0:1024], sx_f[:, 1024:2048]]
    lz_stage_res = [["kbf"], ["KTq0", "KTq1", "KTq2", "KTq3"]]
    lz_out = [sx[:, 4096:5120], sx[:, 5120:6144]]
    lz_out_res = [["cvA"], ["cvB"]]
    lazy_pieces = []
    lz_ctr = [0]

    def lazy_piece(ci, off, wap, kt0, nk, col0, ncols):
        k = lz_ctr[0] % 2
        lz_ctr[0] += 1
        n = nk * ncols
        S.dma("sp", [(lz_stage[k][:, 0:n].rearrange("p (k n) -> p k n", k=nk),
                      wap[kt0 * 128:(kt0 + nk) * 128, col0:col0 + ncols].rearrange("(k p) n -> p k n", p=128))], [], lz_stage_res[k], "lzs%d" % k)
        eng = ("dve", "act")[lz_ctr[0] % 2]
        if eng == "act":
            A("act", "activation", lz_stage_res[k], lz_out_res[k], out=lz_out[k][:, 0:n], in_=lz_stage[k][:, 0:n], func=AF.Copy)
        else:
            A("dve", "tensor_copy", lz_stage_res[k], lz_out_res[k], out=lz_out[k][:, 0:n], in_=lz_stage[k][:, 0:n])
        S.dma("pool", [(wsc[ci, :, off:off + n], lz_out[k][:, 0:n])], lz_out_res[k], ["wsc%d" % ci], "lzo%d" % k)

    conv_order = list(range(N_EAGER)) + [c_ for c_ in (18, 19, 20, 21, 17) if c_ >= N_EAGER] + [c_ for c_ in range(22, 41) if c_ >= N_EAGER]
    for ci in conv_order:
        pieces = chunks[ci]
        if ci >= N_EAGER:
            off = 0
            for (wap, kt0, nk, col0, ncols) in pieces:
                step = max(1, 1024 // ncols)
                for q0 in range(0, nk, step):
                    nq = min(step, nk - q0)
                    lazy_pieces.append((ci, off, wap, kt0 + q0, nq, col0, ncols))
                    off += nq * ncols
            continue
        sidx = ci % 2
        stg = stage[sidx]
        pairs = []
        off = 0
        for (wap, kt0, nk, col0, ncols) in pieces:
            pairs.append((stg[:, off:off + nk * ncols].rearrange("p (k n) -> p k n", k=nk),
                          wap[kt0 * 128:(kt0 + nk) * 128, col0:col0 + ncols].rearrange("(k p) n -> p k n", p=128)))
            off += nk * ncols
        S.dma("sp", pairs, [], stage_res[sidx], "pst%d" % sidx)
        cslot = 1 + sidx
        eng = cast_eng[ci % 3]
        if eng == "act":
            A("act", "activation", stage_res[sidx], ["w%d" % cslot], out=wring[:, cslot, 0:off], in_=stg[:, 0:off], func=AF.Copy)
        else:
            A(eng, "tensor_copy", stage_res[sidx], ["w%d" % cslot], out=wring[:, cslot, 0:off], in_=stg[:, 0:off])
        S.dma("pool", [(wsc[ci, :, 0:off], wring[:, cslot, 0:off])], ["w%d" % cslot], ["wsc%d" % ci], "stc%d" % sidx)

    chunk_seq = []
    for t_ in range(NT):
        chunk_seq += (list(range(17)) + [18, 19, 20, 21]) if t_ < NL else list(range(41))
    if do_sample:
        chunk_seq += list(range(41))
    total_chunks = len(chunk_seq)
    ws = dict(cur=0, issued=0)

    def next_chunk():
        i = ws["cur"]
        ws["cur"] += 1
        while ws["issued"] < min(total_chunks, i + NS):
            k = ws["issued"]
            slot = k % NS
            ci = chunk_seq[k]
            n = chunk_len[ci]
            S.dma("sp", [(wring[:, slot, 0:n], wsc[ci, :, 0:n])], ["wsc%d" % ci], ["w%d" % slot], "w%d" % slot)
            ws["issued"] += 1
        return i % NS

    def rsqrt_small(ap, n_inv, res, rows=slice(0, 128)):
        A("dve", "tensor_scalar", [res], [res], out=ap, in0=ap, scalar1=n_inv, scalar2=EPS, op0=ALU.mult, op1=ALU.add)
        A("act", "activation", [res], [res], out=ap, in_=ap, func=AF.Ln)
        A("act", "activation", [res], [res], out=ap, in_=ap, func=AF.Exp, scale=-0.5)

    def norm_T(gi):
        for sub in range(4):
            A("act", "activation", XTH[sub], ["sg0", "sg1", "ssn"], out=sg[:].rearrange("p a b -> p (a b)"), in_=xt[:, sub, :], func=AF.Square,
              accum_out=st[:, sub:sub + 1])
        rsqrt_small(st[:, 0:4], 1.0 / 1024, "ssn")
        for sub in range(4):
            xb = "xsb%d" % (sub % 2)
            A("act", "activation", XTH[sub] + ["ssn"], [xb], out=xsb[:, sub % 2, :], in_=xt[:, sub, :], func=AF.Copy,
              scale=st[:, sub:sub + 1])
            bk = 6 + (sub % 2)
            for kt in range(8):
                A("pe", "transpose", [xb, "identb"], ["ps%d" % bk], out=psb[bk][:, kt * 128:(kt + 1) * 128],
                  in_=xsb[:, sub % 2, kt * 128:(kt + 1) * 128], identity=identb[:])
            A("dve", "tensor_tensor", ["ps%d" % bk, "const"], [XN[sub]], out=xnT[:, :, sub * 128:(sub + 1) * 128],
              in0=psb[bk][:, 0:1024].rearrange("p (k n) -> p k n", k=8),
              in1=gT[:, gi, :].unsqueeze(2).to_broadcast([128, 8, 128]), op=ALU.mult)

    pending = dict(ops=None)

    def ffn():
        pend = pending["ops"]
        rate = (len(pend) + 19) // 20 if pend else 0
        for grp in range(11):
            slot = next_chunk()
            wres = "w%d" % slot
            w = wring[:, slot, :].rearrange("p (g k n) -> p g k n", g=2, k=8)
            for j in range(2):
                nt = grp * 2 + j
                bg, bu = (0, 1) if nt % 2 == 0 else (2, 3)
                for kt in range(8):
                    A("pe", "matmul", [wres] + XN, ["ps%d" % bg], out=ps[bg][:], lhsT=w[:, 0, kt, j * 128:(j + 1) * 128],
                      rhs=xnT[:, kt, :], start=(kt == 0), stop=(kt == 7))
                if pend:
                    pump(pend, 1)
                for kt in range(8):
                    A("pe", "matmul", [wres] + XN, ["ps%d" % bu], out=ps[bu][:], lhsT=w[:, 1, kt, j * 128:(j + 1) * 128],
                      rhs=xnT[:, kt, :], start=(kt == 0), stop=(kt == 7))
                if pend:
                    pump(pend, 1)
                sgn = "sg%d" % (nt % 2)
                A("act", "activation", ["ps%d" % bg], [sgn], out=sg[:, nt % 2, :], in_=ps[bg][:], func=AF.Exp, scale=-1.0)
                A("act", "activation", [sgn], [sgn], out=sg[:, nt % 2, :], in_=sg[:, nt % 2, :], func=AF.Ln, bias=1.0)
                A("act", "activation", [sgn], [sgn], out=sg[:, nt % 2, :], in_=sg[:, nt % 2, :], func=AF.Exp, scale=-1.0)
                if pend:
                    pump(pend, 1)
                A("dve", "tensor_tensor", [sgn, "ps%d" % bg], [sgn], out=sg[:, nt % 2, :], in0=sg[:, nt % 2, :], in1=ps[bg][:], op=ALU.mult)
                A("dve", "tensor_tensor", [sgn, "ps%d" % bu], [HID[nt]], out=hidT[:, nt, :], in0=sg[:, nt % 2, :], in1=ps[bu][:], op=ALU.mult)
                if pend:
                    pump(pend, max(1, rate - 3))
                if lazy_pieces:
                    lazy_piece(*lazy_pieces.pop(0))
        if pend is not None:
            pump(pend, len(pend))
            pending["ops"] = None
        for half in range(2):
            banks = [4, 5, 6, 7] if half == 0 else [0, 1, 2, 3]
            for g in range(3):
                slot = next_chunk()
                wres = "w%d" % slot
                w = wring[:, slot, :].rearrange("p (k n) -> p k n", k=8)
                for kti, kt in enumerate(range(g * 8, min(22, g * 8 + 8))):
                    for sub in range(4):
                        A("pe", "matmul", [wres, HID[kt]], ["ps%d" % banks[sub]], out=ps[banks[sub]][:],
                          lhsT=hidT[:, kt, sub * 128:(sub + 1) * 128], rhs=w[:, kti, :], start=(kt == 0), stop=(kt == 21))
            for sub in range(4):
                A("dve", "scalar_tensor_tensor", ["ps%d" % banks[sub], XTH[sub][half]], [XTH[sub][half]],
                  out=xt[:, sub, half * 512:(half + 1) * 512], in0=ps[banks[sub]][:], scalar=0.5,
                  in1=xt[:, sub, half * 512:(half + 1) * 512], op0=ALU.mult, op1=ALU.add)

    def w_in(sample, light=False):
        for blk in range(3):
            if light and blk == 0:
                continue
            slot = next_chunk()
            wres = "w%d" % slot
            w = wring[:, slot, :].rearrange("p (k n) -> p k n", k=8)
            banks = [0, 1, 2, 3] if blk % 2 == 0 else [4, 5, 6, 7]
            for sub in range(4):
                for kt in range(8):
                    A("pe", "matmul", [wres, XN[sub]], ["ps%d" % banks[sub]], out=ps[banks[sub]][:],
                      lhsT=xnT[:, kt, sub * 128:(sub + 1) * 128], rhs=w[:, kt, :], start=(kt == 0), stop=(kt == 7))
            for sub in range(4):
                pb = "ps%d" % banks[sub]
                pv = ps[banks[sub]][:]
                if blk < 2:
                    ti = sub % 2
                    tn = "tmpn%d" % ti
                    ssq = st[:, 8 + 8 * ti:16 + 8 * ti]
                    A("act", "activation", [pb], [tn], out=tmpn[:, ti, :], in_=pv, func=AF.Square)
                    A("dve", "tensor_reduce", [tn], ["ssq%d" % ti], out=ssq, in_=tmpn[:, ti, :].rearrange("p (g d) -> p g d", g=8),
                      axis=AX.X, op=ALU.add)
                    rsqrt_small(ssq, 1.0 / 64, "ssq%d" % ti)
                    A("dve", "tensor_tensor", [pb, "ssq%d" % ti, tn], [tn], out=tmpn[:, ti, :].rearrange("p (g d) -> p g d", g=8),
                      in0=pv.rearrange("p (g d) -> p g d", g=8), in1=ssq.unsqueeze(2).to_broadcast([128, 8, 64]), op=ALU.mult)
                    if blk == 0:
                        A("pool", "tensor_tensor", [tn, "const"], ["q_tm%d" % sub], out=q_tm[:, sub, :], in0=tmpn[:, ti, :],
                          in1=gq_rep[:].rearrange("p g d -> p (g d)"), op=ALU.mult)
                    else:
                        A("pool", "tensor_tensor", [tn, "const"], ["k_tm%d" % sub], out=k_tm[:, sub, :], in0=tmpn[:, ti, :],
                          in1=gk_rep[:].rearrange("p g d -> p (g d)"), op=ALU.mult)
                        A("pool", "tensor_copy", ["k_tm%d" % sub], ["k_bf%d" % sub], out=k_bf[:, sub, :], in_=k_tm[:, sub, :])
                else:
                    A("act", "activation", [pb], ["v_tm%d" % sub], out=v_tm[:, sub, :], in_=pv, func=AF.Copy)
                    if light:
                        A("dve", "tensor_scalar", ["v_tm%d" % sub, "const"], ["VAn"], out=VAn[:, sub, :, 0:128],
                          in0=v_tm[:, sub, :].rearrange("p (h e) -> p h e", h=4), scalar1=flagt[:, 0:1], scalar2=None, op0=ALU.mult)
                    else:
                        A("pool", "tensor_copy", ["v_tm%d" % sub], ["VAn"], out=VAn[:, sub, :, 0:128],
                          in_=v_tm[:, sub, :].rearrange("p (h e) -> p h e", h=4))
        for blk in (3, 4):
            slot = next_chunk()
            wres = "w%d" % slot
            w = wring[:, slot, :].rearrange("p (k n) -> p k n", k=8)
            for c in range(4):
                bk = 4 + c if blk == 3 else c
                for kt in range(8):
                    A("pe", "matmul", [wres] + XN, ["ps%d" % bk], out=ps[bk][:], lhsT=w[:, kt, c * 128:(c + 1) * 128],
                      rhs=xnT[:, kt, :], start=(kt == 0), stop=(kt == 7))
                if blk == 3:
                    if sample:
                        A("act", "activation", ["ps%d" % bk], ["xr%d" % c], out=xrS[:, c, :, 3:67],
                          in_=ps[bk][:].rearrange("p (s n) -> p s n", s=8), func=AF.Copy)
                    else:
                        A("act", "activation", ["ps%d" % bk], ["xr%d" % c], out=xrP[:, c, 3:515], in_=ps[bk][:], func=AF.Copy)
                else:
                    A("dve", "tensor_copy", ["ps%d" % bk], ["xg%d" % c], out=xgT[:, c, :], in_=ps[bk][:])
        for (src, sname, dst, dname, bk) in ((q_tm, "q_tm", QT, "QT", 6), (k_bf, "k_bf", KTn, "KTn", 7)):
            if light and sname == "q_tm":
                continue
            for h in range(4):
                for sub in range(4):
                    A("pe", "transpose", ["%s%d" % (sname, sub), "identb"], ["ps%d" % bk], out=psb[bk][:, sub * 128:(sub + 1) * 128],
                      in_=src[:, sub, h * 128:(h + 1) * 128], identity=identb[:])
                A("dve" if h % 2 == 0 else "act", "tensor_copy" if h % 2 == 0 else "copy", ["ps%d" % bk], [dname],
                  out=dst[:, h, :], in_=psb[bk][:, 0:512])

    def Oacc(idx):
        b = 4 + idx // 3
        c0 = (idx % 3) * 130
        return b, c0

    pt_ctr = [0]

    def attn_prefetch(t, extra_writes=()):
        nprefix = 4 * t
        ch_list = [(c0, min(16, nprefix - c0)) for c0 in range(0, nprefix, 16)]
        stt = dict(t=t, nprefix=nprefix, ch_list=ch_list, loads=[(h, ci) for h in range(4) for ci in range(len(ch_list))],
                   issued=0, slot_of={}, extra=list(extra_writes))
        issue_load(stt)
        issue_load(stt)
        return stt

    def issue_load(stt):
        k = stt["issued"]
        if k >= len(stt["loads"]):
            return
        h, ci = stt["loads"][k]
        c0, n = stt["ch_list"][ci]
        slot = kv_ctr[0] % 2
        kv_ctr[0] += 1
        stt["slot_of"][(h, ci)] = slot
        tl = sorted(set((c0 + j) // 4 for j in range(n)))
        S.dma("sp", [(KTc[:, slot, 0:n * 128], KT_s[h, :, c0 * 128:(c0 + n) * 128])], ["kts%d" % tt for tt in tl],
              ["ktc%d" % slot] + stt["extra"], "ktc%d" % slot)
        S.dma("sp", [(VAc[:, slot, 0:n, :], VA_s[h, :, c0:c0 + n, :])], ["vas%d" % tt for tt in tl],
              ["vac%d" % slot] + stt["extra"], "vac%d" % slot)
        stt["issued"] += 1

    def attn_prompt(stt, side_ops=None):
        nprefix = stt["nprefix"]
        ch_list = stt["ch_list"]
        slot_of = stt["slot_of"]
        blocks = []
        for h in range(4):
            for ci, (c0, n) in enumerate(ch_list):
                for j in range(n):
                    blocks.append(dict(h=h, load=(h, ci), j=j, q_lo=0, bias=("pl" if c0 + j == nprefix - 1 else None), local=None))
            for jb in range(4):
                blocks.append(dict(h=h, load=None, j=jb, q_lo=128 * jb, bias="loc", local=jb))

        def qk(bl):
            h = bl["h"]
            pbi = pt_ctr[0] % 2
            ptb = pt_ctr[0] % 3
            pt_ctr[0] += 1
            bl["pb"] = ptb
            q_lo = bl["q_lo"]
            if bl["load"] is not None:
                slot = slot_of[bl["load"]]
                bl["slot"] = slot
                kt_ap = KTc[:, slot, bl["j"] * 128:(bl["j"] + 1) * 128]
                kres = ["ktc%d" % slot]
            else:
                kt_ap = KTn[:, h, bl["j"] * 128:(bl["j"] + 1) * 128]
                kres = ["KTn"]
            for c in range(2):
                bk = pbi * 2 + c
                A("pe", "matmul", kres + ["QT"], ["ps%d" % bk], out=ps[bk][:, q_lo:512], lhsT=kt_ap[64 * c:64 * c + 64, :],
                  rhs=QT[64 * c:64 * c + 64, h, q_lo:512], start=True, stop=True)
            ptn = ["pt%d0" % ptb, "pt%d1" % ptb]
            A("act", "activation", ["ps%d" % (pbi * 2), "ps%d" % (pbi * 2 + 1)], ptn, out=PT[:, ptb, :, q_lo:512],
              in_=psall[:, pbi * 2:pbi * 2 + 2, q_lo:512], func=AF.Exp, scale=0.125)
            if bl["bias"] == "pl":
                A("dve", "tensor_tensor", ptn + ["ET"], ptn, out=PT[:, ptb, :, 0:128], in0=PT[:, ptb, :, 0:128],
                  in1=ET[:, h, 128:256].unsqueeze(1).to_broadcast([128, 2, 128]), op=ALU.mult)
            elif bl["bias"] == "loc":
                hi = min(512, q_lo + 256)
                A("dve", "tensor_tensor", ptn + ["ET"], ptn, out=PT[:, ptb, :, q_lo:hi], in0=PT[:, ptb, :, q_lo:hi],
                  in1=ET[:, h, 0:hi - q_lo].unsqueeze(1).to_broadcast([128, 2, hi - q_lo]), op=ALU.mult)

        started = set()

        def pv(bl, first):
            h = bl["h"]
            if first:
                started.clear()
            pbi = bl["pb"]
            if bl["load"] is not None:
                va_ap = VAc[:, bl["slot"], bl["j"], 0:129]
                vres = ["vac%d" % bl["slot"]]
            else:
                va_ap = VAn[:, bl["j"], h, 0:129]
                vres = ["VAn"]
            for qs in range(bl["q_lo"] // 128, 4):
                last = (bl["local"] == qs)
                for c in range(2):
                    b, c0 = Oacc(c * 4 + qs)
                    st_flag = first and (b not in started)
                    started.add(b)
                    A("pe", "matmul", vres + ["pt%d%d" % (pbi, c)], ["ps%d" % b], out=ps[b][:, c0:c0 + 129],
                      lhsT=PT[:, pbi, c, qs * 128:(qs + 1) * 128], rhs=va_ap, start=st_flag, stop=last, skip_group_check=True)

        def epilogue(h):
            for k in range(3):
                A("dve", "tensor_copy", ["ps%d" % (4 + k)], ["Osb%d" % k], out=Osb[:, k, :], in_=ps[4 + k][:, 0:390])
            for qs in range(4):
                b0, c0 = Oacc(qs)
                b1, c1 = Oacc(4 + qs)
                O0 = Osb[:, b0 - 4, c0:c0 + 129]
                O1 = Osb[:, b1 - 4, c1:c1 + 129]
                r0, r1 = "Osb%d" % (b0 - 4), "Osb%d" % (b1 - 4)
                A("dve", "reciprocal", [r0], ["rec0"], out=st[:, 32:33], in_=O0[:, 128:129])
                A("dve", "reciprocal", [r1], ["rec1"], out=st[:, 33:34], in_=O1[:, 128:129])
                A("dve", "tensor_scalar", [r1, "rec1", "neglam"], ["osb4"], out=osb[:, 4, :], in0=O1[:, 0:128],
                  scalar1=st[:, 33:34], scalar2=neglam[:, 0:1], op0=ALU.mult, op1=ALU.mult)
                A("dve", "scalar_tensor_tensor", [r0, "rec0", "osb4"], ["osb%d" % qs], out=osb[:, qs, :], in0=O0[:, 0:128],
                  scalar=st[:, 32:33], in1=osb[:, 4, :], op0=ALU.mult, op1=ALU.add)
                A("dve", "tensor_tensor", ["osb%d" % qs], ["osb4"], out=osb[:, 4, :], in0=osb[:, qs, :], in1=osb[:, qs, :], op=ALU.mult)
                A("dve", "tensor_reduce", ["osb4"], ["ssh"], out=st[:, 36 + qs:37 + qs], in_=osb[:, 4, :], axis=AX.X, op=ALU.add)
            A("dve", "tensor_scalar", ["ssh"], ["ssh"], out=st[:, 36:40], in0=st[:, 36:40], scalar1=1.0 / 128, scalar2=EPS, op0=ALU.mult, op1=ALU.add)
            A("pool", "tensor_tensor", ["ssh", "cm05"], ["ssh"], out=st[:, 36:40], in0=st[:, 36:40], in1=cm05[:, 0:1].to_broadcast([128, 4]), op=ALU.pow)
            for qs in range(4):
                A("dve", "scalar_tensor_tensor", ["osb%d" % qs, "ssh", "gsub"], ["o_tm%d" % qs], out=o_tm[:, qs, h * 128:(h + 1) * 128],
                  in0=osb[:, qs, :], scalar=st[:, 36 + qs:37 + qs], in1=gsub_rep[:], op0=ALU.mult, op1=ALU.mult)

        nb = len(blocks)
        side_rate = (len(side_ops) + nb - 9) // max(1, nb - 8) if side_ops else 0
        qk(blocks[0])
        if nb > 1:
            qk(blocks[1])
        for i, bl in enumerate(blocks):
            if i + 2 < nb:
                qk(blocks[i + 2])
            first = (i == 0) or (blocks[i - 1]["h"] != bl["h"])
            pv(bl, first)
            if side_ops:
                pump(side_ops, side_rate)
            if bl["load"] is not None and (i + 1 >= nb or blocks[i + 1]["load"] != bl["load"]):
                issue_load(stt)
            if i + 1 >= nb or blocks[i + 1]["h"] != bl["h"]:
                epilogue(bl["h"])

    kv_ctr = [0]

    def attn_sample(side_ops=None):
        nch = PKB // 4
        seq_ch = [(s_, ch_) for s_ in range(NSR) for ch_ in range(nch)]

        def load_chunk(k):
            if k >= len(seq_ch):
                return
            s_, ch_ = seq_ch[k]
            S.dma("sp", [(kst, ck_d[s_, ch_ * 512:(ch_ + 1) * 512, :].rearrange("(k p) d -> p k d", p=128))], [],
                  ["ktc0", "ktc1"], "kst")
            S.dma("sp", [(vst, cv_d[s_, ch_ * 512:(ch_ + 1) * 512, :].rearrange("(k p) d -> p k d", p=128))], [],
                  ["vac0", "vac1"], "vst")

        side_rate = (len(side_ops) + len(seq_ch) * 8 - 9) // (len(seq_ch) * 8 - 8) if side_ops else 0
        load_chunk(0)
        for s in range(NSR):
            pair = s // 2
            started_s = set()
            qc0 = pair * 128
            R0 = (s % 2) * 64
            for ch in range(nch):
                A("pool", "tensor_copy", ["ktc0", "ktc1"], ["kbf"], out=kbf, in_=kst)
                A("dve", "tensor_copy", ["vac0", "vac1"], ["VAq"], out=VAq[:, :, :, 0:128],
                  in_=vst.rearrange("p k (h e) -> p k h e", h=4))
                load_chunk(s * nch + ch + 1)
                pend_pv = [None]

                def flush_pv():
                    if pend_pv[0] is not None:
                        h_, c_, gi_, bk_ = pend_pv[0]
                        pend_pv[0] = None
                        ptn_ = "pt%d%d" % (gi_ // 2, gi_ % 2)
                        b, c0 = Oacc(h_ * 2 + c_)
                        for kb in range(4):
                            st_flag = (ch == 0 and kb == 0 and b not in started_s)
                            started_s.add(b)
                            A("pe", "matmul", ["VAq", ptn_], ["ps%d" % b], out=ps[b][:, c0:c0 + 129],
                              lhsT=PT[:, gi_ // 2, gi_ % 2, kb * 128:(kb + 1) * 128], rhs=VAq[:, kb, h_, 0:129],
                              start=st_flag, stop=False, skip_group_check=True)

                for h in range(4):
                    for kb in range(4):
                        A("pe", "transpose", ["kbf", "identb"], ["ps7"], out=psb[7][:, kb * 128:(kb + 1) * 128],
                          in_=kbf[:, kb, h * 128:(h + 1) * 128], identity=identb[:])
                    A("dve", "tensor_copy", ["ps7"], ["KTq%d" % h], out=KTq[:, h, :], in_=psb[7][:, 0:512])
                    for c in range(2):
                        gi = pt_ctr[0] % 6
                        bk = pt_ctr[0] % 4
                        pt_ctr[0] += 1
                        ptn = "pt%d%d" % (gi // 2, gi % 2)
                        for kb in range(4):
                            A("pe", "matmul", ["KTq%d" % h, "QT"], ["ps%d" % bk], out=ps[bk][:, kb * 128:(kb + 1) * 128],
                              lhsT=KTq[64 * c:64 * c + 64, h, kb * 128:(kb + 1) * 128], rhs=QT[64 * c:64 * c + 64, h, qc0:qc0 + 128],
                              start=True, stop=True)
                        A("act", "activation", ["ps%d" % bk], [ptn], out=PT[:, gi // 2, gi % 2, :], in_=ps[bk][:], func=AF.Exp, scale=0.125)
                        if ch == nch - 1:
                            a0 = 3 * 128 + R0
                            A("dve", "tensor_tensor", [ptn, "ET"], [ptn], out=PT[:, gi // 2, gi % 2, a0:a0 + 64],
                              in0=PT[:, gi // 2, gi % 2, a0:a0 + 64], in1=ET[:, h, 128:192], op=ALU.mult)
                        flush_pv()
                        pend_pv[0] = (h, c, gi, bk)
                        if side_ops:
                            pump(side_ops, side_rate)
                flush_pv()
            for h in range(4):
                for c in range(2):
                    gi = pt_ctr[0] % 6
                    bk = pt_ctr[0] % 4
                    pt_ctr[0] += 1
                    ptn = "pt%d%d" % (gi // 2, gi % 2)
                    A("pe", "matmul", ["KTn", "QT"], ["ps%d" % bk], out=ps[bk][:, 0:128], lhsT=KTn[64 * c:64 * c + 64, h, qc0:qc0 + 128],
                      rhs=QT[64 * c:64 * c + 64, h, qc0:qc0 + 128], start=True, stop=True)
                    A("act", "activation", ["ps%d" % bk], [ptn], out=PT[:, gi // 2, gi % 2, 0:128], in_=ps[bk][:, 0:128], func=AF.Exp, scale=0.125)
                    A("dve", "tensor_tensor", [ptn, "ET"], [ptn], out=PT[:, gi // 2, gi % 2, 0:128], in0=PT[:, gi // 2, gi % 2, 0:128],
                      in1=ETp[:, h, :], op=ALU.mult)
                    b, c0 = Oacc(h * 2 + c)
                    A("pe", "matmul", ["VAn", ptn], ["ps%d" % b], out=ps[b][:, c0:c0 + 129], lhsT=PT[:, gi // 2, gi % 2, 0:128],
                      rhs=VAn[:, pair, h, 0:129], start=False, stop=True, skip_group_check=True)
            R = slice(R0, R0 + 64)
            for h in range(4):
                b0, c0 = Oacc(h * 2)
                b1, c1 = Oacc(h * 2 + 1)
                O0 = ps[b0][R, c0:c0 + 129]
                O1 = ps[b1][R, c1:c1 + 129]
                A("dve", "reciprocal", ["ps%d" % b0], ["rec"], out=st[R, 32:33], in_=O0[:, 128:129])
                A("dve", "reciprocal", ["ps%d" % b1], ["rec"], out=st[R, 33:34], in_=O1[:, 128:129])
                A("dve", "tensor_scalar", ["ps%d" % b1, "rec", "neglam"], ["osb4"], out=osb[R, 4, :], in0=O1[:, 0:128],
                  scalar1=st[R, 33:34], scalar2=neglam[R, 0:1], op0=ALU.mult, op1=ALU.mult)
                A("dve", "scalar_tensor_tensor", ["ps%d" % b0, "rec", "osb4"], ["osb%d" % h], out=osb[R, h, :], in0=O0[:, 0:128],
                  scalar=st[R, 32:33], in1=osb[R, 4, :], op0=ALU.mult, op1=ALU.add)
                A("act", "activation", ["osb%d" % h], ["sg0", "ssh"], out=sg[R, 0, 0:128], in_=osb[R, h, :], func=AF.Square,
                  accum_out=st[R, 36 + h:37 + h])
            rsqrt_small(st[R, 36:40], 1.0 / 128, "ssh", rows=R)
            for h in range(4):
                A("dve", "scalar_tensor_tensor", ["osb%d" % h, "ssh", "gsub"], ["o_tm%d" % pair], out=o_tm[R, pair, h * 128:(h + 1) * 128],
                  in0=osb[R, h, :], scalar=st[R, 36 + h:37 + h], in1=gsub_rep[R, :], op0=ALU.mult, op1=ALU.mult)

    def o_transposes():
        for h in range(4):
            bk = 6 + (h % 2)
            for sub in range(4):
                A("pe", "transpose", ["o_tm%d" % sub, "identb"], ["ps%d" % bk], out=psb[bk][:, sub * 128:(sub + 1) * 128],
                  in_=o_tm[:, sub, h * 128:(h + 1) * 128], identity=identb[:])
            A("dve" if h % 2 == 0 else "act", "tensor_copy" if h % 2 == 0 else "copy", ["ps%d" % bk], [CAT[h]],
              out=catT[:, h, :], in_=psb[bk][:, 0:512])

    def lru_build(sample, light, lt, ltn, gb, nb):
        ops = []
        ops_c = [[] for _ in range(4)]
        cur = [ops]
        two_sets = isinstance(lt, tuple)
        lt_sets, ltn_sets = (lt, ltn) if two_sets else ((lt, lt), (ltn, ltn))

        def flat(lst):
            out = []
            for x in lst:
                if isinstance(x, (list, tuple)):
                    out += list(x)
                else:
                    out.append(x)
            return out

        def A(eng, method, reads, writes, **kw):
            cur[0].append((eng, method, flat(reads), flat(writes), kw))
        nsq, tl = (8, 64) if sample else (1, 512)

        def v3(ap):
            return ap.rearrange("p (s n) -> p s n", s=nsq)

        for c in range(4):
            xr = "xr%d" % c
            cur[0] = ops_c[c]
            lt, ltn = lt_sets[c % 2], ltn_sets[c % 2]

            def xp(j):
                if sample:
                    return xrS[:, c, :, j:j + 64]
                return xrP[:, c, j:j + 512].rearrange("p (s n) -> p s n", s=1)
            A("dve", "tensor_scalar", [xr, "const"], [ltn[0]], out=v3(lt[0]), in0=xp(3), scalar1=cw[:, 3, c:c + 1], scalar2=vecs[:, 0, c:c + 1],
              op0=ALU.mult, op1=ALU.add)
            for j in range(3):
                A("dve", "scalar_tensor_tensor", [xr, "const", ltn[0]], [ltn[0]], out=v3(lt[0]), in0=xp(j), scalar=cw[:, j, c:c + 1],
                  in1=v3(lt[0]), op0=ALU.mult, op1=ALU.add)
            ba_, bx_ = gb[c % len(gb)]
            A("pe", "matmul", [ltn[0], "Wbd"], ["ps%d" % ba_], out=ps[ba_][:], lhsT=Wbd[:, 0, c, :], rhs=lt[0], start=True, stop=True)
            A("act", "activation", ["ps%d" % ba_, "const"], [ltn[1]], out=lt[1], in_=ps[ba_][:], func=AF.Exp, bias=vecs[:, 1, c:c + 1], scale=-1.0)
            A("pe", "matmul", [ltn[0], "Wbd"], ["ps%d" % bx_], out=ps[bx_][:], lhsT=Wbd[:, 1, c, :], rhs=lt[0], start=True, stop=True)
            A("act", "activation", ["ps%d" % bx_, "const"], [ltn[2]], out=lt[2], in_=ps[bx_][:], func=AF.Exp, bias=vecs[:, 2, c:c + 1], scale=-1.0)
            for k_ in (1, 2):
                A("act", "activation", [ltn[k_]], [ltn[k_]], out=lt[k_], in_=lt[k_], func=AF.Ln, bias=1.0)
                A("act", "activation", [ltn[k_]], [ltn[k_]], out=lt[k_], in_=lt[k_], func=AF.Exp, scale=-1.0)
            if not light:
                A("pool", "tensor_tensor", ["xg%d" % c], [ltn[4]], out=lt[4], in0=xgT[:, c, :], in1=xgT[:, c, :], op=ALU.mult)
                A("dve", "tensor_scalar", [ltn[4]], [ltn[4]], out=lt[4], in0=lt[4], scalar1=0.044715 * 1.5957691216057308,
                  scalar2=1.5957691216057308, op0=ALU.mult, op1=ALU.add)
                A("pool", "tensor_tensor", [ltn[4], "xg%d" % c], [ltn[4]], out=lt[4], in0=lt[4], in1=xgT[:, c, :], op=ALU.mult)
                A("act", "activation", [ltn[4]], [ltn[4]], out=lt[4], in_=lt[4], func=AF.Exp, scale=-1.0)
                A("act", "activation", [ltn[4]], [ltn[4]], out=lt[4], in_=lt[4], func=AF.Ln, bias=1.0)
                A("act", "activation", [ltn[4]], [ltn[4]], out=lt[4], in_=lt[4], func=AF.Exp, scale=-1.0)
            A("act", "activation", [ltn[1], "nsp"], [ltn[1]], out=lt[1], in_=lt[1], func=AF.Exp, scale=vecs[:, 5, c:c + 1])
            A("dve", "tensor_tensor", [ltn[1]], [ltn[3]], out=lt[3], in0=lt[1], in1=lt[1], op=ALU.mult)
            A("dve", "tensor_scalar", [ltn[3]], [ltn[3]], out=lt[3], in0=lt[3], scalar1=-1.0, scalar2=1.0, op0=ALU.mult, op1=ALU.add)
            A("act", "activation", [ltn[3]], [ltn[3]], out=lt[3], in_=lt[3], func=AF.Ln)
            A("act", "activation", [ltn[3]], [ltn[3]], out=lt[3], in_=lt[3], func=AF.Exp, scale=0.5)
            A("pool", "tensor_tensor", [ltn[2], ltn[3]], [ltn[2]], out=lt[2], in0=lt[2], in1=lt[3], op=ALU.mult)
            A("pool", "tensor_tensor", [ltn[2], ltn[0]], [ltn[2]], out=lt[2], in0=lt[2], in1=lt[0], op=ALU.mult)
            if sample:
                for s in range(8):
                    A("dve", "tensor_tensor_scan", [ltn[1], ltn[2], "h0T"], [ltn[5]], out=lt[5][:, s * 64:(s + 1) * 64],
                      data0=lt[1][:, s * 64:(s + 1) * 64], data1=lt[2][:, s * 64:(s + 1) * 64], initial=h0T[:, c, s:s + 1],
                      op0=ALU.mult, op1=ALU.add)
                A("dve", "tensor_copy", [ltn[5]], ["hlast"], out=hlast[:, c, :], in_=v3(lt[5])[:, :, 63])
                A("pool", "tensor_copy", [xr], ["tails"], out=tails[:, c, :].rearrange("p (s j) -> p s j", s=8), in_=xrS[:, c, :, 64:67])
            else:
                A("dve", "tensor_tensor_scan", [ltn[1], ltn[2], "hcar"], [ltn[5]], out=lt[5], data0=lt[1], data1=lt[2],
                  initial=hcar[:, c:c + 1], op0=ALU.mult, op1=ALU.add)
                A("dve", "tensor_copy", [ltn[5]], ["hcar"], out=hcar[:, c:c + 1], in_=lt[5][:, 511:512])
                A("pool", "tensor_copy", [xr], [xr], out=xrP[:, c, 0:3], in_=xrP[:, c, 512:515])
            if not light:
                A("dve", "tensor_tensor", [ltn[5], "xg%d" % c], [ltn[3]], out=lt[3], in0=lt[5], in1=xgT[:, c, :], op=ALU.mult)
                A("dve", "tensor_tensor", [ltn[3], ltn[4]], ["hg%d" % c], out=hg[:, c, :], in0=lt[3], in1=lt[4], op=ALU.mult)
        cur[0] = ops
        lt, ltn = lt_sets[0], ltn_sets[0]
        if two_sets:
            for c0 in (0, 2):
                a_, b_ = ops_c[c0], ops_c[c0 + 1]
                for k_ in range(max(len(a_), len(b_))):
                    if k_ < len(a_):
                        ops.append(a_[k_])
                    if k_ < len(b_):
                        ops.append(b_[k_])
        else:
            for c in range(4):
                ops.extend(ops_c[c])
        if light:
            return ops
        for c in range(4):
            A("act", "activation", ["hg%d" % c], [ltn[c]], out=lt[c], in_=hg[:, c, :], func=AF.Square)
        for c in range(4):
            A("pe", "matmul", ["lt%d" % c, "onesf"], ["ps%d" % nb], out=ps[nb][:], lhsT=onesf[:], rhs=lt[c], start=(c == 0), stop=(c == 3))
        A("dve", "tensor_scalar", ["ps%d" % nb], ["rstdL"], out=rstdL, in0=ps[nb][:], scalar1=1.0 / 512, scalar2=EPS, op0=ALU.mult, op1=ALU.add)
        A("act", "activation", ["rstdL"], ["rstdL"], out=rstdL, in_=rstdL, func=AF.Ln)
        A("act", "activation", ["rstdL"], ["rstdL"], out=rstdL, in_=rstdL, func=AF.Exp, scale=-0.5)
        for c in range(4):
            A("dve", "scalar_tensor_tensor", ["hg%d" % c, "rstdL", "const"], [CAT[4 + c]], out=catT[:, 4 + c, :], in0=hg[:, c, :],
              scalar=vecs[:, 4, c:c + 1], in1=rstdL, op0=ALU.mult, op1=ALU.mult)
        return ops

    def pump(ops, n):
        def emit1():
            e_, m_, r_, w_, kw_ = ops.pop(0)
            A(e_, m_, r_, w_, **kw_)
            return e_
        while n > 0 and ops:
            e_ = emit1()
            n -= 1
            if e_ == "pe":
                while ops and ops[0][0] == "pe" and ops[0][4].get("start") is False:
                    emit1()
                if ops:
                    emit1()

    def w_out():
        for half in range(2):
            slot = next_chunk()
            wres = "w%d" % slot
            w = wring[:, slot, :].rearrange("p (k n) -> p k n", k=8)
            banks = [0, 1, 2, 3] if half == 0 else [4, 5, 6, 7]
            for sub in range(4):
                for kt in range(8):
                    A("pe", "matmul", [wres, CAT[kt]], ["ps%d" % banks[sub]], out=ps[banks[sub]][:],
                      lhsT=catT[:, kt, sub * 128:(sub + 1) * 128], rhs=w[:, kt, :], start=(kt == 0), stop=(kt == 7))
            for sub in range(4):
                A("dve", "tensor_tensor", ["ps%d" % banks[sub], XTH[sub][half]], [XTH[sub][half]], out=xt[:, sub, half * 512:(half + 1) * 512],
                  in0=ps[banks[sub]][:], in1=xt[:, sub, half * 512:(half + 1) * 512], op=ALU.add)

    def tm_view(d_ap, r0):
        return d_ap[r0:r0 + 512, :].rearrange("(s p) d -> p s d", p=128)

    tiles = [("p", t) for t in range(NT)] + ([("s", 0)] if do_sample else [])
    for kind, t in tiles:
        sample = kind == "s"
        light = (not sample) and t < NL
        t0 = t * 512
        o0 = (t - NL) * 512
        xsrc = tm_view(xs_d if sample else x_d, 0 if sample else t0)
        for hf in range(2):
            for sub in range(4):
                S.dma("sp", [(xt[:, sub, hf * 512:(hf + 1) * 512], xsrc[:, sub, hf * 512:(hf + 1) * 512])], [], [XTH[sub][hf]],
                      "ld_x%d%d" % (sub, hf))
        att_state = None
        if (not sample) and (not light) and not (half and t == NL):
            att_state = attn_prefetch(t)
        if sample:
            S.dma("sp", [(sm_tm[0:24, 0, :], sconv_d), (sm_tm[0:8, 1, :], slru_d)], [], ["tmpn0", "tmpn1"], "ld_sm")
            for c in range(4):
                A("pe", "transpose", ["tmpn0", "tmpn1", "const"], ["ps6"], out=ps[6][:, c * 32:c * 32 + 24], in_=sm_tm[0:24, 0, c * 128:(c + 1) * 128],
                  identity=identf[0:24, 0:24])
                A("pe", "transpose", ["tmpn0", "tmpn1", "const"], ["ps6"], out=ps[6][:, 128 + c * 8:128 + c * 8 + 8], in_=sm_tm[0:8, 1, c * 128:(c + 1) * 128],
                  identity=identf[0:8, 0:8])
            for c in range(4):
                A("dve", "tensor_copy", ["ps6"], ["xr%d" % c], out=xrS[:, c, :, 0:3],
                  in_=ps[6][:, c * 32:c * 32 + 24].rearrange("p (s j) -> p s j", s=8))
            A("dve", "tensor_copy", ["ps6"], ["h0T"], out=h0T[:], in_=ps[6][:, 128:160].rearrange("p (c s) -> p c s", c=4))
        if half and (not sample) and t == 0:
            A("pool", "tensor_copy", ["const"], ["VAn"], out=VAn[:, :, :, 128:130], in_=flagt[:, 0:1].unsqueeze(1).unsqueeze(1).to_broadcast([128, 4, 4, 2]))
        if half and (not sample) and t == NL:
            A("pool", "memset", [], ["VAn"], ap=VAn[:, :, :, 128:130], constant=1.0)
        mark("%s%d norm1" % (kind, t))
        barrier("dve")
        norm_T(0)
        mark("%s%d ffn1" % (kind, t))
        ffn()
        if half and (not sample) and t == NL:
            A("dve", "tensor_scalar", ["hcar", "const"], ["hcar"], out=hcar[:], in0=hcar[:], scalar1=flagt[:, 0:1], scalar2=None, op0=ALU.mult)
            for c in range(4):
                A("dve", "tensor_scalar", ["xr%d" % c, "const"], ["xr%d" % c], out=xrP[:, c, 0:3], in0=xrP[:, c, 0:3], scalar1=flagt[:, 0:1],
                  scalar2=None, op0=ALU.mult)
            att_state = attn_prefetch(t, extra_writes=LA)
        mark("%s%d norm2" % (kind, t))
        norm_T(1)
        mark("%s%d w_in" % (kind, t))
        w_in(sample, light)
        if sample:
            S.dma("pool", [(tm_view(nks_d, 0), k_tm[:])], ["k_tm%d" % i for i in range(4)], ["nk_out"], "st_k")
            S.dma("pool", [(tm_view(nvs_d, 0), v_tm[:])], ["v_tm%d" % i for i in range(4)], ["nv_out"], "st_v")
        else:
            if not light:
                S.dma("pool", [(tm_view(nk_d, o0), k_tm[:])], ["k_tm%d" % i for i in range(4)], ["nk_out"], "st_k")
                S.dma("pool", [(tm_view(nv_d, o0), v_tm[:])], ["v_tm%d" % i for i in range(4)], ["nv_out"], "st_v")
            if t + 1 < NT:
                S.dma("pool", [(KT_s.rearrange("h p s -> p h s")[:, :, t0:t0 + 512], KTn[:])], ["KTn"], ["kts%d" % t], "st_kt")
                S.dma("pool", [(VA_s[h, :, 4 * t:4 * t + 4, :], VAn[:, :, h, :]) for h in range(4)], ["VAn"], ["vas%d" % t], "st_va")
        if light:
            mark("%s%d lru" % (kind, t))
            pt_f = [PT[:, k, :, :].rearrange("p a b -> p (a b)").bitcast(F32) for k in range(3)]
            q_f = q_tm[:].rearrange("p a b -> p (a b)").bitcast(F32)
            QT_f = QT[:].rearrange("p a b -> p (a b)").bitcast(F32)
            lb = pt_f + [q_f[:, 0:512], q_f[:, 512:1024], QT_f[:, 0:512]]
            LB = [["pt00", "pt01"], ["pt10", "pt11"], ["pt20", "pt21"], ["q_tm0", "q_tm1"], ["q_tm2", "q_tm3"], ["QT"]]
            pending["ops"] = lru_build(False, True, (la, lb), (LA, LB), [(4, 5), (6, 7)], 7)
            mark("%s%d end" % (kind, t))
            continue
        for e_ in ("act", "dve", "pool"):
            barrier(e_)
        mark("%s%d attn" % (kind, t))
        if sample:
            A("pool", "memset", [], ["VAq", "cvA", "cvB"], ap=VAq[:, :, :, 128:130], constant=1.0)
            lops = lru_build(True, False, lt, LTN, [(7, 7)], 7)
            attn_sample(lops)
            mark("%s%d lru" % (kind, t))
            pump(lops, len(lops))
            mark("%s%d otr" % (kind, t))
            o_transposes()
        else:
            lops = lru_build(False, False, lt, LTN, [(7, 7)], 7)
            attn_prompt(att_state, lops)
            mark("%s%d lru" % (kind, t))
            pump(lops, len(lops))
            mark("%s%d otr" % (kind, t))
            o_transposes()
        mark("%s%d w_out" % (kind, t))
        w_out()
        barrier("dve")
        mark("%s%d norm3" % (kind, t))
        norm_T(2)
        mark("%s%d ffn2" % (kind, t))
        ffn()
        mark("%s%d end" % (kind, t))
        ydst = tm_view(ys_d if sample else y_d, 0 if sample else o0)
        for hf in range(2):
            for sub in range(4):
                S.dma("pool", [(ydst[:, sub, hf * 512:(hf + 1) * 512], xt[:, sub, hf * 512:(hf + 1) * 512])], [XTH[sub][hf]],
                      ["y_out%d%d" % (sub, hf)], "st_y%d%d" % (sub, hf))
        if sample:
            for c in range(4):
                A("pe", "transpose", ["tails", "const"], ["ps6"], out=ps[6][0:24, c * 128:(c + 1) * 128], in_=tails[:, c, :], identity=identf[:])
                A("pe", "transpose", ["hlast", "const"], ["ps7"], out=ps[7][0:8, c * 128:(c + 1) * 128], in_=hlast[:, c, :], identity=identf[:])
            A("dve", "tensor_copy", ["ps6"], ["tmpn0", "tmpn1"], out=sm_tm[0:24, 0, :], in_=ps[6][0:24, :])
            A("dve", "tensor_copy", ["ps7"], ["tmpn0", "tmpn1"], out=sm_tm[0:8, 1, :], in_=ps[7][0:8, :])
            S.dma("pool", [(nconvs_d, sm_tm[0:24, 0, :]), (nlrus_d, sm_tm[0:8, 1, :])], ["tmpn0", "tmpn1"], ["st_out"], "st_sm")
        elif t == NT - 1:
            for c in range(4):
                A("pe", "transpose", ["xr%d" % c, "const"], ["ps6"], out=ps[6][0:3, c * 128:(c + 1) * 128], in_=xrP[:, c, 0:3], identity=identf[:])
                A("pe", "transpose", ["hcar", "const"], ["ps7"], out=ps[7][0:1, c * 128:(c + 1) * 128], in_=hcar[:, c:c + 1], identity=identf[:])
            A("dve", "tensor_copy", ["ps6"], ["tmpn0", "tmpn1"], out=sm_tm[0:3, 0, :], in_=ps[6][0:3, :])
            A("dve", "tensor_copy", ["ps7"], ["tmpn0", "tmpn1"], out=sm_tm[0:1, 1, :], in_=ps[7][0:1, :])
            S.dma("pool", [(nconv_d, sm_tm[0:3, 0, :]), (nlru_d, sm_tm[0:1, 1, :])], ["tmpn0", "tmpn1"], ["st_out"], "st_sm")
    S.wait_all_dma("sp")
    S.emit()
    es.close()
    return nc


_NC_CACHE = {}


def _get_nc(SEQ, PAST):
    key = (SEQ, PAST)
    if key not in _NC_CACHE:
        _NC_CACHE[key] = build(SEQ, PAST)
    return _NC_CACHE[key]


def make_in_maps(inputs, n_prompt):
    consts = host_consts()
    wmap = {}
    for n, s in W_SPECS:
        a = np.asarray(inputs[n], dtype=np.float32)
        if n != "rel_bias":
            a = a[0]
        wmap[n] = np.ascontiguousarray(a.reshape(s))
    SEQ = inputs["x_prompt"].shape[1]
    H = SEQ // 2
    maps = []
    for c in range(2 * n_prompt):
        b = c // 2
        m = dict(wmap)
        m.update(consts)
        xb = np.asarray(inputs["x_prompt"][b], dtype=np.float32)
        if c % 2 == 0:
            m["x"] = np.ascontiguousarray(np.concatenate([np.zeros((H, 1024), np.float32), xb[:H]], 0))
            m["flag"] = np.zeros((128, 1), np.float32)
        else:
            m["x"] = np.ascontiguousarray(xb)
            m["flag"] = np.ones((128, 1), np.float32)
        sl = slice(4 * c, 4 * c + 4)
        z = lambda *shape: np.zeros(shape, np.float32)
        m["xs"] = np.ascontiguousarray(np.concatenate([inputs["x_sample"][sl].reshape(256, 1024), z(256, 1024)], 0))
        past = inputs["cache_k"].shape[2]
        m["ck"] = np.ascontiguousarray(inputs["cache_k"][0, sl].reshape(4, past, 512))
        m["cv"] = np.ascontiguousarray(inputs["cache_v"][0, sl].reshape(4, past, 512))
        m["slru"] = np.ascontiguousarray(np.concatenate([inputs["state_lru"][0, sl], z(4, 512)], 0))
        m["sconv"] = np.ascontiguousarray(np.concatenate([inputs["state_conv"][0, sl].reshape(12, 512), z(12, 512)], 0))
        maps.append(m)
    return maps


def assemble(results, n_prompt, SEQ):
    B = n_prompt
    r = results
    H = SEQ // 2
    cat2 = lambda name, b: np.concatenate([r[2 * b][name], r[2 * b + 1][name]], 0)
    yp = np.stack([cat2("y", b) for b in range(B)], 0)
    NC_ = 2 * B
    ys = np.concatenate([r[c]["ys"].reshape(8, 64, 1024)[:4] for c in range(NC_)], 0)
    nk = np.stack([cat2("nk", b).reshape(SEQ, 4, 2, 64) for b in range(B)], 0)[None]
    nv = np.stack([cat2("nv", b).reshape(SEQ, 4, 128) for b in range(B)], 0)[None]
    nl = np.stack([r[2 * b + 1]["nlru"].reshape(512) for b in range(B)], 0)[None]
    ncv = np.stack([r[2 * b + 1]["nconv"].reshape(3, 512) for b in range(B)], 0)[None]
    nks = np.concatenate([r[c]["nks"].reshape(8, 64, 4, 2, 64)[:4] for c in range(NC_)], 0)[None]
    nvs = np.concatenate([r[c]["nvs"].reshape(8, 64, 4, 128)[:4] for c in range(NC_)], 0)[None]
    nls = np.concatenate([r[c]["nlrus"].reshape(8, 512)[:4] for c in range(NC_)], 0)[None]
    ncs = np.concatenate([r[c]["nconvs"].reshape(8, 3, 512)[:4] for c in range(NC_)], 0)[None]
    return tuple(np.ascontiguousarray(a, dtype=np.float32) for a in (yp, ys, nk, nv, nl, ncv, nks, nvs, nls, ncs))


def kernel(**inputs):
    inputs = {k: np.asarray(v) for k, v in inputs.items()}
    B, SEQ = inputs["x_prompt"].shape[0], inputs["x_prompt"].shape[1]
    PAST = inputs["cache_k"].shape[2]
    assert B == 4 and inputs["x_sample"].shape[0] == 32
    nc = _get_nc(SEQ, PAST)
    maps = make_in_maps(inputs, 4)
    res = run_bass_kernel_spmd(nc, maps, core_ids=list(range(8)))
    return assemble(res.results, 4, SEQ)
```
